# Optimizing a Trainium2 kernel written in Bass

```python
import jax, jax.numpy as jnp
from jax import lax
import numpy as np

D_MODEL = 1024
BATCH = 4
SEQ = 4096
DEPTH = 2

GRID_W = 64
CTX_LEN = 256
HEAD_DIM_A = 128
HEADS_A = D_MODEL // (2 * HEAD_DIM_A)
KV_HEADS_A = HEADS_A // 2
GROUP_A = HEADS_A // KV_HEADS_A
HEAD_DIM_B = 64
HEADS_B = D_MODEL // (2 * HEAD_DIM_B)
NA_MAX_ROWS = 8
NA_WIN_COLS = 16
Q_BLOCK = 128
FOURIER_GROUPS = 4
D_FF = 2816
CONV_W = 3
ROPE_THETA = 10000.0
ROPE_PAIRS_AXIS = HEAD_DIM_A // 4
EPS = 1e-6
QA_COLS = HEADS_A * HEAD_DIM_A
KA_COLS = KV_HEADS_A * HEAD_DIM_A
B_COLS = HEADS_B * HEAD_DIM_B
Q_COLS = QA_COLS + B_COLS
IN_COLS = Q_COLS + 2 * KA_COLS + 2 * B_COLS
MIX_WIDTH = QA_COLS + B_COLS

kernel_name = "hybrid_gqa_natten_fnet_convffn_dit"


def rmsnorm(x, g):
    xf = x.astype(jnp.float32)
    y = xf * lax.rsqrt(jnp.mean(xf * xf, axis=-1, keepdims=True) + EPS)
    return (y * g.astype(jnp.float32)).astype(x.dtype)


def adaln(vec, w, b):
    m = jax.nn.silu(vec) @ w + b
    return jnp.split(m[..., None, :], 6, axis=-1)


def modulate(h, shift, scale):
    return h * (1.0 + scale) + shift


def heads(t, n):
    return t.reshape(*t.shape[:-1], n, -1)


def axial_rope(length):
    t = jnp.arange(length)
    row = (t // GRID_W).astype(jnp.float32)
    col = (t % GRID_W).astype(jnp.float32)
    freqs = ROPE_THETA ** (-jnp.arange(ROPE_PAIRS_AXIS, dtype=jnp.float32) / ROPE_PAIRS_AXIS)
    ang = jnp.concatenate([row[:, None] * freqs, col[:, None] * freqs], axis=-1)
    return jnp.cos(ang)[:, None, :], jnp.sin(ang)[:, None, :]


def apply_rope(x, cos, sin):
    xf = x.astype(jnp.float32).reshape(*x.shape[:-1], -1, 2)
    x0, x1 = xf[..., 0], xf[..., 1]
    out = jnp.stack([x0 * cos - x1 * sin, x0 * sin + x1 * cos], axis=-1)
    return out.reshape(x.shape).astype(x.dtype)


def attend(q, k, v):
    s = jnp.einsum('bqkgd,bskd->bkgqs', q, k).astype(jnp.float32)
    p = jax.nn.softmax(s, axis=-1).astype(v.dtype)
    o = jnp.einsum('bkgqs,bskd->bqkgd', p, v)
    return o.reshape(*o.shape[:2], -1)


def gqa_blocks(q, k, v):
    bn, length = q.shape[:2]
    nb = length // Q_BLOCK
    qb = q.reshape(bn, nb, Q_BLOCK, KV_HEADS_A, GROUP_A, HEAD_DIM_A).transpose(1, 0, 2, 3, 4, 5)
    o = lax.map(lambda qi: attend(qi, k, v), qb)
    return o.transpose(1, 0, 2, 3).reshape(bn, length, -1)


def neighbourhood_attention(q, k, v, ck, cv, rpb):
    bn, length, nh, hd = q.shape
    rows = length // GRID_W
    wr = min(NA_MAX_ROWS, rows)
    grid = lambda t: t.reshape(bn, rows, GRID_W, nh, hd)
    kg, vg = grid(k), grid(v)
    r = jnp.arange(rows)
    col = jnp.arange(GRID_W)
    row_start = jnp.clip(r - wr // 2, 0, rows - wr)
    col_idx = (jnp.clip(col - NA_WIN_COLS // 2, 0, GRID_W - NA_WIN_COLS)[:, None]
               + jnp.arange(NA_WIN_COLS))
    row_off = row_start[:, None] + jnp.arange(wr) - r[:, None] + NA_MAX_ROWS - 1
    col_off = col_idx - col[:, None] + NA_WIN_COLS - 1
    n_win = wr * NA_WIN_COLS

    def row_block(args):
        q_r, r0, off_r = args
        kw = lax.dynamic_slice_in_dim(kg, r0, wr, axis=1)[:, :, col_idx]
        vw = lax.dynamic_slice_in_dim(vg, r0, wr, axis=1)[:, :, col_idx]
        bias = rpb[:, off_r][:, :, col_off].transpose(0, 2, 1, 3)
        s_win = jnp.einsum('bchd,bicjhd->bhcij', q_r, kw).astype(jnp.float32) + bias.astype(jnp.float32)
        s_ctx = jnp.einsum('bchd,bshd->bhcs', q_r, ck).astype(jnp.float32)
        s = jnp.concatenate([s_win.reshape(bn, nh, GRID_W, n_win), s_ctx], axis=-1)
        p = jax.nn.softmax(s, axis=-1).astype(v.dtype)
        p_win = p[..., :n_win].reshape(bn, nh, GRID_W, wr, NA_WIN_COLS)
        o = (jnp.einsum('bhcij,bicjhd->bchd', p_win, vw)
             + jnp.einsum('bhcs,bshd->bchd', p[..., n_win:], cv))
        return o.reshape(bn, GRID_W, nh * hd)

    o = lax.map(row_block, (grid(q).transpose(1, 0, 2, 3, 4), row_start, row_off))
    return o.transpose(1, 0, 2, 3).reshape(bn, length, nh * hd)


def attention_mixer(h, hc, w_in, w_out, q_g, k_g, rpb, ctx_out):
    bn, length, _ = h.shape
    kv_splits = [KA_COLS, 2 * KA_COLS, 2 * KA_COLS + B_COLS]
    qa, qb, ka, va, kb, vb = jnp.split(h @ w_in, [QA_COLS, Q_COLS] + [Q_COLS + s for s in kv_splits], axis=-1)
    cka, cva, ckb, cvb = jnp.split(hc @ w_in[:, Q_COLS:], kv_splits, axis=-1)
    scale_a = HEAD_DIM_A ** -0.5
    scale_b = HEAD_DIM_B ** -0.5
    cos, sin = axial_rope(length)
    qa = apply_rope(rmsnorm(heads(qa, HEADS_A), q_g), cos, sin) * scale_a
    ka = apply_rope(rmsnorm(heads(ka, KV_HEADS_A), k_g), cos, sin)
    cka = rmsnorm(heads(cka, KV_HEADS_A), k_g)
    cva = heads(cva, KV_HEADS_A)
    oa = gqa_blocks(qa, jnp.concatenate([ka, cka], axis=1),
                    jnp.concatenate([heads(va, KV_HEADS_A), cva], axis=1))
    ckb = heads(ckb, HEADS_B)
    cvb = heads(cvb, HEADS_B)
    ob = neighbourhood_attention(heads(qb, HEADS_B) * scale_b, heads(kb, HEADS_B), heads(vb, HEADS_B),
                                 ckb, cvb, rpb)
    y = jnp.concatenate([oa, ob], axis=-1) @ w_out
    if not ctx_out:
        return y, None
    cq = hc @ w_in[:, :Q_COLS]
    cqa, cqb = jnp.split(cq, [QA_COLS], axis=-1)
    cqa = (rmsnorm(heads(cqa, HEADS_A), q_g) * scale_a).reshape(*hc.shape[:2], KV_HEADS_A, GROUP_A, HEAD_DIM_A)
    coa = attend(cqa, cka, cva)
    cqb = (heads(cqb, HEADS_B) * scale_b)[:, :, :, None, :]
    cob = attend(cqb.reshape(*hc.shape[:2], HEADS_B, 1, HEAD_DIM_B), ckb, cvb)
    yc = jnp.concatenate([coa, cob], axis=-1) @ w_out
    return y, yc


def fourier_mixer(h, w_out):
    bn, length, d = h.shape
    hg = h.astype(jnp.float32).reshape(bn, length, FOURIER_GROUPS, d // FOURIER_GROUPS)
    y = jnp.fft.fft2(hg, axes=(1, 3), norm='ortho').real
    return y.reshape(bn, length, d).astype(h.dtype) @ w_out


def conv_ffn(h, w_up, conv_w, conv_b, w_down):
    u = h @ w_up
    up = jnp.pad(u, ((0, 0), (1, 1), (0, 0)))
    u = up[:, :-2] * conv_w[0] + up[:, 1:-1] * conv_w[1] + up[:, 2:] * conv_w[2] + conv_b
    g, val = jnp.split(u, 2, axis=-1)
    return (jax.nn.silu(g) * val) @ w_down


def setup_inputs(seed: int = 0) -> dict:
    key = jax.random.key(seed)
    ks = jax.random.split(key, 20)
    n_even = (DEPTH + 1) // 2
    n_odd = DEPTH // 2
    nrm = lambda k, shape, s: jax.random.normal(k, shape, jnp.float32) * s
    return {
        "x": nrm(ks[0], (BATCH, SEQ, D_MODEL), 1.0),
        "c": nrm(ks[1], (BATCH, D_MODEL), 1.0),
        "ctx": nrm(ks[2], (BATCH, CTX_LEN, D_MODEL), 1.0),
        "c_ctx": nrm(ks[3], (D_MODEL,), 1.0),
        "mod_w": nrm(ks[4], (DEPTH, D_MODEL, 6 * D_MODEL), 0.5 * D_MODEL ** -0.5),
        "mod_b": nrm(ks[5], (DEPTH, 6 * D_MODEL), 0.02),
        "norm1_g": 1.0 + nrm(ks[6], (DEPTH, D_MODEL), 0.02),
        "norm2_g": 1.0 + nrm(ks[7], (DEPTH, D_MODEL), 0.02),
        "attn_w_in": nrm(ks[8], (n_even, D_MODEL, IN_COLS), D_MODEL ** -0.5),
        "attn_w_out": nrm(ks[9], (n_even, MIX_WIDTH, D_MODEL), MIX_WIDTH ** -0.5),
        "q_norm_g": 1.0 + nrm(ks[10], (n_even, HEAD_DIM_A), 0.02),
        "k_norm_g": 1.0 + nrm(ks[11], (n_even, HEAD_DIM_A), 0.02),
        "na_rpb": nrm(ks[12], (n_even, HEADS_B, 2 * NA_MAX_ROWS - 1, 2 * NA_WIN_COLS - 1), 0.1),
        "fourier_w_out": nrm(ks[13], (n_odd, D_MODEL, D_MODEL), D_MODEL ** -0.5),
        "ffn_w_up": nrm(ks[14], (DEPTH, D_MODEL, 2 * D_FF), D_MODEL ** -0.5),
        "ffn_conv_w": nrm(ks[15], (DEPTH, CONV_W, 2 * D_FF), CONV_W ** -0.5),
        "ffn_conv_b": nrm(ks[16], (DEPTH, 2 * D_FF), 0.02),
        "ffn_w_down": nrm(ks[17], (DEPTH, D_FF, D_MODEL), D_FF ** -0.5),
        "final_g": 1.0 + nrm(ks[18], (D_MODEL,), 0.02),
    }


def reference(x, c, ctx, c_ctx, mod_w, mod_b, norm1_g, norm2_g, attn_w_in, attn_w_out, q_norm_g,
              k_norm_g, na_rpb, fourier_w_out, ffn_w_up, ffn_conv_w, ffn_conv_b, ffn_w_down, final_g):
    for i in range(DEPTH):
        ctx_live = any(j % 2 == 0 for j in range(i + 1, DEPTH))
        sh1, sc1, g1, sh2, sc2, g2 = adaln(c, mod_w[i], mod_b[i])
        h = modulate(rmsnorm(x, norm1_g[i]), sh1, sc1)
        need_hc = (i % 2 == 0) or ctx_live
        if need_hc:
            csh1, csc1, cg1, csh2, csc2, cg2 = adaln(c_ctx, mod_w[i], mod_b[i])
            hc = modulate(rmsnorm(ctx, norm1_g[i]), csh1, csc1)
        if i % 2 == 0:
            e = i // 2
            y, yc = attention_mixer(h, hc, attn_w_in[e], attn_w_out[e], q_norm_g[e], k_norm_g[e],
                                    na_rpb[e], ctx_live)
        else:
            y = fourier_mixer(h, fourier_w_out[i // 2])
            yc = fourier_mixer(hc, fourier_w_out[i // 2]) if ctx_live else None
        x = x + g1 * y
        x = x + g2 * conv_ffn(modulate(rmsnorm(x, norm2_g[i]), sh2, sc2),
                              ffn_w_up[i], ffn_conv_w[i], ffn_conv_b[i], ffn_w_down[i])
        if ctx_live:
            ctx = ctx + cg1 * yc
            ctx = ctx + cg2 * conv_ffn(modulate(rmsnorm(ctx, norm2_g[i]), csh2, csc2),
                                       ffn_w_up[i], ffn_conv_w[i], ffn_conv_b[i], ffn_w_down[i])
    return rmsnorm(x, final_g)
```

```python
import numpy as np
import ml_dtypes
from contextlib import ExitStack
import concourse.bass as bass
import concourse.mybir as mybir
from concourse.bass_utils import run_bass_kernel_spmd

F32 = mybir.dt.float32
BF = mybir.dt.bfloat16
AF = mybir.ActivationFunctionType
ALU = mybir.AluOpType

D = 1024
L = 4096
CTX = 256
DFF = 2816
NPAIR = DFF // 128
GW = 64
EPS = 1e-6
NEG = -30000.0
SCALE_A = 128 ** -0.5
SCALE_B = 64 ** -0.5


class Tok:
    __slots__ = ("w", "r", "name")

    def __init__(self, name=""):
        self.w = None
        self.r = []
        self.name = name


class Lane:
    def __init__(self, sem):
        self.sem = sem
        self.count = 0
        self.last = None


class Prog:
    ENGS = ("pe", "act", "dve", "pool", "sp")

    def __init__(self, nc, stack):
        self.nc = nc
        self.stack = stack
        self.ops = []
        self.lanes = []
        self.esem = {e: stack.enter_context(nc.semaphore("es_" + e)) for e in self.ENGS}
        self.last_on = {e: None for e in self.ENGS}

    def tok(self, name=""):
        return Tok(name)

    def toks(self, n, name=""):
        return [Tok(name + str(i)) for i in range(n)]

    def lane(self, name):
        ln = Lane(self.stack.enter_context(self.nc.semaphore("ln%d_%s" % (len(self.lanes), name))))
        self.lanes.append(ln)
        return ln

    def op(self, eng, fn, reads=(), writes=(), lane=None, extra_deps=()):
        i = len(self.ops)
        deps = set(extra_deps)
        for t in reads:
            if t.w is not None:
                deps.add(t.w)
        for t in writes:
            if t.w is not None:
                deps.add(t.w)
            last = {}
            for j in t.r:
                oj = self.ops[j]
                if oj["lane"] is not None:
                    deps.add(j)
                else:
                    last[oj["eng"]] = max(last.get(oj["eng"], -1), j)
            deps.update(last.values())
        for t in reads:
            t.r.append(i)
        for t in writes:
            t.w = i
            t.r = []
        deps.discard(i)
        self.ops.append(dict(eng=eng, fn=fn, deps=deps, lane=lane, signal=False, sigval=None))
        self.last_on[eng] = i
        if lane is not None:
            lane.last = i
        return i

    def dma(self, q, out, in_, reads=(), writes=(), lane=None):
        assert lane is not None
        return self.op(q, lambda e: e.dma_start(out=out, in_=in_), reads, writes, lane=lane)

    def barrier(self):
        deps = set(v for v in self.last_on.values() if v is not None)
        deps.update(ln.last for ln in self.lanes if ln.last is not None)
        for e in self.ENGS:
            self.op(e, lambda eng: eng.nop(), extra_deps=deps)

    def emit(self, final_lanes=()):
        nc = self.nc
        ops = self.ops
        for i, o in enumerate(ops):
            for j in o["deps"]:
                d = ops[j]
                if d["lane"] is not None:
                    continue
                if d["eng"] == "pe" and o["eng"] == "pe" and o["lane"] is None:
                    continue
                d["signal"] = True
        cnt = {e: 0 for e in self.ENGS}
        for o in ops:
            if o["lane"] is not None:
                o["lane"].count += 16
                o["sigval"] = o["lane"].count
            elif o["signal"]:
                cnt[o["eng"]] += 1
                o["sigval"] = cnt[o["eng"]]
        per_eng = {e: [] for e in self.ENGS}
        for i, o in enumerate(ops):
            per_eng[o["eng"]].append(i)

        def run(ename, eng):
            waited = {}
            for i in per_eng[ename]:
                o = ops[i]
                need = {}
                for j in o["deps"]:
                    d = ops[j]
                    if d["lane"] is not None:
                        sem = d["lane"].sem
                    else:
                        if d["eng"] == "pe" and ename == "pe" and o["lane"] is None:
                            continue
                        sem = self.esem[d["eng"]]
                    k = id(sem)
                    if need.get(k, (None, 0))[1] < d["sigval"]:
                        need[k] = (sem, d["sigval"])
                for k, (sem, v) in need.items():
                    if waited.get(k, 0) < v:
                        eng.wait_ge(sem, v)
                        waited[k] = v
                ins = o["fn"](eng)
                if o["lane"] is not None:
                    ins.then_inc(o["lane"].sem, 16)
                elif o["signal"]:
                    ins.then_inc(self.esem[ename], 1)
            if ename == "sp":
                for ln in final_lanes:
                    if ln.count:
                        eng.wait_ge(ln.sem, ln.count)

        with nc.Block() as block:
            @block.tensor
            def _(e):
                run("pe", e)

            @block.scalar
            def _(e):
                run("act", e)

            @block.vector
            def _(e):
                run("dve", e)

            @block.gpsimd
            def _(e):
                run("pool", e)

            @block.sync
            def _(e):
                run("sp", e)


class Rot:
    def __init__(self, items):
        self.items = items
        self.i = 0

    def next(self):
        it = self.items[self.i % len(self.items)]
        self.i += 1
        return it


def na_row_plan():
    keys = {}
    plan = []
    for r in range(64):
        r0 = min(max(r - 4, 0), 56)
        m_lo = r0 // 2
        m_hi = (r0 + 7) // 2
        row = []
        for m in range(m_lo, m_hi + 1):
            v = tuple(1 if r0 <= 2 * m + e < r0 + 8 else 0 for e in (0, 1))
            key = (2 * m - r, v[0], v[1])
            if key not in keys:
                keys[key] = len(keys)
            row.append((m, keys[key]))
        plan.append(row)
    return plan, keys


def build_bias_tiles(rpb):
    plan, keys = na_row_plan()
    nt = len(keys)
    out = np.full((nt, 8, 128, 64), NEG, np.float32)
    c = np.arange(64)
    c0 = np.clip(c - 8, 0, 48)
    kc = np.arange(64)
    colok = (kc[:, None] >= c0[None, :]) & (kc[:, None] < c0[None, :] + 16)
    dc = kc[:, None] - c[None, :] + 15
    dcc = np.clip(dc, 0, 30)
    for (dr0, v0, v1), tid in keys.items():
        for e, v in ((0, v0), (1, v1)):
            if not v:
                continue
            a = dr0 + e + 7
            assert 0 <= a <= 14
            vals = rpb[:, a, :][:, dcc]
            blk = np.where(colok[None], vals, np.float32(NEG))
            out[tid, :, e * 64:(e + 1) * 64, :] = blk
    out = out.reshape(nt * 8, 128, 64).transpose(1, 0, 2)
    return np.ascontiguousarray(out), plan, nt


def rope_tables():
    t = np.arange(L)
    row = (t // GW).astype(np.float32)
    col = (t % GW).astype(np.float32)
    freqs = (10000.0 ** (-np.arange(32, dtype=np.float32) / 32)).astype(np.float32)
    ang = np.concatenate([row[:, None] * freqs, col[:, None] * freqs], axis=-1)
    cos = np.cos(ang).astype(np.float32)
    sin = np.sin(ang).astype(np.float32)
    COS = np.repeat(cos, 2, axis=1).T
    SINS = np.repeat(sin, 2, axis=1).T.copy()
    SINS[0::2] *= -1.0
    return (np.ascontiguousarray(COS).astype(ml_dtypes.bfloat16),
            np.ascontiguousarray(SINS).astype(ml_dtypes.bfloat16))


def dft_tables():
    t = np.arange(L, dtype=np.int64)
    ph = (np.outer(t, t) % L).astype(np.float64) * (2 * np.pi / L)
    CL = (np.cos(ph) / 64.0).astype(ml_dtypes.bfloat16)
    SL = (np.sin(ph) / 64.0).astype(ml_dtypes.bfloat16)
    j = np.arange(256, dtype=np.int64)
    ph2 = (np.outer(j, j) % 256).astype(np.float64) * (2 * np.pi / 256)
    C2 = (np.cos(ph2) / 16.0).astype(ml_dtypes.bfloat16)
    S2n = (-np.sin(ph2) / 16.0).astype(ml_dtypes.bfloat16)
    return CL, SL, C2, S2n


_CONST = {}


def consts():
    if not _CONST:
        COS, SINS = rope_tables()
        CL, SL, C2, S2n = dft_tables()
        ident = np.eye(128, dtype=np.float32)
        rm = np.zeros((128, 128), np.float32)
        for d in range(128):
            rm[d ^ 1, d] = 1.0
        _CONST.update(
            rope_cos=COS, rope_sin=SINS, dftc_full=CL, dfts_full=SL, c256=C2, s256n=S2n,
            ident_bf=ident.astype(ml_dtypes.bfloat16),
            ones_bf=np.ones((128, 128), np.float32).astype(ml_dtypes.bfloat16),
            rm_bf=rm.astype(ml_dtypes.bfloat16),
            ident32=ident.copy(),
        )
    return _CONST


def build(debug=(), stop_after=None, skip_att=False, ffn0_src=None, four_src=None):
    plan, keys = na_row_plan()
    NTB = len(keys)
    nc = bass.Bass("TRN2", target_bir_lowering=False)

    def din(name, shape, dt=F32):
        return nc.dram_tensor(name, list(shape), dt, kind="ExternalInput").ap()

    xT = din("xT", [D, L])
    ctxT = din("ctxT", [D, CTX])
    cc = din("cc", [128, 8, 2])
    mod_w = din("mod_w", [2, D, 6 * D])
    mod_b2 = din("mod_b2", [2, 2, 6 * D])
    ng = din("ng", [128, 5, 8])
    w_in = din("w_in", [D, 2560])
    w_out = din("w_out", [D, D])
    qkg = din("qkg", [128, 2])
    btiles = din("btiles", [128, NTB * 8, 64])
    rope_cos = din("rope_cos", [128, L], BF)
    rope_sin = din("rope_sin", [128, L], BF)
    fw = din("fw", [D, D])
    w_up = din("w_up", [2, NPAIR, 128, 8 * 256])
    cw = din("cw", [128, 2, 44, 4])
    w_down = din("w_down", [2, 8, 128, NPAIR * 128])
    NLOC = 2048
    dftc = din("dftc", [L, NLOC + 2], BF)
    dfts = din("dfts", [L, NLOC + 2], BF)
    hmask = din("hmask", [128, 2])
    c256 = din("c256", [256, 256], BF)
    s256n = din("s256n", [256, 256], BF)
    ident_bf_d = din("ident_bf", [128, 128], BF)
    ones_bf_d = din("ones_bf", [128, 128], BF)
    rm_bf_d = din("rm_bf", [128, 128], BF)
    ident32_d = din("ident32", [128, 128])

    yT = nc.dram_tensor("yT", [D, 2048], F32, kind="ExternalOutput").ap()
    dbg = {}
    for name, shape in debug:
        dbg[name] = nc.dram_tensor("dbg_" + name, list(shape), F32, kind="ExternalOutput").ap()

    x1T = nc.dram_tensor("x1T_scr", [D, L], F32, kind="Internal").ap()
    x2T = nc.dram_tensor("x2T_scr", [D, L], F32, kind="Internal").ap()
    x3T = nc.dram_tensor("x3T_scr", [D, 2048 + 2], F32, kind="Internal").ap()
    wupb = nc.dram_tensor("wupb_scr", [2, NPAIR, 128, 8 * 256], BF, kind="Internal").ap()
    wdb = nc.dram_tensor("wdb_scr", [2, 8, 128, NPAIR * 128], BF, kind="Internal").ap()

    def fm(ap):
        return ap.rearrange("(c p) t -> p c t", p=128)

    with ExitStack() as G:
        p = Prog(nc, G)
        lane_out = p.lane("out")
        lane_c = p.lane("const")
        lane_scr = p.lane("scr")
        lane_r = p.lane("rope")

        ucnt = [0]

        def sbuf(st, name, shape, dt):
            ucnt[0] += 1
            return st.enter_context(nc.sbuf_tensor("s%d_%s" % (ucnt[0], name), list(shape), dt))

        banks = []
        for i in range(8):
            t = G.enter_context(nc.psum_tensor("psb%d" % i, [128, 512], F32))
            banks.append((t, p.tok("psb%d" % i)))

        ident_bf = sbuf(G, "ident_bf", [128, 128], BF)
        ones_bf = sbuf(G, "ones_bf", [128, 128], BF)
        rm_bf = sbuf(G, "rm_bf", [128, 128], BF)
        ident32 = sbuf(G, "ident32", [128, 128], F32)
        ng_sb = sbuf(G, "ng_sb", [128, 5, 8], F32)
        qkg_sb = sbuf(G, "qkg_sb", [128, 2], F32)
        cw_sb = sbuf(G, "cw_sb", [128, 2, 44, 4], F32)
        cc_sb = sbuf(G, "cc_sb", [128, 8, 2], F32)
        hm_sb = sbuf(G, "hm_sb", [128, 2], F32)
        sc_sb = sbuf(G, "sc_sb", [128, 8, 2], F32)
        modT = sbuf(G, "modT", [128, 2, 48, 2], F32)
        gs = sbuf(G, "gs", [128, 2, 3, 8], F32)
        tconst = p.tok("const")
        for dst, src in ((ident_bf, ident_bf_d), (ones_bf, ones_bf_d), (rm_bf, rm_bf_d), (ident32, ident32_d),
                         (ng_sb, ng), (qkg_sb, qkg), (cw_sb, cw), (cc_sb, cc), (hm_sb, hmask)):
            p.dma("sp", dst[:], src, writes=[tconst], lane=lane_c)

        tmod = p.tok("modT")
        with ExitStack() as S:
            mrow = sbuf(S, "mrow", [2, 6 * D], F32)
            mb_sb = sbuf(S, "mb_sb", [2, 6 * D], F32)
            stg = [sbuf(S, "mw_stg%d" % i, [128, 8, 512], F32) for i in range(2)]
            stg_t = p.toks(2, "mwstg")
            stg_l = [p.lane("mw%d" % i) for i in range(2)]
            tmrow = p.tok("mrow")
            tmb = p.tok("mb")
            lane_mb = p.lane("mb")
            p.op("act", lambda e: e.activation(out=sc_sb[:], in_=cc_sb[:], func=AF.Silu), reads=[tconst], writes=[tconst])
            it = 0
            for l in range(2):
                p.dma("sp", mb_sb[:], mod_b2[l], writes=[tmb], lane=lane_mb)
                for blk in range(12):
                    s = it % 2
                    it += 1
                    src = mod_w[l].rearrange("(c p) n -> p c n", p=128)[:, :, blk * 512:(blk + 1) * 512]
                    p.dma("sp", stg[s][:], src, writes=[stg_t[s]], lane=stg_l[s])
                    bk, bt = banks[s]
                    for k in range(8):
                        p.op("pe", (lambda e, s=s, k=k, bk=bk: e.matmul(bk[0:2, :], sc_sb[:, k, :], stg[s][:, k, :],
                                                                         start=(k == 0), stop=(k == 7))),
                             reads=[stg_t[s], tconst], writes=[bt])
                    p.op("dve", (lambda e, bk=bk, blk=blk: e.tensor_tensor(
                        out=mrow[:, blk * 512:(blk + 1) * 512], in0=bk[0:2, :],
                        in1=mb_sb[:, blk * 512:(blk + 1) * 512], op=ALU.add)),
                        reads=[bt, tmb], writes=[tmrow])
                bk, bt = banks[2]
                for ch in range(48):
                    p.op("pe", (lambda e, ch=ch, bk=bk: e.matmul(bk[:, ch * 2:ch * 2 + 2], mrow[0:2, ch * 128:(ch + 1) * 128],
                                                                ident32[0:2, 0:2], start=True, stop=True)),
                         reads=[tmrow, tconst], writes=[bt])
                p.op("dve", (lambda e, l=l, bk=bk: e.tensor_copy(out=modT[:, l].rearrange("p a b -> p (a b)"), in_=bk[:, 0:96])),
                     reads=[bt], writes=[tmod])
            for l in range(2):
                for (i, chunk0, j, ngi) in ((0, 8, 0, 2 * l), (1, 32, 0, 2 * l + 1), (2, 8, 1, 2 * l)):
                    p.op("dve", (lambda e, l=l, i=i, chunk0=chunk0, j=j, ngi=ngi: e.scalar_tensor_tensor(
                        out=gs[:, l, i, :], in0=modT[:, l, chunk0:chunk0 + 8, j], scalar=1.0, op0=ALU.add,
                        in1=ng_sb[:, ngi, :], op1=ALU.mult)),
                        reads=[tmod, tconst], writes=[tmod])
        p.barrier()

        _par = {}

        def core_par(e):
            if "v" not in _par:
                _par["v"] = e.partition_id() % 2
            return _par["v"]

        def mcol(l, chunk, j=0):
            return modT[:, l, chunk, j:j + 1]

        def norm_mod(xt, xoff, n, blocks, gcols, bcols, hout, hoff, t_x, t_h, ss_banks, tmp):
            sqc, lnv, rstd, xnc, t_sq, t_r, t_xn = tmp
            for c in range(8):
                i = c % 2
                p.op("act", (lambda e, c=c, i=i: e.activation(out=sqc[i][:, 0:n], in_=xt[:, c, xoff:xoff + n], func=AF.Square)),
                     reads=[t_x], writes=[t_sq[i]])
                for bi, (c0, w) in enumerate(blocks):
                    bk, bt = ss_banks[bi]
                    p.op("pe", (lambda e, c=c, i=i, bk=bk, c0=c0, w=w: e.matmul(bk[:, 0:w], ones_bf[:], sqc[i][:, c0:c0 + w],
                                                                               start=(c == 0), stop=(c == 7))),
                         reads=[t_sq[i], tconst], writes=[bt])
            for bi, (c0, w) in enumerate(blocks):
                bk, bt = ss_banks[bi]
                p.op("act", (lambda e, bk=bk, c0=c0, w=w: e.activation(out=lnv[:, c0:c0 + w], in_=bk[:, 0:w], func=AF.Ln,
                                                                       scale=1.0 / D, bias=eps_sb[:, 0:1])),
                     reads=[bt, tconst], writes=[t_r])
            p.op("act", lambda e: e.activation(out=rstd[:, 0:n], in_=lnv[:, 0:n], func=AF.Exp, scale=-0.5), reads=[t_r], writes=[t_r])
            for c in range(8):
                i = c % 2
                if bcols is not None:
                    p.op("dve", (lambda e, c=c, i=i: e.scalar_tensor_tensor(
                        out=xnc[i][:, 0:n], in0=xt[:, c, xoff:xoff + n], scalar=gcols(c), op0=ALU.mult,
                        in1=rstd[:, 0:n], op1=ALU.mult)),
                        reads=[t_x, t_r, tmod, tconst], writes=[t_xn[i]])
                    p.op("act", (lambda e, c=c, i=i: e.activation(out=hout[:, c, hoff:hoff + n], in_=xnc[i][:, 0:n],
                                                                  func=AF.Identity, bias=bcols(c))),
                         reads=[t_xn[i], tmod, tconst], writes=[t_h])
                else:
                    p.op("dve", (lambda e, c=c: e.scalar_tensor_tensor(
                        out=hout[:, c, hoff:hoff + n], in0=xt[:, c, xoff:xoff + n], scalar=gcols(c), op0=ALU.mult,
                        in1=rstd[:, 0:n], op1=ALU.mult)),
                        reads=[t_x, t_r, tmod, tconst], writes=[t_h])

        def make_norm_tmps(S_, tag, n):
            sqc = [sbuf(S_, "%s_sq%d" % (tag, i), [128, n], BF) for i in range(2)]
            lnv = sbuf(S_, tag + "_lnv", [128, n], F32)
            rstd = sbuf(S_, tag + "_rstd", [128, n], F32)
            xnc = [sbuf(S_, "%s_xn%d" % (tag, i), [128, n], F32) for i in range(2)]
            return (sqc, lnv, rstd, xnc, p.toks(2, tag + "sq"), p.tok(tag + "r"), p.toks(2, tag + "xn"))

        def load_cols(dst, t_dst, src2d, ranges, tag):
            with ExitStack() as S2:
                stg = [sbuf(S2, "%s_stg%d" % (tag, i), [128, 8, 512], F32) for i in range(2)]
                stg_t = p.toks(2, tag + "stg")
                stg_l = [p.lane("%s_l%d" % (tag, i)) for i in range(2)]
                it = 0
                d0 = 0
                for (c0, n) in ranges:
                    for b0 in range(0, n, 512):
                        w = min(512, n - b0)
                        s = it % 2
                        it += 1
                        src = src2d.rearrange("(c p) n -> p c n", p=128)[:, :, c0 + b0:c0 + b0 + w]
                        p.dma("sp", stg[s][:, :, 0:w], src, writes=[stg_t[s]], lane=stg_l[s])
                        p.op("pool", (lambda e, s=s, d0=d0, w=w: e.tensor_copy(out=dst[:, :, d0:d0 + w], in_=stg[s][:, :, 0:w])),
                             reads=[stg_t[s]], writes=[t_dst])
                        d0 += w
            p.barrier()

        eps_sb = sbuf(G, "eps_sb", [128, 1], F32)
        p.op("pool", lambda e: e.memset(eps_sb[:], EPS), writes=[tconst])
        ones32 = sbuf(G, "ones32", [128, 128], F32)
        p.op("pool", lambda e: e.memset(ones32[:], 1.0), writes=[tconst])

        def load_cast(S_, src_ap, stage, stage_t, stage_l, dst_ap, dst_t, shape_ok=True):
            p.dma("sp", stage, src_ap, writes=[stage_t], lane=stage_l)
            p.op("pool", lambda e: e.tensor_copy(out=dst_ap, in_=stage), reads=[stage_t], writes=[dst_t])

        def finish():
            p.emit(final_lanes=[lane_out])
            return nc

        def attention_phase():
            A = ExitStack()
            CKbT = sbuf(A, "CKbT", [128, 4, CTX], BF)
            CVb = sbuf(A, "CVb", [128, 2, 512], BF)
            OaT = sbuf(A, "OaT", [128, 4, L], BF)
            t_ckb = p.tok("CKb")
            t_oa = p.toks(8, "OaT")
            A1 = ExitStack()
            KaT = sbuf(A1, "KaT", [128, 2, L + CTX], BF)
            Va = sbuf(A1, "Va", [128, 34, 256], BF)
            t_ka = p.tok("KaT")
            t_va = p.tok("Va")

            QA0, QB0, KA0, VA0, KB0, VB0 = 0, 512, 1024, 1280, 1536, 2048

            def qk_head(ps_bank, n, gcol, dst_ap, tok_dst, cos_ap, sin_ap, t_rope, scale, rots, tmps):
                (pk, pt) = ps_bank
                ss_b, ss_t = rots["ss"].next()
                qb, qsq, lnv, rstd, t1, t2, t_q, t_r, t_1, t_2 = tmps.next()
                rope = cos_ap is not None
                p.op("act", lambda e: e.activation(out=qb[:, 0:n], in_=pk[:, 0:n], func=AF.Copy, scale=gcol),
                     reads=[pt, tconst], writes=[t_q])
                p.op("act", lambda e: e.activation(out=qsq[:, 0:n], in_=pk[:, 0:n], func=AF.Square), reads=[pt], writes=[t_q])
                p.op("pe", lambda e: e.matmul(ss_b[:, 0:n], ones_bf[:], qsq[:, 0:n], start=True, stop=True),
                     reads=[t_q, tconst], writes=[ss_t])
                if rope:
                    qr_b, qr_t = rots["qr"].next()
                    p.op("pe", lambda e: e.matmul(qr_b[:, 0:n], rm_bf[:], qb[:, 0:n], start=True, stop=True),
                         reads=[t_q, tconst], writes=[qr_t])
                p.op("act", lambda e: e.activation(out=lnv[:, 0:n], in_=ss_b[:, 0:n], func=AF.Ln, scale=1.0 / 128, bias=eps_sb[:, 0:1]),
                     reads=[ss_t, tconst], writes=[t_r])
                p.op("act", lambda e: e.activation(out=rstd[:, 0:n], in_=lnv[:, 0:n], func=AF.Exp, scale=-0.5), reads=[t_r], writes=[t_r])
                if rope:
                    p.op("pool", lambda e: e.tensor_tensor(out=t1[:, 0:n], in0=qb[:, 0:n], in1=cos_ap, op=ALU.mult),
                         reads=[t_q, t_rope], writes=[t_1])
                    p.op("dve", lambda e: e.tensor_tensor(out=t2[:, 0:n], in0=qr_b[:, 0:n], in1=sin_ap, op=ALU.mult),
                         reads=[qr_t, t_rope], writes=[t_2])
                    p.op("dve", lambda e: e.tensor_tensor(out=t2[:, 0:n], in0=t2[:, 0:n], in1=t1[:, 0:n], op=ALU.add),
                         reads=[t_1, t_2], writes=[t_2])
                    p.op("dve", lambda e: e.scalar_tensor_tensor(out=dst_ap, in0=t2[:, 0:n], scalar=float(scale), op0=ALU.mult,
                                                                 in1=rstd[:, 0:n], op1=ALU.mult),
                         reads=[t_2, t_r], writes=[tok_dst])
                else:
                    p.op("dve", lambda e: e.scalar_tensor_tensor(out=dst_ap, in0=qb[:, 0:n], scalar=float(scale), op0=ALU.mult,
                                                                 in1=rstd[:, 0:n], op1=ALU.mult),
                         reads=[t_q, t_r], writes=[tok_dst])

            def proj_fm(wsb, t_w, htile, t_h, n, col0, bank):
                bk, bt = bank
                for k in range(8):
                    p.op("pe", (lambda e, k=k: e.matmul(bk[:, 0:n], wsb[:, k, col0:col0 + 128], htile[:, k, 0:n],
                                                        start=(k == 0), stop=(k == 7))),
                         reads=[t_h, t_w], writes=[bt])

            def proj_tm(wsb, t_w, htile, t_h, tok0, col0, ncols, bank):
                bk, bt = bank
                for k in range(8):
                    p.op("pe", (lambda e, k=k: e.matmul(bk[:, 0:ncols], htile[:, k, tok0:tok0 + 128],
                                                        wsb[:, k, col0:col0 + ncols], start=(k == 0), stop=(k == 7))),
                         reads=[t_h, t_w], writes=[bt])

            def make_qk_tmps(S_, tag, nbuf=2):
                items = []
                for i in range(nbuf):
                    qb = sbuf(S_, "%s_qb%d" % (tag, i), [128, 512], BF)
                    qsq = sbuf(S_, "%s_qsq%d" % (tag, i), [128, 512], BF)
                    lnv = sbuf(S_, "%s_lnv%d" % (tag, i), [128, 512], F32)
                    rstd = sbuf(S_, "%s_rstd%d" % (tag, i), [128, 512], F32)
                    t1 = sbuf(S_, "%s_t1%d" % (tag, i), [128, 512], F32)
                    t2 = sbuf(S_, "%s_t2%d" % (tag, i), [128, 512], F32)
                    items.append((qb, qsq, lnv, rstd, t1, t2) + tuple(p.toks(4, tag + "tmp")))
                return Rot(items)

            with ExitStack() as S:
                wsb = sbuf(S, "pk_w", [128, 8, 1536], BF)
                t_w = p.tok("pk_w")
                load_cols(wsb, t_w, w_in, [(KA0, 1536)], "pkw")
                cos_sb = sbuf(S, "pk_cos", [128, L], BF)
                sin_sb = sbuf(S, "pk_sin", [128, L], BF)
                t_rope = p.tok("pk_rope")
                p.dma("sp", cos_sb[:], rope_cos, writes=[t_rope], lane=lane_r)
                p.dma("sp", sin_sb[:], rope_sin, writes=[t_rope], lane=lane_r)
                xt = [sbuf(S, "pk_xt%d" % i, [128, 8, 512], F32) for i in range(2)]
                xt_t = p.toks(2, "pk_xt")
                xt_l = [p.lane("pk_xt%d" % i) for i in range(2)]
                ht = [sbuf(S, "pk_ht%d" % i, [128, 8, 512], BF) for i in range(2)]
                ht_t = p.toks(2, "pk_ht")
                ntmp = make_norm_tmps(S, "pk_n", 512)
                qtmps = make_qk_tmps(S, "pk_q")
                rots = dict(ss=Rot([banks[3], banks[4]]), qr=Rot([banks[5], banks[6]]))
                prj = Rot([banks[1], banks[2]])
                for ti in range(9):
                    s = ti % 2
                    is_ctx = (ti == 8)
                    n = CTX if is_ctx else 512
                    tok0 = ti * 512
                    src = fm(ctxT) if is_ctx else fm(xT)[:, :, tok0:tok0 + 512]
                    p.dma("sp", xt[s][:, :, 0:n], src, writes=[xt_t[s]], lane=xt_l[s])
                    li = 2 if is_ctx else 0
                    jj = 1 if is_ctx else 0
                    norm_mod(xt[s], 0, n, [(0, n)], lambda c, li=li: gs[:, 0, li, c:c + 1],
                             lambda c, jj=jj: mcol(0, c, jj), ht[s], 0, xt_t[s], ht_t[s], [banks[0]], ntmp)
                    for kv in range(2):
                        bank = prj.next()
                        proj_fm(wsb, t_w, ht[s], ht_t[s], n, kv * 128, bank)
                        if is_ctx:
                            qk_head(bank, n, qkg_sb[:, 1:2], KaT[:, kv, tok0:tok0 + n], t_ka, None, None, None,
                                    1.0, rots, qtmps)
                        else:
                            qk_head(bank, n, qkg_sb[:, 1:2], KaT[:, kv, tok0:tok0 + n], t_ka,
                                    cos_sb[:, tok0:tok0 + n], sin_sb[:, tok0:tok0 + n], t_rope, 1.0, rots, qtmps)
                    for sub in range(n // 128):
                        chunk = ti * 4 + sub
                        proj_tm(wsb, t_w, ht[s], ht_t[s], sub * 128, 256, 256, banks[7])
                        p.op("act", (lambda e, chunk=chunk: e.activation(out=Va[:, chunk, :], in_=banks[7][0][:, 0:256], func=AF.Copy)),
                             reads=[banks[7][1]], writes=[t_va])
                    if is_ctx:
                        for ch in range(4):
                            bank = prj.next()
                            proj_fm(wsb, t_w, ht[s], ht_t[s], n, 512 + ch * 128, bank)
                            p.op("act", (lambda e, ch=ch, bank=bank: e.activation(out=CKbT[:, ch, :], in_=bank[0][:, 0:CTX], func=AF.Copy)),
                                 reads=[bank[1]], writes=[t_ckb])
                        for sub in range(2):
                            proj_tm(wsb, t_w, ht[s], ht_t[s], sub * 128, 1024, 512, banks[7])
                            p.op("act", (lambda e, sub=sub: e.activation(out=CVb[:, sub, :], in_=banks[7][0][:, 0:512], func=AF.Copy)),
                                 reads=[banks[7][1]], writes=[t_ckb])
            p.barrier()

            def dump_bf(name, tens, shape, tk):
                if name in dbg:
                    with ExitStack() as S:
                        tmpf = sbuf(S, "dbgt_" + name, shape, F32)
                        tt = p.tok()
                        p.op("dve", lambda e: e.tensor_copy(out=tmpf[:], in_=tens[:]), reads=tk, writes=[tt])
                        p.dma("sp", dbg[name], tmpf[:], reads=[tt], lane=lane_out)
                        p.barrier()

            dump_bf("KaT", KaT, [128, 2, L + CTX], [t_ka])
            dump_bf("Va", Va, [128, 34, 256], [t_va])
            if "modT" in dbg:
                p.dma("sp", dbg["modT"], modT[:], reads=[tmod], lane=lane_out)

            if stop_after == "PK":
                A1.close()
                A.close()
                return True

            precast_jobs = []
            for l_ in range(2):
                for j_ in range(NPAIR):
                    precast_jobs.append((w_up[l_, j_], wupb[l_, j_], 8 * 256))
                for dc_ in range(8):
                    precast_jobs.append((w_down[l_, dc_], wdb[l_, dc_], NPAIR * 128))
            precast_it = [0]
            for half in range(2):
                q0 = half * 2048
                with ExitStack() as H:
                    QaT = sbuf(H, "QaT", [128, 4, 2048], BF)
                    t_qa = p.tok("QaT")
                    with ExitStack() as S:
                        wsb = sbuf(S, "gq_w", [128, 8, 512], BF)
                        t_w = p.tok("gq_w")
                        load_cols(wsb, t_w, w_in, [(QA0, 512)], "gqw%d" % half)
                        cos_sb = sbuf(S, "gq_cos", [128, 2048], BF)
                        sin_sb = sbuf(S, "gq_sin", [128, 2048], BF)
                        t_rope = p.tok("gq_rope")
                        p.dma("sp", cos_sb[:], rope_cos[:, q0:q0 + 2048], writes=[t_rope], lane=lane_r)
                        p.dma("sp", sin_sb[:], rope_sin[:, q0:q0 + 2048], writes=[t_rope], lane=lane_r)
                        xt = [sbuf(S, "gq_xt%d" % i, [128, 8, 512], F32) for i in range(2)]
                        xt_t = p.toks(2, "gq_xt")
                        xt_l = [p.lane("gq_xt%d_%d" % (half, i)) for i in range(2)]
                        ht = [sbuf(S, "gq_ht%d" % i, [128, 8, 512], BF) for i in range(2)]
                        ht_t = p.toks(2, "gq_ht")
                        ntmp = make_norm_tmps(S, "gq_n", 512)
                        qtmps = make_qk_tmps(S, "gq_q")
                        rots = dict(ss=Rot([banks[3], banks[4]]), qr=Rot([banks[5], banks[6]]))
                        prj = Rot([banks[1], banks[2]])
                        for ti in range(4):
                            s = ti % 2
                            tok0 = q0 + ti * 512
                            lt0 = ti * 512
                            p.dma("sp", xt[s][:], fm(xT)[:, :, tok0:tok0 + 512], writes=[xt_t[s]], lane=xt_l[s])
                            norm_mod(xt[s], 0, 512, [(0, 512)], lambda c: gs[:, 0, 0, c:c + 1], lambda c: mcol(0, c, 0),
                                     ht[s], 0, xt_t[s], ht_t[s], [banks[0]], ntmp)
                            for a in range(4):
                                bank = prj.next()
                                proj_fm(wsb, t_w, ht[s], ht_t[s], 512, a * 128, bank)
                                qk_head(bank, 512, qkg_sb[:, 0:1], QaT[:, a, lt0:lt0 + 512], t_qa,
                                        cos_sb[:, lt0:lt0 + 512], sin_sb[:, lt0:lt0 + 512], t_rope, SCALE_A, rots, qtmps)
                    p.barrier()
                    with ExitStack() as S:
                        pts = [sbuf(S, "g_pt%d" % i, [128, 512], BF) for i in range(4)]
                        pt_t = p.toks(4, "g_pt")
                        rl = sbuf(S, "g_rl", [128, 512], F32)
                        t_rl = p.tok("g_rl")
                        accs = [sbuf(S, "g_acc%d" % i, [128, 512], F32) for i in range(2)]
                        accs_t = p.toks(2, "g_acc")
                        pc32 = [sbuf(S, "pc32_%d" % i, [128, NPAIR * 128], F32) for i in range(2)]
                        pc16 = [sbuf(S, "pc16_%d" % i, [128, NPAIR * 128], BF) for i in range(2)]
                        pc32_t = p.toks(2, "pc32")
                        pc16_t = p.toks(2, "pc16")
                        pc32_l = [p.lane("pc32_%d_%d" % (half, i)) for i in range(2)]
                        pc16_l = [p.lane("pc16_%d_%d" % (half, i)) for i in range(2)]
                        srot = Rot([banks[0], banks[1], banks[2], banks[7]])
                        orot = Rot([(banks[3], banks[5]), (banks[4], banks[6])])
                        NKC = 34
                        for a in range(4):
                            kv = a // 2
                            for qt in range(4):
                                for _ in range(2):
                                    if precast_jobs:
                                        (srcw, dstw, nel) = precast_jobs.pop(0)
                                        s = precast_it[0] % 2
                                        precast_it[0] += 1
                                        p.dma("sp", pc32[s][:, 0:nel], srcw, writes=[pc32_t[s]], lane=pc32_l[s])
                                        p.op("pool", (lambda e, s=s, nel=nel: e.tensor_copy(out=pc16[s][:, 0:nel], in_=pc32[s][:, 0:nel])),
                                             reads=[pc32_t[s]], writes=[pc16_t[s]])
                                        p.dma("sp", dstw, pc16[s][:, 0:nel], reads=[pc16_t[s]], lane=pc16_l[s])
                                (ob, ot), (lb, lt) = orot.next()
                                acc_i = (a * 4 + qt) % 2
                                qsl = slice(qt * 512, (qt + 1) * 512)
                                gsl = slice(q0 + qt * 512, q0 + (qt + 1) * 512)

                                def s_mm(kc, a=a, kv=kv, qsl=qsl):
                                    sb_, st_ = srot.next()
                                    p.op("pe", lambda e: e.matmul(sb_[:, :], KaT[:, kv, kc * 128:(kc + 1) * 128], QaT[:, a, qsl],
                                                                  start=True, stop=True),
                                         reads=[t_ka, t_qa], writes=[st_])
                                    return sb_, st_

                                def pv_mm(kc, sbk, stk, kv=kv, ob=ob, ot=ot, acc=None, acc_t=None):
                                    i = kc % 4
                                    p.op("act", lambda e: e.activation(out=pts[i][:], in_=sbk[:, :], func=AF.Exp),
                                         reads=[stk], writes=[pt_t[i]])
                                    p.op("pe", lambda e: e.matmul(ob[:, :], Va[:, kc, kv * 128:(kv + 1) * 128], pts[i][:],
                                                                  start=(kc == 0), stop=(kc == NKC - 1)),
                                         reads=[pt_t[i], t_va], writes=[ot])
                                    if kc == 0:
                                        p.op("dve", lambda e: e.tensor_copy(out=acc[:], in_=pts[i][:]), reads=[pt_t[i]], writes=[acc_t])
                                    else:
                                        p.op("dve", lambda e: e.tensor_tensor(out=acc[:], in0=acc[:], in1=pts[i][:], op=ALU.add),
                                             reads=[pt_t[i]], writes=[acc_t])

                                q_s = [s_mm(0), s_mm(1)]
                                for kc in range(NKC):
                                    if kc + 2 < NKC:
                                        q_s.append(s_mm(kc + 2))
                                    cur = q_s.pop(0)
                                    pv_mm(kc, cur[0], cur[1], acc=accs[acc_i], acc_t=accs_t[acc_i])
                                p.op("pe", (lambda e, lb=lb, acc_i=acc_i: e.matmul(lb[:, :], ones32[:], accs[acc_i][:], start=True, stop=True)),
                                     reads=[accs_t[acc_i], tconst], writes=[lt])
                                p.op("dve", lambda e, lb=lb: e.reciprocal(out=rl[:], in_=lb[:, :]), reads=[lt], writes=[t_rl])
                                p.op("dve", (lambda e, ob=ob, a=a, gsl=gsl: e.tensor_tensor(out=OaT[:, a, gsl], in0=ob[:, :], in1=rl[:],
                                                                                           op=ALU.mult)),
                                     reads=[ot, t_rl], writes=[t_oa[half * 4 + qt]])
                        assert half == 0 or not precast_jobs
                    p.barrier()
            A1.close()
            p.barrier()
            dump_bf("OaT", OaT, [128, 4, L], t_oa)
            if stop_after == "GQA":
                A.close()
                return True

            for half in range(2):
                q0 = half * 2048
                kf0 = 0 if half == 0 else 1536
                with ExitStack() as H:
                    QbT = sbuf(H, "QbT", [128, 4, 2048], BF)
                    KbT = sbuf(H, "KbT", [128, 4, 2560], BF)
                    Vb = sbuf(H, "Vb", [128, 20, 512], BF)
                    t_qb, t_kb, t_vb = p.toks(3, "na")
                    with ExitStack() as S:
                        wsb = sbuf(S, "na_w", [128, 8, 1536], BF)
                        t_w = p.tok("na_w")
                        load_cols(wsb, t_w, w_in, [(QB0, 512), (KB0, 1024)], "naw%d" % half)
                        xt = [sbuf(S, "na_xt%d" % i, [128, 8, 512], F32) for i in range(2)]
                        xt_t = p.toks(2, "na_xt")
                        xt_l = [p.lane("na_xt%d_%d" % (half, i)) for i in range(2)]
                        ht = [sbuf(S, "na_ht%d" % i, [128, 8, 512], BF) for i in range(2)]
                        ht_t = p.toks(2, "na_ht")
                        ntmp = make_norm_tmps(S, "na_n", 512)
                        prj = Rot([banks[1], banks[2], banks[3]])
                        vrot = Rot([banks[6], banks[7]])
                        for ti in range(5):
                            s = ti % 2
                            tok0 = kf0 + ti * 512
                            own = (q0 <= tok0 < q0 + 2048)
                            p.dma("sp", xt[s][:], fm(xT)[:, :, tok0:tok0 + 512], writes=[xt_t[s]], lane=xt_l[s])
                            norm_mod(xt[s], 0, 512, [(0, 512)], lambda c: gs[:, 0, 0, c:c + 1], lambda c: mcol(0, c, 0),
                                     ht[s], 0, xt_t[s], ht_t[s], [banks[0]], ntmp)
                            for ch in range(4):
                                bank = prj.next()
                                proj_fm(wsb, t_w, ht[s], ht_t[s], 512, 512 + ch * 128, bank)
                                p.op("act", (lambda e, ch=ch, bank=bank, ti=ti: e.activation(
                                    out=KbT[:, ch, ti * 512:(ti + 1) * 512], in_=bank[0][:, 0:512], func=AF.Copy)),
                                    reads=[bank[1]], writes=[t_kb])
                            for sub in range(4):
                                vb_ = vrot.next()
                                proj_tm(wsb, t_w, ht[s], ht_t[s], sub * 128, 1024, 512, vb_)
                                p.op("dve", (lambda e, c=ti * 4 + sub, vb_=vb_: e.tensor_copy(out=Vb[:, c, :], in_=vb_[0][:, 0:512])),
                                     reads=[vb_[1]], writes=[t_vb])
                            if own:
                                lt0 = tok0 - q0
                                for ch in range(4):
                                    bank = prj.next()
                                    proj_fm(wsb, t_w, ht[s], ht_t[s], 512, ch * 128, bank)
                                    p.op("act", (lambda e, ch=ch, bank=bank, lt0=lt0: e.activation(
                                        out=QbT[:, ch, lt0:lt0 + 512], in_=bank[0][:, 0:512], func=AF.Copy, scale=SCALE_B)),
                                        reads=[bank[1]], writes=[t_qb])
                    p.barrier()
                    with ExitStack() as S:
                        bt_sb = sbuf(S, "bt_sb", [128, NTB * 8, 64], BF)
                        t_bt = p.tok("bt")
                        wout_bf = sbuf(S, "wout_bf", [128, 8, D], BF)
                        t_wo = p.tok("wout")
                        load_cols(wout_bf, t_wo, w_out, [(0, D)], "wo%d" % half)
                        with ExitStack() as S2:
                            NP = 8
                            per = (NTB * 8 + NP - 1) // NP
                            stg = sbuf(S2, "bt_stg", [128, per, 64], F32)
                            stg_t = p.tok("btstg")
                            stg_l = p.lane("btstg%d" % half)
                            for pi in range(NP):
                                a0 = pi * per
                                a1 = min(NTB * 8, a0 + per)
                                if a1 <= a0:
                                    continue
                                p.dma("sp", stg[:, 0:a1 - a0, :], btiles[:, a0:a1, :], writes=[stg_t], lane=stg_l)
                                p.op("pool", (lambda e, a0=a0, a1=a1: e.tensor_copy(out=bt_sb[:, a0:a1, :], in_=stg[:, 0:a1 - a0, :])),
                                     reads=[stg_t], writes=[t_bt])
                        p.barrier()
                        ObT = [sbuf(S, "ObT%d" % i, [128, 4, 512], BF) for i in range(2)]
                        ob_t = p.toks(2, "ObT")
                        ptA = [sbuf(S, "na_ptA%d" % i, [128, 512], BF) for i in range(2)]
                        ptB = [sbuf(S, "na_ptB%d" % i, [128, 512], BF) for i in range(2)]
                        ptA_t = p.toks(2, "na_ptA")
                        ptB_t = p.toks(2, "na_ptB")
                        rln = [sbuf(S, "na_rl%d" % i, [128, 64], F32) for i in range(2)]
                        rln_t = p.toks(2, "na_rl")
                        xr = [sbuf(S, "na_xr%d" % i, [128, 8, 512], F32) for i in range(2)]
                        xr_t = p.toks(2, "na_xr")
                        xr_l = [p.lane("na_xr%d_%d" % (half, i)) for i in range(2)]
                        xo_l = [p.lane("na_xo%d_%d" % (half, i)) for i in range(2)]
                        srot = Rot([(banks[0], banks[1]), (banks[2], banks[3])])
                        olrot = Rot([banks[4], banks[5]])
                        worot = Rot([banks[6], banks[7]])
                        it = 0
                        for rt in range(4):
                            ts_ = rt % 2
                            tokA = q0 + rt * 512
                            p.dma("sp", xr[ts_][:], fm(xT)[:, :, tokA:tokA + 512], writes=[xr_t[ts_]], lane=xr_l[ts_])
                            for rr in range(8):
                                r = half * 32 + rt * 8 + rr
                                rowplan = plan[r]
                                nw = len(rowplan)
                                ncols = (nw + 2) * 64
                                ql = slice((rt * 8 + rr) * 64, (rt * 8 + rr + 1) * 64)
                                for pr in range(4):
                                    (sA, sB) = srot.next()
                                    olb, olt = olrot.next()
                                    bsel = it % 2
                                    it += 1
                                    for hh, (sbk, stk) in enumerate((sA, sB)):
                                        h = pr * 2 + hh
                                        prt = slice(hh * 64, hh * 64 + 64)
                                        for wi, (m, tid) in enumerate(rowplan):
                                            kt0 = m * 128 - kf0
                                            p.op("pe", (lambda e, sbk=sbk, wi=wi, kt0=kt0, prt=prt, pr=pr, ql=ql: e.matmul(
                                                sbk[:, wi * 64:(wi + 1) * 64], KbT[prt, pr, kt0:kt0 + 128], QbT[prt, pr, ql],
                                                start=True, stop=False)),
                                                reads=[t_kb, t_qb], writes=[stk])
                                            p.op("pe", (lambda e, sbk=sbk, wi=wi, tid=tid, h=h: e.matmul(
                                                sbk[:, wi * 64:(wi + 1) * 64], ident_bf[:], bt_sb[:, tid * 8 + h, :],
                                                start=False, stop=True)),
                                                reads=[t_bt, tconst], writes=[stk])
                                        for cc_ in range(2):
                                            wi = nw + cc_
                                            p.op("pe", (lambda e, sbk=sbk, wi=wi, cc_=cc_, prt=prt, pr=pr, ql=ql: e.matmul(
                                                sbk[:, wi * 64:(wi + 1) * 64], CKbT[prt, pr, cc_ * 128:(cc_ + 1) * 128], QbT[prt, pr, ql],
                                                start=True, stop=True)),
                                                reads=[t_ckb, t_qb], writes=[stk])
                                        ptx, ptt = ((ptA, ptA_t), (ptB, ptB_t))[hh]
                                        p.op("act", (lambda e, sbk=sbk, ptx=ptx, bsel=bsel, ncols=ncols: e.activation(
                                            out=ptx[bsel][:, 0:ncols], in_=sbk[:, 0:ncols], func=AF.Exp)),
                                            reads=[stk], writes=[ptt[bsel]])
                                    for hh in range(2):
                                        h = pr * 2 + hh
                                        ptx, ptt = ((ptA, ptA_t), (ptB, ptB_t))[hh]
                                        orow = slice(hh * 64, hh * 64 + 64)
                                        for wi in range(nw + 2):
                                            if wi < nw:
                                                vch = rowplan[wi][0] - kf0 // 128
                                                lhs = Vb[:, vch, h * 64:(h + 1) * 64]
                                                rd = [t_vb]
                                            else:
                                                lhs = CVb[:, wi - nw, h * 64:(h + 1) * 64]
                                                rd = [t_ckb]
                                            p.op("pe", (lambda e, lhs=lhs, ptx=ptx, bsel=bsel, wi=wi, orow=orow, olb=olb, nw=nw: e.matmul(
                                                olb[orow, 0:64], lhs, ptx[bsel][:, wi * 64:(wi + 1) * 64],
                                                start=(wi == 0), stop=(wi == nw + 1))),
                                                reads=rd + [ptt[bsel]], writes=[olt])
                                    for hh in range(2):
                                        ptx, ptt = ((ptA, ptA_t), (ptB, ptB_t))[hh]
                                        orow = slice(hh * 64, hh * 64 + 64)
                                        for wi in range(nw + 2):
                                            p.op("pe", (lambda e, ptx=ptx, bsel=bsel, wi=wi, orow=orow, olb=olb, nw=nw: e.matmul(
                                                olb[orow, 64:128], ones_bf[:, 0:64], ptx[bsel][:, wi * 64:(wi + 1) * 64],
                                                start=(wi == 0), stop=(wi == nw + 1))),
                                                reads=[ptt[bsel], tconst], writes=[olt])
                                    p.op("dve", (lambda e, olb=olb, bsel=bsel: e.reciprocal(out=rln[bsel][:], in_=olb[:, 64:128])),
                                         reads=[olt], writes=[rln_t[bsel]])
                                    p.op("dve", (lambda e, olb=olb, bsel=bsel, ts_=ts_, pr=pr, rr=rr: e.tensor_tensor(
                                        out=ObT[ts_][:, pr, rr * 64:(rr + 1) * 64], in0=olb[:, 0:64], in1=rln[bsel][:], op=ALU.mult)),
                                        reads=[olt, rln_t[bsel]], writes=[ob_t[ts_]])
                            for dc in range(8):
                                wb, wt = worot.next()
                                for kc in range(8):
                                    if kc < 4:
                                        rhs = OaT[:, kc, tokA:tokA + 512]
                                        rd = [t_oa[half * 4 + rt]]
                                    else:
                                        rhs = ObT[ts_][:, kc - 4, :]
                                        rd = [ob_t[ts_]]
                                    p.op("pe", (lambda e, wb=wb, kc=kc, dc=dc, rhs=rhs: e.matmul(
                                        wb[:, :], wout_bf[:, kc, dc * 128:(dc + 1) * 128], rhs, start=(kc == 0), stop=(kc == 7))),
                                        reads=rd + [t_wo], writes=[wt])
                                p.op("dve", (lambda e, wb=wb, dc=dc, ts_=ts_: e.scalar_tensor_tensor(
                                    out=xr[ts_][:, dc, :], in0=wb[:, :], scalar=mcol(0, 16 + dc), op0=ALU.mult,
                                    in1=xr[ts_][:, dc, :], op1=ALU.add)),
                                    reads=[wt, tmod], writes=[xr_t[ts_]])
                            p.dma("sp", fm(x1T)[:, :, tokA:tokA + 512], xr[ts_][:], reads=[xr_t[ts_]], lane=xo_l[ts_])
                    p.barrier()
            A.close()
            p.barrier()


            return False

        if not skip_att:
            if attention_phase():
                return finish()

        def dump_scr(name, scr):
            if name in dbg:
                with ExitStack() as S:
                    tmpf = sbuf(S, "dbgs_" + name, [128, 8, 512], F32)
                    tt = p.tok()
                    ll = p.lane("dbg" + name)
                    for ti in range(8):
                        p.dma("sp", tmpf[:], fm(scr)[:, :, ti * 512:(ti + 1) * 512], writes=[tt], lane=ll)
                        p.dma("sp", fm(dbg[name])[:, :, ti * 512:(ti + 1) * 512], tmpf[:], reads=[tt], writes=[tt], lane=ll)
                    p.barrier()

        dump_scr("x1T", x1T)
        if stop_after == "ATT":
            return finish()

        def ffn_layer(l, src, dst, final_norm, local=False):
            with ExitStack() as S:
                NS = 1024
                NW = NS + 2
                xt = sbuf(S, "f_xt", [128, 8, NW], F32)
                t_xt = p.tok("f_xt")
                l_xt = p.lane("f_xt%d" % l)
                h2 = sbuf(S, "f_h2", [128, 8, NW], BF)
                t_h2 = p.tok("f_h2")
                ntmp = make_norm_tmps(S, "f_n", NW)
                aT = sbuf(S, "f_aT", [128, NPAIR, NS], BF)
                t_aT = p.tok("f_aT")
                NWB = 3
                wbf = [sbuf(S, "f_wbf%d" % i, [128, 8, 256], BF) for i in range(NWB)]
                wbf_t = p.toks(NWB, "f_wbf")
                wbf_l = [p.lane("f_wbf%d_%d" % (l, i)) for i in range(NWB)]
                wdbf = [sbuf(S, "f_wdbf%d" % i, [128, NPAIR, 128], BF) for i in range(2)]
                wdbf_t = p.toks(2, "f_wdbf")
                wdbf_l = [p.lane("f_wdbf%d_%d" % (l, i)) for i in range(2)]
                ug = [[sbuf(S, "f_ug%d_%d" % (g_, i), [128, NW], F32) for i in range(2)] for g_ in range(2)]
                ug_t = [p.toks(2, "f_ug%d" % g_) for g_ in range(2)]
                a1 = [[sbuf(S, "f_a1%d_%d" % (g_, i), [128, NS], F32) for i in range(2)] for g_ in range(2)]
                a1_t = [p.toks(2, "f_a1%d" % g_) for g_ in range(2)]
                sg = [sbuf(S, "f_sg%d" % i, [128, NS], F32) for i in range(2)]
                sg_t = p.toks(2, "f_sg")
                l_xo = p.lane("f_xo%d" % l)
                blocks = [(0, 342), (342, 342), (684, 342)]
                ctr = [((1, 342), (0, 341)), ((0, 342), (341, 683)), ((0, 341), (683, 1024))]
                urot = Rot([(banks[0], banks[1], banks[2]), (banks[3], banks[4], banks[5])])
                drot = Rot([banks[6], banks[7]])
                pair_it = 0
                wd_it = 0
                NTOK = 2048 if local else L
                for st_i in range(NTOK // NS):
                    tok0 = st_i * NS
                    lo = tok0 - 1
                    hi = tok0 + NS + 1
                    clo = max(lo, 0)
                    chi = min(hi, L)
                    if local:
                        p.dma("sp", xt[:, :, 0:NW], fm(src)[:, :, tok0:tok0 + NW], writes=[t_xt], lane=l_xt)
                    else:
                        if lo < 0:
                            p.op("pool", lambda e: e.memset(xt[:, :, 0:1], 1.0), writes=[t_xt])
                        if hi > L:
                            p.op("pool", lambda e: e.memset(xt[:, :, NW - 1:NW], 1.0), writes=[t_xt])
                        p.dma("sp", xt[:, :, clo - lo:chi - lo], fm(src)[:, :, clo:chi], writes=[t_xt], lane=l_xt)
                    norm_mod(xt, 0, NW, blocks, lambda c: gs[:, l, 1, c:c + 1], lambda c: mcol(l, 24 + c),
                             h2, 0, t_xt, t_h2, [banks[0], banks[1], banks[2]], ntmp)
                    if local:
                        if st_i == 0:
                            p.op("dve", lambda e: e.tensor_scalar(out=h2[:, :, 0:1], in0=h2[:, :, 0:1], scalar1=hm_sb[:, 0:1],
                                                                  scalar2=None, op0=ALU.mult),
                                 reads=[tconst], writes=[t_h2])
                        if st_i == NTOK // NS - 1:
                            p.op("dve", lambda e: e.tensor_scalar(out=h2[:, :, NW - 1:NW], in0=h2[:, :, NW - 1:NW],
                                                                  scalar1=hm_sb[:, 1:2], scalar2=None, op0=ALU.mult),
                                 reads=[tconst], writes=[t_h2])
                    else:
                        if lo < 0:
                            p.op("pool", lambda e: e.memset(h2[:, :, 0:1], 0.0), writes=[t_h2])
                        if hi > L:
                            p.op("pool", lambda e: e.memset(h2[:, :, NW - 1:NW], 0.0), writes=[t_h2])
                    for j in range(NPAIR):
                        s = pair_it % NWB
                        bsel = pair_it % 2
                        pair_it += 1
                        p.dma("sp", wbf[s][:].rearrange("p k n -> p (k n)"), wupb[l, j], writes=[wbf_t[s]], lane=wbf_l[s])
                        for gv in range(2):
                            ub = urot.next()
                            fch = j if gv == 0 else NPAIR + j
                            ugx, ugt = ug[gv][bsel], ug_t[gv][bsel]
                            a1x, a1t = a1[gv][bsel], a1_t[gv][bsel]
                            for bi, (c0, w) in enumerate(blocks):
                                bk, bt = ub[bi]
                                for k in range(8):
                                    p.op("pe", (lambda e, bk=bk, k=k, c0=c0, w=w, s=s, gv=gv: e.matmul(
                                        bk[:, 0:w], wbf[s][:, k, gv * 128:(gv + 1) * 128], h2[:, k, c0:c0 + w],
                                        start=(k == 0), stop=(k == 7))),
                                        reads=[wbf_t[s], t_h2], writes=[bt])
                                p.op("act", (lambda e, bk=bk, c0=c0, w=w, ugx=ugx: e.activation(
                                    out=ugx[:, c0:c0 + w], in_=bk[:, 0:w], func=AF.Copy)),
                                    reads=[bt], writes=[ugt])
                                (b0, b1), (d0, d1) = ctr[bi]
                                p.op("act", (lambda e, bk=bk, b0=b0, b1=b1, d0=d0, d1=d1, a1x=a1x, fch=fch: e.activation(
                                    out=a1x[:, d0:d1], in_=bk[:, b0:b1], func=AF.Identity,
                                    scale=cw_sb[:, l, fch, 1:2], bias=cw_sb[:, l, fch, 3:4])),
                                    reads=[bt, tconst], writes=[a1t])
                            p.op("dve", (lambda e, fch=fch, ugx=ugx, a1x=a1x: e.scalar_tensor_tensor(
                                out=a1x[:], in0=ugx[:, 0:NS], scalar=cw_sb[:, l, fch, 0:1], op0=ALU.mult,
                                in1=a1x[:], op1=ALU.add)),
                                reads=[ugt, tconst], writes=[a1t])
                            p.op("dve", (lambda e, fch=fch, ugx=ugx, a1x=a1x: e.scalar_tensor_tensor(
                                out=a1x[:], in0=ugx[:, 2:NS + 2], scalar=cw_sb[:, l, fch, 2:3], op0=ALU.mult,
                                in1=a1x[:], op1=ALU.add)),
                                reads=[ugt, tconst], writes=[a1t])
                        p.op("act", (lambda e, bsel=bsel: e.activation(out=sg[bsel][:], in_=a1[0][bsel][:], func=AF.Silu)),
                             reads=[a1_t[0][bsel]], writes=[sg_t[bsel]])
                        p.op("dve", (lambda e, j=j, bsel=bsel: e.tensor_tensor(out=aT[:, j, :], in0=sg[bsel][:], in1=a1[1][bsel][:],
                                                                              op=ALU.mult)),
                             reads=[sg_t[bsel], a1_t[1][bsel]], writes=[t_aT])
                    for dc in range(8):
                        ws_ = wd_it % 2
                        wd_it += 1
                        p.dma("sp", wdbf[ws_][:].rearrange("p j n -> p (j n)"), wdb[l, dc], writes=[wdbf_t[ws_]], lane=wdbf_l[ws_])
                        for hb in range(2):
                            db, dt_ = drot.next()
                            for j in range(NPAIR):
                                p.op("pe", (lambda e, db=db, j=j, ws_=ws_, hb=hb: e.matmul(
                                    db[:, :], wdbf[ws_][:, j, :], aT[:, j, hb * 512:(hb + 1) * 512],
                                    start=(j == 0), stop=(j == NPAIR - 1))),
                                    reads=[wdbf_t[ws_], t_aT], writes=[dt_])
                            p.op("dve", (lambda e, db=db, dc=dc, hb=hb: e.scalar_tensor_tensor(
                                out=xt[:, dc, 1 + hb * 512:1 + (hb + 1) * 512], in0=db[:, :], scalar=mcol(l, 40 + dc), op0=ALU.mult,
                                in1=xt[:, dc, 1 + hb * 512:1 + (hb + 1) * 512], op1=ALU.add)),
                                reads=[dt_, tmod], writes=[t_xt])
                    if not final_norm:
                        p.dma("sp", fm(dst)[:, :, tok0:tok0 + NS], xt[:, :, 1:NS + 1], reads=[t_xt], lane=l_xo)
                    else:
                        norm_mod(xt, 1, NS, [(0, 512), (512, 512)], lambda c: ng_sb[:, 4, c:c + 1], None,
                                 xt, 1, t_xt, t_xt, [banks[0], banks[1]], ntmp)
                        p.dma("sp", fm(dst)[:, :, tok0:tok0 + NS], xt[:, :, 1:NS + 1], reads=[t_xt], lane=lane_out)
            p.barrier()

        if skip_att and ffn0_src != 'skip':
            with ExitStack() as S:
                pc32 = sbuf(S, "tpc32", [128, NPAIR * 128], F32)
                pc16 = sbuf(S, "tpc16", [128, NPAIR * 128], BF)
                tp32, tp16 = p.toks(2, "tpc")
                lp = p.lane("tpc")
                jobs = [(w_up[0, j_], wupb[0, j_], 2048) for j_ in range(NPAIR)] + [(w_down[0, d_], wdb[0, d_], NPAIR * 128) for d_ in range(8)]
                for (srcw, dstw, nel) in jobs:
                    p.dma("sp", pc32[:, 0:nel], srcw, writes=[tp32], lane=lp)
                    p.op("pool", (lambda e, nel=nel: e.tensor_copy(out=pc16[:, 0:nel], in_=pc32[:, 0:nel])), reads=[tp32], writes=[tp16])
                    p.dma("sp", dstw, pc16[:, 0:nel], reads=[tp16], writes=[tp16], lane=lp)
            p.barrier()
        if ffn0_src != 'skip':
            ffn_layer(0, xT if ffn0_src == 'xT' else x1T, x2T, False)
        dump_scr("x2T", x2T)
        if stop_after == "FFN0":
            return finish()

        fsrc = xT if four_src == 'xT' else x2T
        with ExitStack() as S:
            Htok = sbuf(S, "Htok", [128, 32, D], BF)
            t_H = p.tok("Htok")
            wcs = sbuf(S, "wcs", [128, 2, 8, D], BF)
            t_wcs = p.tok("wcs")
            with ExitStack() as S2:
                fw_bf = sbuf(S2, "fw_bf", [128, 8, D], BF)
                t_fw = p.tok("fw")
                load_cols(fw_bf, t_fw, fw, [(0, D)], "fw")
                cs_sb = sbuf(S2, "cs_sb", [128, 2, 2, 256], BF)
                p.dma("sp", cs_sb[:, 0], c256.rearrange("(c p) n -> p c n", p=128), writes=[t_fw], lane=lane_r)
                p.dma("sp", cs_sb[:, 1], s256n.rearrange("(c p) n -> p c n", p=128), writes=[t_fw], lane=lane_r)
                wrot = Rot([banks[4], banks[5]])
                for cs in range(2):
                    for g in range(4):
                        for jc in range(2):
                            for nb in range(2):
                                bk, bt = wrot.next()
                                for kc in range(2):
                                    p.op("pe", (lambda e, bk=bk, cs=cs, g=g, jc=jc, nb=nb, kc=kc: e.matmul(
                                        bk[:, :], cs_sb[:, cs, kc, jc * 128:(jc + 1) * 128], fw_bf[:, g * 2 + kc, nb * 512:(nb + 1) * 512],
                                        start=(kc == 0), stop=(kc == 1))),
                                        reads=[t_fw], writes=[bt])
                                p.op("act", (lambda e, bk=bk, cs=cs, g=g, jc=jc, nb=nb: e.activation(
                                    out=wcs[:, cs, g * 2 + jc, nb * 512:(nb + 1) * 512], in_=bk[:, :], func=AF.Copy)),
                                    reads=[bt], writes=[t_wcs])
            p.barrier()
            with ExitStack() as S2:
                xt = [sbuf(S2, "l1_xt%d" % i, [128, 8, 512], F32) for i in range(2)]
                xt_t = p.toks(2, "l1_xt")
                xt_l = [p.lane("l1_xt%d" % i) for i in range(2)]
                ht = [sbuf(S2, "l1_ht%d" % i, [128, 8, 512], BF) for i in range(2)]
                ht_t = p.toks(2, "l1_ht")
                ntmp = make_norm_tmps(S2, "l1_n", 512)
                trot = Rot([banks[6], banks[7]])
                for ti in range(8):
                    s = ti % 2
                    tok0 = ti * 512
                    p.dma("sp", xt[s][:], fm(fsrc)[:, :, tok0:tok0 + 512], writes=[xt_t[s]], lane=xt_l[s])
                    norm_mod(xt[s], 0, 512, [(0, 512)], lambda c: gs[:, 1, 0, c:c + 1], lambda c: mcol(1, c),
                             ht[s], 0, xt_t[s], ht_t[s], [banks[0]], ntmp)
                    for sub in range(4):
                        tch = ti * 4 + sub
                        bk, bt = trot.next()
                        bkb = bk.bitcast(BF)
                        for c in range(8):
                            p.op("pe", (lambda e, bkb=bkb, c=c, s=s, sub=sub: e.transpose(
                                bkb[:, c * 128:(c + 1) * 128], ht[s][:, c, sub * 128:(sub + 1) * 128], ident_bf[:])),
                                reads=[ht_t[s], tconst], writes=[bt])
                        p.op("dve", (lambda e, bkb=bkb, tch=tch: e.tensor_copy(out=Htok[:, tch, :], in_=bkb[:, 0:1024])),
                             reads=[bt], writes=[t_H])
            p.barrier()
            KB = 256
            dct = [sbuf(S, "dct%d" % i, [128, 32, KB], BF) for i in range(2)]
            dst_ = [sbuf(S, "dst%d" % i, [128, 32, KB], BF) for i in range(2)]
            d_t = p.toks(2, "dft")
            d_l = [p.lane("dft%d" % i) for i in range(2)]
            uT = [sbuf(S, "uT%d" % i, [128, 8, 2, KB], BF) for i in range(2)]
            u_t = p.toks(2, "uT")
            xr = [sbuf(S, "l1_xr%d" % i, [128, 8, KB], F32) for i in range(2)]
            xr_t = p.toks(2, "l1_xr")
            xr_l = [p.lane("l1_xr_%d" % i) for i in range(2)]
            xo_l = [p.lane("l1_xo_%d" % i) for i in range(2)]
            urot = Rot([banks[0], banks[1], banks[2]])
            prot = Rot([banks[3], banks[4]])
            kblocks = [(i * KB, KB) for i in range(2048 // KB)] + [(2048, 2)]
            for kb, (k0, w) in enumerate(kblocks):
                s = kb % 2
                p.dma("sp", dct[s][:, :, 0:w], dftc.rearrange("(c p) k -> p c k", p=128)[:, :, k0:k0 + w], writes=[d_t[s]], lane=d_l[s])
                p.dma("sp", dst_[s][:, :, 0:w], dfts.rearrange("(c p) k -> p c k", p=128)[:, :, k0:k0 + w], writes=[d_t[s]], lane=d_l[s])
                if w == KB:
                    p.op("sp", (lambda e, s=s, k0=k0: e.dma_start(
                        out=xr[s][:], in_=fm(fsrc)[:, :, bass.ds(core_par(e) * 2048 + k0, KB)])),
                        writes=[xr_t[s]], lane=xr_l[s])
                else:
                    p.op("sp", (lambda e, s=s: e.dma_start(
                        out=xr[s][:, :, 0:1], in_=fm(fsrc)[:, :, bass.ds(core_par(e) * 2047, 1)], allow_slow_non_contiguous=True)),
                        writes=[xr_t[s]], lane=xr_l[s])
                    p.op("sp", (lambda e, s=s: e.dma_start(
                        out=xr[s][:, :, 1:2], in_=fm(fsrc)[:, :, bass.ds(core_par(e) * 2047 + 2048, 1)], allow_slow_non_contiguous=True)),
                        writes=[xr_t[s]], lane=xr_l[s])
                for jc in range(8):
                    bk, bt = urot.next()
                    for cs, tab in enumerate((dct, dst_)):
                        for tc_ in range(32):
                            p.op("pe", (lambda e, bk=bk, cs=cs, tab=tab, tc_=tc_, jc=jc, s=s, w=w: e.matmul(
                                bk[:, cs * KB:cs * KB + w], Htok[:, tc_, jc * 128:(jc + 1) * 128], tab[s][:, tc_, 0:w],
                                start=(tc_ == 0), stop=(tc_ == 31))),
                                reads=[t_H, d_t[s]], writes=[bt])
                    p.op("act", (lambda e, bk=bk, jc=jc, s=s, w=w: e.activation(
                        out=uT[s][:, jc, :, 0:w], in_=bk[:, 0:2 * KB].rearrange("p (c k) -> p c k", c=2)[:, :, 0:w], func=AF.Copy)),
                        reads=[bt], writes=[u_t[s]])
                for nch in range(8):
                    bk, bt = prot.next()
                    for cs in range(2):
                        for jc in range(8):
                            p.op("pe", (lambda e, bk=bk, cs=cs, jc=jc, nch=nch, s=s, w=w: e.matmul(
                                bk[:, 0:w], wcs[:, cs, jc, nch * 128:(nch + 1) * 128], uT[s][:, jc, cs, 0:w],
                                start=(cs == 0 and jc == 0), stop=(cs == 1 and jc == 7))),
                                reads=[t_wcs, u_t[s]], writes=[bt])
                    p.op("dve", (lambda e, bk=bk, nch=nch, s=s, w=w: e.scalar_tensor_tensor(
                        out=xr[s][:, nch, 0:w], in0=bk[:, 0:w], scalar=mcol(1, 16 + nch), op0=ALU.mult,
                        in1=xr[s][:, nch, 0:w], op1=ALU.add)),
                        reads=[bt, tmod], writes=[xr_t[s]])
                if w == KB:
                    p.dma("sp", fm(x3T)[:, :, 1 + k0:1 + k0 + KB], xr[s][:], reads=[xr_t[s]], lane=xo_l[s])
                else:
                    p.op("sp", (lambda e, s=s: e.dma_start(out=fm(x3T)[:, :, 0:1], in_=xr[s][:, :, 0:1], allow_slow_non_contiguous=True)),
                         reads=[xr_t[s]], lane=xo_l[s])
                    p.op("sp", (lambda e, s=s: e.dma_start(out=fm(x3T)[:, :, 2049:2050], in_=xr[s][:, :, 1:2], allow_slow_non_contiguous=True)),
                         reads=[xr_t[s]], lane=xo_l[s])
        p.barrier()
        if stop_after == "FOUR":
            return finish()

        ffn_layer(1, x3T, yT, True, local=True)
        return finish()

def prep_core_inputs(b, x, c, ctx, c_ctx, mod_w, mod_b, norm1_g, norm2_g, attn_w_in, attn_w_out, q_norm_g,
                     k_norm_g, na_rpb, fourier_w_out, ffn_w_up, ffn_conv_w, ffn_conv_b, ffn_w_down, final_g, shared):
    f32 = np.float32
    m = dict(shared)
    m["xT"] = np.ascontiguousarray(x[b].T.astype(f32))
    m["ctxT"] = np.ascontiguousarray(ctx[b].T.astype(f32))
    ccv = np.stack([c[b], c_ctx], axis=-1).astype(f32)
    m["cc"] = np.ascontiguousarray(ccv.reshape(8, 128, 2).transpose(1, 0, 2))
    return m


_DFT_CORE = {}


def dft_core_tables(s):
    if s not in _DFT_CORE:
        cst = consts()
        base = s * 2048
        cols = np.concatenate([np.arange(base, base + 2048), [max(base - 1, 0)], [min(base + 2048, L - 1)]])
        hm = np.zeros((128, 2), np.float32)
        hm[:, 0] = 0.0 if s == 0 else 1.0
        hm[:, 1] = 1.0 if s == 0 else 0.0
        _DFT_CORE[s] = dict(dftc=np.ascontiguousarray(cst["dftc_full"][:, cols]),
                            dfts=np.ascontiguousarray(cst["dfts_full"][:, cols]), hmask=hm)
    return _DFT_CORE[s]


def prep_shared(mod_w, mod_b, norm1_g, norm2_g, attn_w_in, attn_w_out, q_norm_g, k_norm_g, na_rpb,
                fourier_w_out, ffn_w_up, ffn_conv_w, ffn_conv_b, ffn_w_down, final_g):
    f32 = np.float32
    sh = {}
    sh["mod_w"] = np.ascontiguousarray(mod_w.astype(f32))
    sh["mod_b2"] = np.ascontiguousarray(np.repeat(mod_b.astype(f32)[:, None, :], 2, axis=1))
    ngs = np.stack([norm1_g[0], norm2_g[0], norm1_g[1], norm2_g[1], final_g], axis=0).astype(f32)
    sh["ng"] = np.ascontiguousarray(ngs.reshape(5, 8, 128).transpose(2, 0, 1))
    sh["w_in"] = np.ascontiguousarray(attn_w_in[0].astype(f32))
    sh["w_out"] = np.ascontiguousarray(attn_w_out[0].astype(f32))
    sh["qkg"] = np.ascontiguousarray(np.stack([q_norm_g[0], k_norm_g[0]], axis=-1).astype(f32))
    bt, _, _ = build_bias_tiles(na_rpb[0].astype(f32))
    sh["btiles"] = bt
    sh["fw"] = np.ascontiguousarray(fourier_w_out[0].astype(f32))
    wu = ffn_w_up.astype(f32).reshape(2, 8, 128, 2, NPAIR, 128)
    sh["w_up"] = np.ascontiguousarray(wu.transpose(0, 4, 2, 1, 3, 5).reshape(2, NPAIR, 128, 8 * 256))
    cwb = np.concatenate([ffn_conv_w.astype(f32), ffn_conv_b.astype(f32)[:, None, :]], axis=1)
    sh["cw"] = np.ascontiguousarray(cwb.reshape(2, 4, 44, 128).transpose(3, 0, 2, 1))
    wd = ffn_w_down.astype(f32).reshape(2, NPAIR, 128, 8, 128)
    sh["w_down"] = np.ascontiguousarray(wd.transpose(0, 3, 2, 1, 4).reshape(2, 8, 128, NPAIR * 128))
    sh.update({k: v for k, v in consts().items() if not k.endswith('_full')})
    return sh


def kernel(x, c, ctx, c_ctx, mod_w, mod_b, norm1_g, norm2_g, attn_w_in, attn_w_out, q_norm_g,
           k_norm_g, na_rpb, fourier_w_out, ffn_w_up, ffn_conv_w, ffn_conv_b, ffn_w_down, final_g):
    args = [np.asarray(a) for a in (x, c, ctx, c_ctx, mod_w, mod_b, norm1_g, norm2_g, attn_w_in, attn_w_out,
                                    q_norm_g, k_norm_g, na_rpb, fourier_w_out, ffn_w_up, ffn_conv_w,
                                    ffn_conv_b, ffn_w_down, final_g)]
    shared = prep_shared(*args[4:])
    in_maps = [prep_core_inputs(core // 2, *args, shared) for core in range(8)]
    for core in range(8):
        in_maps[core].update(dft_core_tables(core % 2))
    nc = build()
    res = run_bass_kernel_spmd(nc, in_maps, core_ids=list(range(8)))
    out = np.empty((4, L, D), np.float32)
    for b in range(4):
        y0 = res.results[2 * b]["yT"]
        y1 = res.results[2 * b + 1]["yT"]
        out[b, :2048] = y0.T
        out[b, 2048:] = y1.T
    return out
```

```python
import numpy as np
import ml_dtypes
from contextlib import ExitStack
import concourse.bass as bass
import concourse.mybir as mybir
from concourse.bass_utils import run_bass_kernel_spmd

F32 = mybir.dt.float32
BF = mybir.dt.bfloat16
AF = mybir.ActivationFunctionType
ALU = mybir.AluOpType

D = 1024
L = 4096
CTX = 256
DFF = 2816
NPAIR = DFF // 128
GW = 64
EPS = 1e-6
NEG = -30000.0
SCALE_A = 128 ** -0.5
SCALE_B = 64 ** -0.5


class Tok:
    __slots__ = ("w", "r", "name")

    def __init__(self, name=""):
        self.w = None
        self.r = []
        self.name = name


class Lane:
    def __init__(self, sem):
        self.sem = sem
        self.count = 0
        self.last = None


class Prog:
    ENGS = ("pe", "act", "dve", "pool", "sp")

    def __init__(self, nc, stack):
        self.nc = nc
        self.stack = stack
        self.ops = []
        self.lanes = []
        self.esem = {e: stack.enter_context(nc.semaphore("es_" + e)) for e in self.ENGS}
        self.last_on = {e: None for e in self.ENGS}

    def tok(self, name=""):
        return Tok(name)

    def toks(self, n, name=""):
        return [Tok(name + str(i)) for i in range(n)]

    def lane(self, name):
        ln = Lane(self.stack.enter_context(self.nc.semaphore("ln%d_%s" % (len(self.lanes), name))))
        self.lanes.append(ln)
        return ln

    def op(self, eng, fn, reads=(), writes=(), lane=None, extra_deps=()):
        i = len(self.ops)
        deps = set(extra_deps)
        for t in reads:
            if t.w is not None:
                deps.add(t.w)
        for t in writes:
            if t.w is not None:
                deps.add(t.w)
            last = {}
            for j in t.r:
                oj = self.ops[j]
                if oj["lane"] is not None:
                    deps.add(j)
                else:
                    last[oj["eng"]] = max(last.get(oj["eng"], -1), j)
            deps.update(last.values())
        for t in reads:
            t.r.append(i)
        for t in writes:
            t.w = i
            t.r = []
        deps.discard(i)
        self.ops.append(dict(eng=eng, fn=fn, deps=deps, lane=lane, signal=False, sigval=None))
        self.last_on[eng] = i
        if lane is not None:
            lane.last = i
        return i

    def dma(self, q, out, in_, reads=(), writes=(), lane=None):
        assert lane is not None
        return self.op(q, lambda e: e.dma_start(out=out, in_=in_), reads, writes, lane=lane)

    def barrier(self):
        deps = set(v for v in self.last_on.values() if v is not None)
        deps.update(ln.last for ln in self.lanes if ln.last is not None)
        for e in self.ENGS:
            self.op(e, lambda eng: eng.nop(), extra_deps=deps)

    def emit(self, final_lanes=()):
        nc = self.nc
        ops = self.ops
        for i, o in enumerate(ops):
            for j in o["deps"]:
                d = ops[j]
                if d["lane"] is not None:
                    continue
                if d["eng"] == "pe" and o["eng"] == "pe" and o["lane"] is None:
                    continue
                d["signal"] = True
        cnt = {e: 0 for e in self.ENGS}
        for o in ops:
            if o["lane"] is not None:
                o["lane"].count += 16
                o["sigval"] = o["lane"].count
            elif o["signal"]:
                cnt[o["eng"]] += 1
                o["sigval"] = cnt[o["eng"]]
        per_eng = {e: [] for e in self.ENGS}
        for i, o in enumerate(ops):
            per_eng[o["eng"]].append(i)

        def run(ename, eng):
            waited = {}
            for i in per_eng[ename]:
                o = ops[i]
                need = {}
                for j in o["deps"]:
                    d = ops[j]
                    if d["lane"] is not None:
                        sem = d["lane"].sem
                    else:
                        if d["eng"] == "pe" and ename == "pe" and o["lane"] is None:
                            continue
                        sem = self.esem[d["eng"]]
                    k = id(sem)
                    if need.get(k, (None, 0))[1] < d["sigval"]:
                        need[k] = (sem, d["sigval"])
                for k, (sem, v) in need.items():
                    if waited.get(k, 0) < v:
                        eng.wait_ge(sem, v)
                        waited[k] = v
                ins = o["fn"](eng)
                if o["lane"] is not None:
                    ins.then_inc(o["lane"].sem, 16)
                elif o["signal"]:
                    ins.then_inc(self.esem[ename], 1)
            if ename == "sp":
                for ln in final_lanes:
                    if ln.count:
                        eng.wait_ge(ln.sem, ln.count)

        with nc.Block() as block:
            @block.tensor
            def _(e):
                run("pe", e)

            @block.scalar
            def _(e):
                run("act", e)

            @block.vector
            def _(e):
                run("dve", e)

            @block.gpsimd
            def _(e):
                run("pool", e)

            @block.sync
            def _(e):
                run("sp", e)


class Rot:
    def __init__(self, items):
        self.items = items
        self.i = 0

    def next(self):
        it = self.items[self.i % len(self.items)]
        self.i += 1
        return it


STRIPS = (("RE", (-4, -2, 0, 2)), ("RO", (-5, -3, -1, 1, 3)), ("UE", tuple(range(-6, 7, 2))), ("UO", tuple(range(-7, 6, 2))))
NBT = sum(len(v) for _, v in STRIPS)


def na_row_plan():
    off = {}
    o = 0
    for name, v in STRIPS:
        off[name] = o
        o += len(v)
    plan = []
    for r in range(64):
        r0 = min(max(r - 4, 0), 56)
        m_lo = r0 // 2
        m_hi = (r0 + 7) // 2
        nw = m_hi - m_lo + 1
        dr0 = 2 * m_lo - r
        if 4 <= r <= 60:
            if r % 2 == 0:
                assert dr0 == -4 and nw == 4
                k = off["RE"]
            else:
                assert dr0 == -5 and nw == 5
                k = off["RO"]
        else:
            assert nw == 4
            if dr0 % 2 == 0:
                k = off["UE"] + (dr0 + 6) // 2
            else:
                k = off["UO"] + (dr0 + 7) // 2
        plan.append((m_lo, nw, k))
    return plan


def build_bias_tiles(rpb):
    out = np.full((8, NBT, 128, 64), NEG, np.float32)
    c = np.arange(64)
    c0 = np.clip(c - 8, 0, 48)
    kc = np.arange(64)
    colok = (kc[:, None] >= c0[None, :]) & (kc[:, None] < c0[None, :] + 16)
    dcc = np.clip(kc[:, None] - c[None, :] + 15, 0, 30)
    k = 0
    for name, drs in STRIPS:
        for dr0 in drs:
            for e in (0, 1):
                dr = dr0 + e
                ok = (-4 <= dr <= 3) if name[0] == "R" else (-7 <= dr <= 7)
                if ok:
                    vals = rpb[:, dr + 7, :][:, dcc]
                    out[:, k, e * 64:(e + 1) * 64, :] = np.where(colok[None], vals, np.float32(NEG))
            k += 1
    out = out.reshape(8 * NBT, 128, 64).transpose(1, 0, 2)
    return np.ascontiguousarray(out)


def rope_tables():
    t = np.arange(L)
    row = (t // GW).astype(np.float32)
    col = (t % GW).astype(np.float32)
    freqs = (10000.0 ** (-np.arange(32, dtype=np.float32) / 32)).astype(np.float32)
    ang = np.concatenate([row[:, None] * freqs, col[:, None] * freqs], axis=-1)
    cos = np.cos(ang).astype(np.float32)
    sin = np.sin(ang).astype(np.float32)
    COS = np.repeat(cos, 2, axis=1).T
    SINS = np.repeat(sin, 2, axis=1).T.copy()
    SINS[0::2] *= -1.0
    return (np.ascontiguousarray(COS).astype(ml_dtypes.bfloat16),
            np.ascontiguousarray(SINS).astype(ml_dtypes.bfloat16))


def dft_tables():
    t = np.arange(L, dtype=np.int64)
    ph = (np.outer(t, t) % L).astype(np.float64) * (2 * np.pi / L)
    CL = (np.cos(ph) / 64.0).astype(ml_dtypes.bfloat16)
    SL = (np.sin(ph) / 64.0).astype(ml_dtypes.bfloat16)
    j = np.arange(256, dtype=np.int64)
    ph2 = (np.outer(j, j) % 256).astype(np.float64) * (2 * np.pi / 256)
    C2 = (np.cos(ph2) / 16.0).astype(ml_dtypes.bfloat16)
    S2n = (-np.sin(ph2) / 16.0).astype(ml_dtypes.bfloat16)
    return CL, SL, C2, S2n


_CONST = {}


def consts():
    if not _CONST:
        COS, SINS = rope_tables()
        CL, SL, C2, S2n = dft_tables()
        ident = np.eye(128, dtype=np.float32)
        rm = np.zeros((128, 128), np.float32)
        for d in range(128):
            rm[d ^ 1, d] = 1.0
        _CONST.update(
            rope_cos=COS, rope_sin=SINS, dftc_full=CL, dfts_full=SL, c256=C2, s256n=S2n,
            ident_bf=ident.astype(ml_dtypes.bfloat16),
            ones_bf=np.ones((128, 128), np.float32).astype(ml_dtypes.bfloat16),
            rm_bf=rm.astype(ml_dtypes.bfloat16),
            ident32=ident.copy(),
        )
    return _CONST


def build(debug=(), stop_after=None, skip_att=False, ffn0_src=None, four_src=None):
    plan = na_row_plan()
    NTB = NBT
    nc = bass.Bass("TRN2", target_bir_lowering=False)

    def din(name, shape, dt=F32):
        return nc.dram_tensor(name, list(shape), dt, kind="ExternalInput").ap()

    xT = din("xT", [D, L])
    ctxT = din("ctxT", [D, CTX])
    cc = din("cc", [128, 8, 2])
    mod_w = din("mod_w", [2, D, 6 * D])
    mod_b2 = din("mod_b2", [2, 2, 6 * D])
    ng = din("ng", [128, 5, 8])
    w_in = din("w_in", [D, 2560])
    w_out = din("w_out", [D, D])
    qkg = din("qkg", [128, 2])
    btiles = din("btiles", [128, NTB * 8, 64])
    rope_cos = din("rope_cos", [128, L], BF)
    rope_sin = din("rope_sin", [128, L], BF)
    fw = din("fw", [D, D])
    w_up = din("w_up", [2, NPAIR, 128, 8 * 256])
    cw = din("cw", [128, 2, 44, 4])
    w_down = din("w_down", [2, 8, 128, NPAIR * 128])
    NLOC = 2048
    dftc = din("dftc", [L, NLOC + 2], BF)
    dfts = din("dfts", [L, NLOC + 2], BF)
    hmask = din("hmask", [128, 2])
    c256 = din("c256", [256, 256], BF)
    s256n = din("s256n", [256, 256], BF)
    ident_bf_d = din("ident_bf", [128, 128], BF)
    ones_bf_d = din("ones_bf", [128, 128], BF)
    rm_bf_d = din("rm_bf", [128, 128], BF)
    ident32_d = din("ident32", [128, 128])

    yT = nc.dram_tensor("yT", [D, 2048], F32, kind="ExternalOutput").ap()
    dbg = {}
    for name, shape in debug:
        dbg[name] = nc.dram_tensor("dbg_" + name, list(shape), F32, kind="ExternalOutput").ap()

    x1T = nc.dram_tensor("x1T_scr", [D, L], F32, kind="Internal").ap()
    x2T = nc.dram_tensor("x2T_scr", [D, L], F32, kind="Internal").ap()
    x3T = nc.dram_tensor("x3T_scr", [D, 2048 + 2], F32, kind="Internal").ap()
    wupb = nc.dram_tensor("wupb_scr", [2, NPAIR, 128, 8 * 256], BF, kind="Internal").ap()
    wdb = nc.dram_tensor("wdb_scr", [2, 8, 128, NPAIR * 128], BF, kind="Internal").ap()

    def fm(ap):
        return ap.rearrange("(c p) t -> p c t", p=128)

    with ExitStack() as G:
        p = Prog(nc, G)
        lane_out = p.lane("out")
        lane_c = p.lane("const")
        lane_scr = p.lane("scr")
        lane_r = p.lane("rope")

        ucnt = [0]

        def sbuf(st, name, shape, dt):
            ucnt[0] += 1
            return st.enter_context(nc.sbuf_tensor("s%d_%s" % (ucnt[0], name), list(shape), dt))

        banks = []
        for i in range(8):
            t = G.enter_context(nc.psum_tensor("psb%d" % i, [128, 512], F32))
            banks.append((t, p.tok("psb%d" % i)))

        ident_bf = sbuf(G, "ident_bf", [128, 128], BF)
        ones_bf = sbuf(G, "ones_bf", [128, 128], BF)
        rm_bf = sbuf(G, "rm_bf", [128, 128], BF)
        ident32 = sbuf(G, "ident32", [128, 128], F32)
        ng_sb = sbuf(G, "ng_sb", [128, 5, 8], F32)
        qkg_sb = sbuf(G, "qkg_sb", [128, 2], F32)
        cw_sb = sbuf(G, "cw_sb", [128, 2, 44, 4], F32)
        cc_sb = sbuf(G, "cc_sb", [128, 8, 2], F32)
        hm_sb = sbuf(G, "hm_sb", [128, 2], F32)
        sc_sb = sbuf(G, "sc_sb", [128, 8, 2], F32)
        modT = sbuf(G, "modT", [128, 2, 48, 2], F32)
        gs = sbuf(G, "gs", [128, 2, 3, 8], F32)
        tconst = p.tok("const")
        for dst, src in ((ident_bf, ident_bf_d), (ones_bf, ones_bf_d), (rm_bf, rm_bf_d), (ident32, ident32_d),
                         (ng_sb, ng), (qkg_sb, qkg), (cw_sb, cw), (cc_sb, cc), (hm_sb, hmask)):
            p.dma("sp", dst[:], src, writes=[tconst], lane=lane_c)

        tmod = p.tok("modT")
        with ExitStack() as S:
            mrow = sbuf(S, "mrow", [2, 6 * D], F32)
            mb_sb = sbuf(S, "mb_sb", [2, 6 * D], F32)
            stg = [sbuf(S, "mw_stg%d" % i, [128, 8, 512], F32) for i in range(2)]
            stg_t = p.toks(2, "mwstg")
            stg_l = [p.lane("mw%d" % i) for i in range(2)]
            tmrow = p.tok("mrow")
            tmb = p.tok("mb")
            lane_mb = p.lane("mb")
            p.op("act", lambda e: e.activation(out=sc_sb[:], in_=cc_sb[:], func=AF.Silu), reads=[tconst], writes=[tconst])
            it = 0
            for l in range(2):
                p.dma("sp", mb_sb[:], mod_b2[l], writes=[tmb], lane=lane_mb)
                for blk in range(12):
                    s = it % 2
                    it += 1
                    src = mod_w[l].rearrange("(c p) n -> p c n", p=128)[:, :, blk * 512:(blk + 1) * 512]
                    p.dma("sp", stg[s][:], src, writes=[stg_t[s]], lane=stg_l[s])
                    bk, bt = banks[s]
                    for k in range(8):
                        p.op("pe", (lambda e, s=s, k=k, bk=bk: e.matmul(bk[0:2, :], sc_sb[:, k, :], stg[s][:, k, :],
                                                                         start=(k == 0), stop=(k == 7))),
                             reads=[stg_t[s], tconst], writes=[bt])
                    p.op("dve", (lambda e, bk=bk, blk=blk: e.tensor_tensor(
                        out=mrow[:, blk * 512:(blk + 1) * 512], in0=bk[0:2, :],
                        in1=mb_sb[:, blk * 512:(blk + 1) * 512], op=ALU.add)),
                        reads=[bt, tmb], writes=[tmrow])
                bk, bt = banks[2]
                for ch in range(48):
                    p.op("pe", (lambda e, ch=ch, bk=bk: e.matmul(bk[:, ch * 2:ch * 2 + 2], mrow[0:2, ch * 128:(ch + 1) * 128],
                                                                ident32[0:2, 0:2], start=True, stop=True)),
                         reads=[tmrow, tconst], writes=[bt])
                p.op("dve", (lambda e, l=l, bk=bk: e.tensor_copy(out=modT[:, l].rearrange("p a b -> p (a b)"), in_=bk[:, 0:96])),
                     reads=[bt], writes=[tmod])
            for l in range(2):
                for (i, chunk0, j, ngi) in ((0, 8, 0, 2 * l), (1, 32, 0, 2 * l + 1), (2, 8, 1, 2 * l)):
                    p.op("dve", (lambda e, l=l, i=i, chunk0=chunk0, j=j, ngi=ngi: e.scalar_tensor_tensor(
                        out=gs[:, l, i, :], in0=modT[:, l, chunk0:chunk0 + 8, j], scalar=1.0, op0=ALU.add,
                        in1=ng_sb[:, ngi, :], op1=ALU.mult)),
                        reads=[tmod, tconst], writes=[tmod])
        p.barrier()

        _par = {}

        def core_par(e):
            if "v" not in _par:
                _par["v"] = e.partition_id() % 2
            return _par["v"]

        def mcol(l, chunk, j=0):
            return modT[:, l, chunk, j:j + 1]

        def norm_mod(xt, xoff, n, blocks, gcols, bcols, hout, hoff, t_x, t_h, ss_banks, tmp):
            sqc, lnv, rstd, xnc, t_sq, t_r, t_xn = tmp
            for c in range(8):
                i = c % 2
                p.op("act", (lambda e, c=c, i=i: e.activation(out=sqc[i][:, 0:n], in_=xt[:, c, xoff:xoff + n], func=AF.Square)),
                     reads=[t_x], writes=[t_sq[i]])
                for bi, (c0, w) in enumerate(blocks):
                    bk, bt = ss_banks[bi]
                    p.op("pe", (lambda e, c=c, i=i, bk=bk, c0=c0, w=w: e.matmul(bk[:, 0:w], ones_bf[:], sqc[i][:, c0:c0 + w],
                                                                               start=(c == 0), stop=(c == 7))),
                         reads=[t_sq[i], tconst], writes=[bt])
            for bi, (c0, w) in enumerate(blocks):
                bk, bt = ss_banks[bi]
                p.op("act", (lambda e, bk=bk, c0=c0, w=w: e.activation(out=lnv[:, c0:c0 + w], in_=bk[:, 0:w], func=AF.Ln,
                                                                       scale=1.0 / D, bias=eps_sb[:, 0:1])),
                     reads=[bt, tconst], writes=[t_r])
            p.op("act", lambda e: e.activation(out=rstd[:, 0:n], in_=lnv[:, 0:n], func=AF.Exp, scale=-0.5), reads=[t_r], writes=[t_r])
            for c in range(8):
                i = c % 2
                if bcols is not None:
                    p.op("dve", (lambda e, c=c, i=i: e.scalar_tensor_tensor(
                        out=xnc[i][:, 0:n], in0=xt[:, c, xoff:xoff + n], scalar=gcols(c), op0=ALU.mult,
                        in1=rstd[:, 0:n], op1=ALU.mult)),
                        reads=[t_x, t_r, tmod, tconst], writes=[t_xn[i]])
                    p.op("act", (lambda e, c=c, i=i: e.activation(out=hout[:, c, hoff:hoff + n], in_=xnc[i][:, 0:n],
                                                                  func=AF.Identity, bias=bcols(c))),
                         reads=[t_xn[i], tmod, tconst], writes=[t_h])
                else:
                    p.op("dve", (lambda e, c=c: e.scalar_tensor_tensor(
                        out=hout[:, c, hoff:hoff + n], in0=xt[:, c, xoff:xoff + n], scalar=gcols(c), op0=ALU.mult,
                        in1=rstd[:, 0:n], op1=ALU.mult)),
                        reads=[t_x, t_r, tmod, tconst], writes=[t_h])

        def make_norm_tmps(S_, tag, n):
            sqc = [sbuf(S_, "%s_sq%d" % (tag, i), [128, n], BF) for i in range(2)]
            lnv = sbuf(S_, tag + "_lnv", [128, n], F32)
            rstd = sbuf(S_, tag + "_rstd", [128, n], F32)
            xnc = [sbuf(S_, "%s_xn%d" % (tag, i), [128, n], F32) for i in range(2)]
            return (sqc, lnv, rstd, xnc, p.toks(2, tag + "sq"), p.tok(tag + "r"), p.toks(2, tag + "xn"))

        def load_cols(dst, t_dst, src2d, ranges, tag):
            with ExitStack() as S2:
                stg = [sbuf(S2, "%s_stg%d" % (tag, i), [128, 8, 512], F32) for i in range(2)]
                stg_t = p.toks(2, tag + "stg")
                stg_l = [p.lane("%s_l%d" % (tag, i)) for i in range(2)]
                it = 0
                d0 = 0
                for (c0, n) in ranges:
                    for b0 in range(0, n, 512):
                        w = min(512, n - b0)
                        s = it % 2
                        it += 1
                        src = src2d.rearrange("(c p) n -> p c n", p=128)[:, :, c0 + b0:c0 + b0 + w]
                        p.dma("sp", stg[s][:, :, 0:w], src, writes=[stg_t[s]], lane=stg_l[s])
                        p.op("pool", (lambda e, s=s, d0=d0, w=w: e.tensor_copy(out=dst[:, :, d0:d0 + w], in_=stg[s][:, :, 0:w])),
                             reads=[stg_t[s]], writes=[t_dst])
                        d0 += w
            p.barrier()

        eps_sb = sbuf(G, "eps_sb", [128, 1], F32)
        p.op("pool", lambda e: e.memset(eps_sb[:], EPS), writes=[tconst])

        def load_cast(S_, src_ap, stage, stage_t, stage_l, dst_ap, dst_t, shape_ok=True):
            p.dma("sp", stage, src_ap, writes=[stage_t], lane=stage_l)
            p.op("pool", lambda e: e.tensor_copy(out=dst_ap, in_=stage), reads=[stage_t], writes=[dst_t])

        def finish():
            p.emit(final_lanes=[lane_out])
            return nc

        def attention_phase():
            A = ExitStack()
            CKbT = sbuf(A, "CKbT", [128, 4, CTX], BF)
            CVb = sbuf(A, "CVb", [128, 2, 512], BF)
            OaT = sbuf(A, "OaT", [128, 4, L], BF)
            t_ckb = p.tok("CKb")
            t_oa = p.toks(8, "OaT")
            A1 = ExitStack()
            KaT = sbuf(A1, "KaT", [128, 2, L + CTX], BF)
            Va = sbuf(A1, "Va", [128, 34, 256], BF)
            t_ka = p.tok("KaT")
            t_va = p.tok("Va")

            QA0, QB0, KA0, VA0, KB0, VB0 = 0, 512, 1024, 1280, 1536, 2048

            def qk_head(ps_bank, n, gcol, dst_ap, tok_dst, cos_ap, sin_ap, t_rope, scale, rots, tmps):
                (pk, pt) = ps_bank
                ss_b, ss_t = rots["ss"].next()
                qb, qsq, lnv, rstd, t1, t2, t_q, t_r, t_1, t_2 = tmps.next()
                rope = cos_ap is not None
                p.op("act", lambda e: e.activation(out=qb[:, 0:n], in_=pk[:, 0:n], func=AF.Copy, scale=gcol),
                     reads=[pt, tconst], writes=[t_q])
                p.op("act", lambda e: e.activation(out=qsq[:, 0:n], in_=pk[:, 0:n], func=AF.Square), reads=[pt], writes=[t_q])
                p.op("pe", lambda e: e.matmul(ss_b[:, 0:n], ones_bf[:], qsq[:, 0:n], start=True, stop=True),
                     reads=[t_q, tconst], writes=[ss_t])
                if rope:
                    qr_b, qr_t = rots["qr"].next()
                    p.op("pe", lambda e: e.matmul(qr_b[:, 0:n], rm_bf[:], qb[:, 0:n], start=True, stop=True),
                         reads=[t_q, tconst], writes=[qr_t])
                p.op("act", lambda e: e.activation(out=lnv[:, 0:n], in_=ss_b[:, 0:n], func=AF.Ln, scale=1.0 / 128, bias=eps_sb[:, 0:1]),
                     reads=[ss_t, tconst], writes=[t_r])
                p.op("act", lambda e: e.activation(out=rstd[:, 0:n], in_=lnv[:, 0:n], func=AF.Exp, scale=-0.5), reads=[t_r], writes=[t_r])
                if rope:
                    p.op("pool", lambda e: e.tensor_tensor(out=t1[:, 0:n], in0=qb[:, 0:n], in1=cos_ap, op=ALU.mult),
                         reads=[t_q, t_rope], writes=[t_1])
                    p.op("dve", lambda e: e.tensor_tensor(out=t2[:, 0:n], in0=qr_b[:, 0:n], in1=sin_ap, op=ALU.mult),
                         reads=[qr_t, t_rope], writes=[t_2])
                    p.op("dve", lambda e: e.tensor_tensor(out=t2[:, 0:n], in0=t2[:, 0:n], in1=t1[:, 0:n], op=ALU.add),
                         reads=[t_1, t_2], writes=[t_2])
                    p.op("dve", lambda e: e.scalar_tensor_tensor(out=dst_ap, in0=t2[:, 0:n], scalar=float(scale), op0=ALU.mult,
                                                                 in1=rstd[:, 0:n], op1=ALU.mult),
                         reads=[t_2, t_r], writes=[tok_dst])
                else:
                    p.op("dve", lambda e: e.scalar_tensor_tensor(out=dst_ap, in0=qb[:, 0:n], scalar=float(scale), op0=ALU.mult,
                                                                 in1=rstd[:, 0:n], op1=ALU.mult),
                         reads=[t_q, t_r], writes=[tok_dst])

            def proj_fm(wsb, t_w, htile, t_h, n, col0, bank):
                bk, bt = bank
                for k in range(8):
                    p.op("pe", (lambda e, k=k: e.matmul(bk[:, 0:n], wsb[:, k, col0:col0 + 128], htile[:, k, 0:n],
                                                        start=(k == 0), stop=(k == 7))),
                         reads=[t_h, t_w], writes=[bt])

            def proj_tm(wsb, t_w, htile, t_h, tok0, col0, ncols, bank):
                bk, bt = bank
                for k in range(8):
                    p.op("pe", (lambda e, k=k: e.matmul(bk[:, 0:ncols], htile[:, k, tok0:tok0 + 128],
                                                        wsb[:, k, col0:col0 + ncols], start=(k == 0), stop=(k == 7))),
                         reads=[t_h, t_w], writes=[bt])

            def make_qk_tmps(S_, tag, nbuf=2):
                items = []
                for i in range(nbuf):
                    qb = sbuf(S_, "%s_qb%d" % (tag, i), [128, 512], BF)
                    qsq = sbuf(S_, "%s_qsq%d" % (tag, i), [128, 512], BF)
                    lnv = sbuf(S_, "%s_lnv%d" % (tag, i), [128, 512], F32)
                    rstd = sbuf(S_, "%s_rstd%d" % (tag, i), [128, 512], F32)
                    t1 = sbuf(S_, "%s_t1%d" % (tag, i), [128, 512], F32)
                    t2 = sbuf(S_, "%s_t2%d" % (tag, i), [128, 512], F32)
                    items.append((qb, qsq, lnv, rstd, t1, t2) + tuple(p.toks(4, tag + "tmp")))
                return Rot(items)

            with ExitStack() as S:
                wsb = sbuf(S, "pk_w", [128, 8, 1536], BF)
                t_w = p.tok("pk_w")
                load_cols(wsb, t_w, w_in, [(KA0, 1536)], "pkw")
                cos_sb = sbuf(S, "pk_cos", [128, L], BF)
                sin_sb = sbuf(S, "pk_sin", [128, L], BF)
                t_rope = p.tok("pk_rope")
                p.dma("sp", cos_sb[:], rope_cos, writes=[t_rope], lane=lane_r)
                p.dma("sp", sin_sb[:], rope_sin, writes=[t_rope], lane=lane_r)
                xt = [sbuf(S, "pk_xt%d" % i, [128, 8, 512], F32) for i in range(2)]
                xt_t = p.toks(2, "pk_xt")
                xt_l = [p.lane("pk_xt%d" % i) for i in range(2)]
                ht = [sbuf(S, "pk_ht%d" % i, [128, 8, 512], BF) for i in range(2)]
                ht_t = p.toks(2, "pk_ht")
                ntmp = make_norm_tmps(S, "pk_n", 512)
                qtmps = make_qk_tmps(S, "pk_q")
                rots = dict(ss=Rot([banks[3], banks[4]]), qr=Rot([banks[5], banks[6]]))
                prj = Rot([banks[1], banks[2]])
                for ti in range(9):
                    s = ti % 2
                    is_ctx = (ti == 8)
                    n = CTX if is_ctx else 512
                    tok0 = ti * 512
                    src = fm(ctxT) if is_ctx else fm(xT)[:, :, tok0:tok0 + 512]
                    p.dma("sp", xt[s][:, :, 0:n], src, writes=[xt_t[s]], lane=xt_l[s])
                    li = 2 if is_ctx else 0
                    jj = 1 if is_ctx else 0
                    norm_mod(xt[s], 0, n, [(0, n)], lambda c, li=li: gs[:, 0, li, c:c + 1],
                             lambda c, jj=jj: mcol(0, c, jj), ht[s], 0, xt_t[s], ht_t[s], [banks[0]], ntmp)
                    for kv in range(2):
                        bank = prj.next()
                        proj_fm(wsb, t_w, ht[s], ht_t[s], n, kv * 128, bank)
                        if is_ctx:
                            qk_head(bank, n, qkg_sb[:, 1:2], KaT[:, kv, tok0:tok0 + n], t_ka, None, None, None,
                                    1.0, rots, qtmps)
                        else:
                            qk_head(bank, n, qkg_sb[:, 1:2], KaT[:, kv, tok0:tok0 + n], t_ka,
                                    cos_sb[:, tok0:tok0 + n], sin_sb[:, tok0:tok0 + n], t_rope, 1.0, rots, qtmps)
                    for sub in range(n // 128):
                        chunk = ti * 4 + sub
                        proj_tm(wsb, t_w, ht[s], ht_t[s], sub * 128, 256, 256, banks[7])
                        p.op("act", (lambda e, chunk=chunk: e.activation(out=Va[:, chunk, :], in_=banks[7][0][:, 0:256], func=AF.Copy)),
                             reads=[banks[7][1]], writes=[t_va])
                    if is_ctx:
                        for ch in range(4):
                            bank = prj.next()
                            proj_fm(wsb, t_w, ht[s], ht_t[s], n, 512 + ch * 128, bank)
                            p.op("act", (lambda e, ch=ch, bank=bank: e.activation(out=CKbT[:, ch, :], in_=bank[0][:, 0:CTX], func=AF.Copy)),
                                 reads=[bank[1]], writes=[t_ckb])
                        for sub in range(2):
                            proj_tm(wsb, t_w, ht[s], ht_t[s], sub * 128, 1024, 512, banks[7])
                            p.op("act", (lambda e, sub=sub: e.activation(out=CVb[:, sub, :], in_=banks[7][0][:, 0:512], func=AF.Copy)),
                                 reads=[banks[7][1]], writes=[t_ckb])
            p.barrier()

            def dump_bf(name, tens, shape, tk):
                if name in dbg:
                    with ExitStack() as S:
                        tmpf = sbuf(S, "dbgt_" + name, shape, F32)
                        tt = p.tok()
                        p.op("dve", lambda e: e.tensor_copy(out=tmpf[:], in_=tens[:]), reads=tk, writes=[tt])
                        p.dma("sp", dbg[name], tmpf[:], reads=[tt], lane=lane_out)
                        p.barrier()

            dump_bf("KaT", KaT, [128, 2, L + CTX], [t_ka])
            dump_bf("Va", Va, [128, 34, 256], [t_va])
            if "modT" in dbg:
                p.dma("sp", dbg["modT"], modT[:], reads=[tmod], lane=lane_out)

            if stop_after == "PK":
                A1.close()
                A.close()
                return True

            precast_jobs = []
            for l_ in range(2):
                for j_ in range(NPAIR):
                    precast_jobs.append((w_up[l_, j_], wupb[l_, j_], 8 * 256))
                for dc_ in range(8):
                    precast_jobs.append((w_down[l_, dc_], wdb[l_, dc_], NPAIR * 128))
            precast_it = [0]
            for half in range(2):
                q0 = half * 2048
                with ExitStack() as H:
                    QaT = sbuf(H, "QaT", [128, 4, 2048], BF)
                    t_qa = p.tok("QaT")
                    with ExitStack() as S:
                        wsb = sbuf(S, "gq_w", [128, 8, 512], BF)
                        t_w = p.tok("gq_w")
                        load_cols(wsb, t_w, w_in, [(QA0, 512)], "gqw%d" % half)
                        cos_sb = sbuf(S, "gq_cos", [128, 2048], BF)
                        sin_sb = sbuf(S, "gq_sin", [128, 2048], BF)
                        t_rope = p.tok("gq_rope")
                        p.dma("sp", cos_sb[:], rope_cos[:, q0:q0 + 2048], writes=[t_rope], lane=lane_r)
                        p.dma("sp", sin_sb[:], rope_sin[:, q0:q0 + 2048], writes=[t_rope], lane=lane_r)
                        xt = [sbuf(S, "gq_xt%d" % i, [128, 8, 512], F32) for i in range(2)]
                        xt_t = p.toks(2, "gq_xt")
                        xt_l = [p.lane("gq_xt%d_%d" % (half, i)) for i in range(2)]
                        ht = [sbuf(S, "gq_ht%d" % i, [128, 8, 512], BF) for i in range(2)]
                        ht_t = p.toks(2, "gq_ht")
                        ntmp = make_norm_tmps(S, "gq_n", 512)
                        qtmps = make_qk_tmps(S, "gq_q")
                        rots = dict(ss=Rot([banks[3], banks[4]]), qr=Rot([banks[5], banks[6]]))
                        prj = Rot([banks[1], banks[2]])
                        for ti in range(4):
                            s = ti % 2
                            tok0 = q0 + ti * 512
                            lt0 = ti * 512
                            p.dma("sp", xt[s][:], fm(xT)[:, :, tok0:tok0 + 512], writes=[xt_t[s]], lane=xt_l[s])
                            norm_mod(xt[s], 0, 512, [(0, 512)], lambda c: gs[:, 0, 0, c:c + 1], lambda c: mcol(0, c, 0),
                                     ht[s], 0, xt_t[s], ht_t[s], [banks[0]], ntmp)
                            for a in range(4):
                                bank = prj.next()
                                proj_fm(wsb, t_w, ht[s], ht_t[s], 512, a * 128, bank)
                                qk_head(bank, 512, qkg_sb[:, 0:1], QaT[:, a, lt0:lt0 + 512], t_qa,
                                        cos_sb[:, lt0:lt0 + 512], sin_sb[:, lt0:lt0 + 512], t_rope, SCALE_A, rots, qtmps)
                    p.barrier()
                    with ExitStack() as S:
                        pts = [sbuf(S, "g_pt%d" % i, [128, 512], BF) for i in range(4)]
                        pt_t = p.toks(4, "g_pt")
                        rl = sbuf(S, "g_rl", [128, 512], F32)
                        t_rl = p.tok("g_rl")
                        pc32 = [sbuf(S, "pc32_%d" % i, [128, NPAIR * 128], F32) for i in range(2)]
                        pc16 = [sbuf(S, "pc16_%d" % i, [128, NPAIR * 128], BF) for i in range(2)]
                        pc32_t = p.toks(2, "pc32")
                        pc16_t = p.toks(2, "pc16")
                        pc32_l = [p.lane("pc32_%d_%d" % (half, i)) for i in range(2)]
                        pc16_l = [p.lane("pc16_%d_%d" % (half, i)) for i in range(2)]
                        srot = Rot([banks[0], banks[1], banks[2], banks[7]])
                        orot = Rot([(banks[3], banks[5]), (banks[4], banks[6])])
                        NKC = 34
                        for a in range(4):
                            kv = a // 2
                            for qt in range(4):
                                for _ in range(2):
                                    if precast_jobs:
                                        (srcw, dstw, nel) = precast_jobs.pop(0)
                                        s = precast_it[0] % 2
                                        precast_it[0] += 1
                                        p.dma("sp", pc32[s][:, 0:nel], srcw, writes=[pc32_t[s]], lane=pc32_l[s])
                                        p.op("pool", (lambda e, s=s, nel=nel: e.tensor_copy(out=pc16[s][:, 0:nel], in_=pc32[s][:, 0:nel])),
                                             reads=[pc32_t[s]], writes=[pc16_t[s]])
                                        p.dma("sp", dstw, pc16[s][:, 0:nel], reads=[pc16_t[s]], lane=pc16_l[s])
                                (ob, ot), (lb, lt) = orot.next()
                                qsl = slice(qt * 512, (qt + 1) * 512)
                                gsl = slice(q0 + qt * 512, q0 + (qt + 1) * 512)

                                def s_mm(kc, a=a, kv=kv, qsl=qsl):
                                    sb_, st_ = srot.next()
                                    p.op("pe", lambda e: e.matmul(sb_[:, :], KaT[:, kv, kc * 128:(kc + 1) * 128], QaT[:, a, qsl],
                                                                  start=True, stop=True),
                                         reads=[t_ka, t_qa], writes=[st_])
                                    return sb_, st_

                                def pv_mm(kc, sbk, stk, kv=kv, ob=ob, ot=ot, lb=lb, lt=lt):
                                    i = kc % 4
                                    p.op("act", lambda e: e.activation(out=pts[i][:], in_=sbk[:, :], func=AF.Exp),
                                         reads=[stk], writes=[pt_t[i]])
                                    p.op("pe", lambda e: e.matmul(ob[:, :], Va[:, kc, kv * 128:(kv + 1) * 128], pts[i][:],
                                                                  start=(kc == 0), stop=(kc == NKC - 1)),
                                         reads=[pt_t[i], t_va], writes=[ot])
                                    p.op("pe", lambda e: e.matmul(lb[:, :], ones_bf[:], pts[i][:],
                                                                  start=(kc == 0), stop=(kc == NKC - 1)),
                                         reads=[pt_t[i], tconst], writes=[lt])

                                q_s = [s_mm(0), s_mm(1)]
                                for kc in range(NKC):
                                    if kc + 2 < NKC:
                                        q_s.append(s_mm(kc + 2))
                                    cur = q_s.pop(0)
                                    pv_mm(kc, cur[0], cur[1])
                                p.op("dve", lambda e, lb=lb: e.reciprocal(out=rl[:], in_=lb[:, :]), reads=[lt], writes=[t_rl])
                                p.op("dve", (lambda e, ob=ob, a=a, gsl=gsl: e.tensor_tensor(out=OaT[:, a, gsl], in0=ob[:, :], in1=rl[:],
                                                                                           op=ALU.mult)),
                                     reads=[ot, t_rl], writes=[t_oa[half * 4 + qt]])
                        assert half == 0 or not precast_jobs
                    p.barrier()
            A1.close()
            p.barrier()
            dump_bf("OaT", OaT, [128, 4, L], t_oa)
            if stop_after == "GQA":
                A.close()
                return True

            for half in range(2):
                q0 = half * 2048
                kf0 = 0 if half == 0 else 1536
                with ExitStack() as H:
                    QbT = sbuf(H, "QbT", [128, 4, 2048], BF)
                    KbT = sbuf(H, "KbT", [128, 4, 2560], BF)
                    Vb = sbuf(H, "Vb", [128, 20, 512], BF)
                    t_qb, t_kb, t_vb = p.toks(3, "na")
                    with ExitStack() as S:
                        wsb = sbuf(S, "na_w", [128, 8, 1536], BF)
                        t_w = p.tok("na_w")
                        load_cols(wsb, t_w, w_in, [(QB0, 512), (KB0, 1024)], "naw%d" % half)
                        xt = [sbuf(S, "na_xt%d" % i, [128, 8, 512], F32) for i in range(2)]
                        xt_t = p.toks(2, "na_xt")
                        xt_l = [p.lane("na_xt%d_%d" % (half, i)) for i in range(2)]
                        ht = [sbuf(S, "na_ht%d" % i, [128, 8, 512], BF) for i in range(2)]
                        ht_t = p.toks(2, "na_ht")
                        ntmp = make_norm_tmps(S, "na_n", 512)
                        prj = Rot([banks[1], banks[2], banks[3]])
                        vrot = Rot([banks[6], banks[7]])
                        for ti in range(5):
                            s = ti % 2
                            tok0 = kf0 + ti * 512
                            own = (q0 <= tok0 < q0 + 2048)
                            p.dma("sp", xt[s][:], fm(xT)[:, :, tok0:tok0 + 512], writes=[xt_t[s]], lane=xt_l[s])
                            norm_mod(xt[s], 0, 512, [(0, 512)], lambda c: gs[:, 0, 0, c:c + 1], lambda c: mcol(0, c, 0),
                                     ht[s], 0, xt_t[s], ht_t[s], [banks[0]], ntmp)
                            for ch in range(4):
                                bank = prj.next()
                                proj_fm(wsb, t_w, ht[s], ht_t[s], 512, 512 + ch * 128, bank)
                                p.op("act", (lambda e, ch=ch, bank=bank, ti=ti: e.activation(
                                    out=KbT[:, ch, ti * 512:(ti + 1) * 512], in_=bank[0][:, 0:512], func=AF.Copy)),
                                    reads=[bank[1]], writes=[t_kb])
                            for sub in range(4):
                                vb_ = vrot.next()
                                proj_tm(wsb, t_w, ht[s], ht_t[s], sub * 128, 1024, 512, vb_)
                                p.op("dve", (lambda e, c=ti * 4 + sub, vb_=vb_: e.tensor_copy(out=Vb[:, c, :], in_=vb_[0][:, 0:512])),
                                     reads=[vb_[1]], writes=[t_vb])
                            if own:
                                lt0 = tok0 - q0
                                for ch in range(4):
                                    bank = prj.next()
                                    proj_fm(wsb, t_w, ht[s], ht_t[s], 512, ch * 128, bank)
                                    p.op("act", (lambda e, ch=ch, bank=bank, lt0=lt0: e.activation(
                                        out=QbT[:, ch, lt0:lt0 + 512], in_=bank[0][:, 0:512], func=AF.Copy, scale=SCALE_B)),
                                        reads=[bank[1]], writes=[t_qb])
                    p.barrier()
                    with ExitStack() as S:
                        bt_sb = sbuf(S, "bt_sb", [128, NTB * 8, 64], BF)
                        t_bt = p.tok("bt")
                        wout_bf = sbuf(S, "wout_bf", [128, 8, D], BF)
                        t_wo = p.tok("wout")
                        load_cols(wout_bf, t_wo, w_out, [(0, D)], "wo%d" % half)
                        with ExitStack() as S2:
                            NP = 8
                            per = (NTB * 8 + NP - 1) // NP
                            stg = sbuf(S2, "bt_stg", [128, per, 64], F32)
                            stg_t = p.tok("btstg")
                            stg_l = p.lane("btstg%d" % half)
                            for pi in range(NP):
                                a0 = pi * per
                                a1 = min(NTB * 8, a0 + per)
                                if a1 <= a0:
                                    continue
                                p.dma("sp", stg[:, 0:a1 - a0, :], btiles[:, a0:a1, :], writes=[stg_t], lane=stg_l)
                                p.op("pool", (lambda e, a0=a0, a1=a1: e.tensor_copy(out=bt_sb[:, a0:a1, :], in_=stg[:, 0:a1 - a0, :])),
                                     reads=[stg_t], writes=[t_bt])
                        p.barrier()
                        ObT = [sbuf(S, "ObT%d" % i, [128, 4, 512], BF) for i in range(2)]
                        ob_t = p.toks(2, "ObT")
                        ptA = [sbuf(S, "na_ptA%d" % i, [128, 512], BF) for i in range(2)]
                        ptB = [sbuf(S, "na_ptB%d" % i, [128, 512], BF) for i in range(2)]
                        ptA_t = p.toks(2, "na_ptA")
                        ptB_t = p.toks(2, "na_ptB")
                        rln = [sbuf(S, "na_rl%d" % i, [128, 64], F32) for i in range(2)]
                        sadd = [[sbuf(S, "na_sa%d_%d" % (hh_, i), [128, 320], F32) for i in range(2)] for hh_ in range(2)]
                        sadd_t = [p.toks(2, "na_sa%d" % hh_) for hh_ in range(2)]
                        rln_t = p.toks(2, "na_rl")
                        xr = [sbuf(S, "na_xr%d" % i, [128, 8, 512], F32) for i in range(2)]
                        xr_t = p.toks(2, "na_xr")
                        xr_l = [p.lane("na_xr%d_%d" % (half, i)) for i in range(2)]
                        xo_l = [p.lane("na_xo%d_%d" % (half, i)) for i in range(2)]
                        srot = Rot([(banks[0], banks[1]), (banks[2], banks[3])])
                        olrot = Rot([banks[4], banks[5]])
                        worot = Rot([banks[6], banks[7]])
                        it = 0
                        for rt in range(4):
                            ts_ = rt % 2
                            tokA = q0 + rt * 512
                            p.dma("sp", xr[ts_][:], fm(xT)[:, :, tokA:tokA + 512], writes=[xr_t[ts_]], lane=xr_l[ts_])
                            for rr in range(8):
                                r = half * 32 + rt * 8 + rr
                                m_lo, nw, k0t = plan[r]
                                ncols = (nw + 2) * 64
                                ql = slice((rt * 8 + rr) * 64, (rt * 8 + rr + 1) * 64)
                                for pr in range(4):
                                    (sA, sB) = srot.next()
                                    olb, olt = olrot.next()
                                    bsel = it % 2
                                    it += 1
                                    for hh, (sbk, stk) in enumerate((sA, sB)):
                                        h = pr * 2 + hh
                                        prt = slice(hh * 64, hh * 64 + 64)
                                        for wi in range(nw):
                                            kt0 = (m_lo + wi) * 128 - kf0
                                            p.op("pe", (lambda e, sbk=sbk, wi=wi, kt0=kt0, prt=prt, pr=pr, ql=ql: e.matmul(
                                                sbk[:, wi * 64:(wi + 1) * 64], KbT[prt, pr, kt0:kt0 + 128], QbT[prt, pr, ql],
                                                start=True, stop=True)),
                                                reads=[t_kb, t_qb], writes=[stk])
                                        for cc_ in range(2):
                                            wi = nw + cc_
                                            p.op("pe", (lambda e, sbk=sbk, wi=wi, cc_=cc_, prt=prt, pr=pr, ql=ql: e.matmul(
                                                sbk[:, wi * 64:(wi + 1) * 64], CKbT[prt, pr, cc_ * 128:(cc_ + 1) * 128], QbT[prt, pr, ql],
                                                start=True, stop=True)),
                                                reads=[t_ckb, t_qb], writes=[stk])
                                        ptx, ptt = ((ptA, ptA_t), (ptB, ptB_t))[hh]
                                        sa, sat = sadd[hh][bsel], sadd_t[hh][bsel]
                                        wc = nw * 64
                                        p.op("dve", (lambda e, sbk=sbk, sa=sa, wc=wc, h=h, k0t=k0t: e.tensor_tensor(
                                            out=sa[:, 0:wc], in0=sbk[:, 0:wc],
                                            in1=bt_sb[:, h * NTB + k0t:h * NTB + k0t + wc // 64, :].rearrange("p a b -> p (a b)"),
                                            op=ALU.add)),
                                            reads=[stk, t_bt], writes=[sat])
                                        p.op("act", (lambda e, sa=sa, ptx=ptx, bsel=bsel, wc=wc: e.activation(
                                            out=ptx[bsel][:, 0:wc], in_=sa[:, 0:wc], func=AF.Exp)),
                                            reads=[sat], writes=[ptt[bsel]])
                                        p.op("act", (lambda e, sbk=sbk, ptx=ptx, bsel=bsel, wc=wc, ncols=ncols: e.activation(
                                            out=ptx[bsel][:, wc:ncols], in_=sbk[:, wc:ncols], func=AF.Exp)),
                                            reads=[stk], writes=[ptt[bsel]])
                                    for hh in range(2):
                                        h = pr * 2 + hh
                                        ptx, ptt = ((ptA, ptA_t), (ptB, ptB_t))[hh]
                                        orow = slice(hh * 64, hh * 64 + 64)
                                        for wi in range(nw + 2):
                                            if wi < nw:
                                                vch = m_lo + wi - kf0 // 128
                                                lhs = Vb[:, vch, h * 64:(h + 1) * 64]
                                                rd = [t_vb]
                                            else:
                                                lhs = CVb[:, wi - nw, h * 64:(h + 1) * 64]
                                                rd = [t_ckb]
                                            p.op("pe", (lambda e, lhs=lhs, ptx=ptx, bsel=bsel, wi=wi, orow=orow, olb=olb, nw=nw: e.matmul(
                                                olb[orow, 0:64], lhs, ptx[bsel][:, wi * 64:(wi + 1) * 64],
                                                start=(wi == 0), stop=(wi == nw + 1))),
                                                reads=rd + [ptt[bsel]], writes=[olt])
                                    for hh in range(2):
                                        ptx, ptt = ((ptA, ptA_t), (ptB, ptB_t))[hh]
                                        orow = slice(hh * 64, hh * 64 + 64)
                                        for wi in range(nw + 2):
                                            p.op("pe", (lambda e, ptx=ptx, bsel=bsel, wi=wi, orow=orow, olb=olb, nw=nw: e.matmul(
                                                olb[orow, 64:128], ones_bf[:, 0:64], ptx[bsel][:, wi * 64:(wi + 1) * 64],
                                                start=(wi == 0), stop=(wi == nw + 1))),
                                                reads=[ptt[bsel], tconst], writes=[olt])
                                    p.op("dve", (lambda e, olb=olb, bsel=bsel: e.reciprocal(out=rln[bsel][:], in_=olb[:, 64:128])),
                                         reads=[olt], writes=[rln_t[bsel]])
                                    p.op("dve", (lambda e, olb=olb, bsel=bsel, ts_=ts_, pr=pr, rr=rr: e.tensor_tensor(
                                        out=ObT[ts_][:, pr, rr * 64:(rr + 1) * 64], in0=olb[:, 0:64], in1=rln[bsel][:], op=ALU.mult)),
                                        reads=[olt, rln_t[bsel]], writes=[ob_t[ts_]])
                            for dc in range(8):
                                wb, wt = worot.next()
                                for kc in range(8):
                                    if kc < 4:
                                        rhs = OaT[:, kc, tokA:tokA + 512]
                                        rd = [t_oa[half * 4 + rt]]
                                    else:
                                        rhs = ObT[ts_][:, kc - 4, :]
                                        rd = [ob_t[ts_]]
                                    p.op("pe", (lambda e, wb=wb, kc=kc, dc=dc, rhs=rhs: e.matmul(
                                        wb[:, :], wout_bf[:, kc, dc * 128:(dc + 1) * 128], rhs, start=(kc == 0), stop=(kc == 7))),
                                        reads=rd + [t_wo], writes=[wt])
                                p.op("dve", (lambda e, wb=wb, dc=dc, ts_=ts_: e.scalar_tensor_tensor(
                                    out=xr[ts_][:, dc, :], in0=wb[:, :], scalar=mcol(0, 16 + dc), op0=ALU.mult,
                                    in1=xr[ts_][:, dc, :], op1=ALU.add)),
                                    reads=[wt, tmod], writes=[xr_t[ts_]])
                            p.dma("sp", fm(x1T)[:, :, tokA:tokA + 512], xr[ts_][:], reads=[xr_t[ts_]], lane=xo_l[ts_])
                    p.barrier()
            A.close()
            p.barrier()


            return False

        if not skip_att:
            if attention_phase():
                return finish()

        def dump_scr(name, scr):
            if name in dbg:
                with ExitStack() as S:
                    tmpf = sbuf(S, "dbgs_" + name, [128, 8, 512], F32)
                    tt = p.tok()
                    ll = p.lane("dbg" + name)
                    for ti in range(8):
                        p.dma("sp", tmpf[:], fm(scr)[:, :, ti * 512:(ti + 1) * 512], writes=[tt], lane=ll)
                        p.dma("sp", fm(dbg[name])[:, :, ti * 512:(ti + 1) * 512], tmpf[:], reads=[tt], writes=[tt], lane=ll)
                    p.barrier()

        dump_scr("x1T", x1T)
        if stop_after == "ATT":
            return finish()

        def ffn_layer(l, src, dst, final_norm, local=False):
            with ExitStack() as S:
                NS = 1024
                NW = NS + 2
                xt = sbuf(S, "f_xt", [128, 8, NW], F32)
                t_xt = p.tok("f_xt")
                l_xt = p.lane("f_xt%d" % l)
                h2 = sbuf(S, "f_h2", [128, 8, NW], BF)
                t_h2 = p.tok("f_h2")
                ntmp = make_norm_tmps(S, "f_n", NW)
                aT = sbuf(S, "f_aT", [128, NPAIR, NS], BF)
                t_aT = p.tok("f_aT")
                NWB = 3
                wbf = [sbuf(S, "f_wbf%d" % i, [128, 8, 256], BF) for i in range(NWB)]
                wbf_t = p.toks(NWB, "f_wbf")
                wbf_l = [p.lane("f_wbf%d_%d" % (l, i)) for i in range(NWB)]
                wdbf = [sbuf(S, "f_wdbf%d" % i, [128, NPAIR, 128], BF) for i in range(2)]
                wdbf_t = p.toks(2, "f_wdbf")
                wdbf_l = [p.lane("f_wdbf%d_%d" % (l, i)) for i in range(2)]
                ug = [[sbuf(S, "f_ug%d_%d" % (g_, i), [128, NW], F32) for i in range(2)] for g_ in range(2)]
                ug_t = [p.toks(2, "f_ug%d" % g_) for g_ in range(2)]
                a1 = [[sbuf(S, "f_a1%d_%d" % (g_, i), [128, NS], F32) for i in range(2)] for g_ in range(2)]
                a1_t = [p.toks(2, "f_a1%d" % g_) for g_ in range(2)]
                sg = [sbuf(S, "f_sg%d" % i, [128, NS], F32) for i in range(2)]
                sg_t = p.toks(2, "f_sg")
                l_xo = p.lane("f_xo%d" % l)
                blocks = [(0, 342), (342, 342), (684, 342)]
                ctr = [((1, 342), (0, 341)), ((0, 342), (341, 683)), ((0, 341), (683, 1024))]
                urot = Rot([(banks[0], banks[1], banks[2]), (banks[3], banks[4], banks[5])])
                drot = Rot([banks[6], banks[7]])
                pair_it = 0
                wd_it = 0
                NTOK = 2048 if local else L
                for st_i in range(NTOK // NS):
                    tok0 = st_i * NS
                    lo = tok0 - 1
                    hi = tok0 + NS + 1
                    clo = max(lo, 0)
                    chi = min(hi, L)
                    if local:
                        p.dma("sp", xt[:, :, 0:NW], fm(src)[:, :, tok0:tok0 + NW], writes=[t_xt], lane=l_xt)
                    else:
                        if lo < 0:
                            p.op("pool", lambda e: e.memset(xt[:, :, 0:1], 1.0), writes=[t_xt])
                        if hi > L:
                            p.op("pool", lambda e: e.memset(xt[:, :, NW - 1:NW], 1.0), writes=[t_xt])
                        p.dma("sp", xt[:, :, clo - lo:chi - lo], fm(src)[:, :, clo:chi], writes=[t_xt], lane=l_xt)
                    norm_mod(xt, 0, NW, blocks, lambda c: gs[:, l, 1, c:c + 1], lambda c: mcol(l, 24 + c),
                             h2, 0, t_xt, t_h2, [banks[0], banks[1], banks[2]], ntmp)
                    if local:
                        if st_i == 0:
                            p.op("dve", lambda e: e.tensor_scalar(out=h2[:, :, 0:1], in0=h2[:, :, 0:1], scalar1=hm_sb[:, 0:1],
                                                                  scalar2=None, op0=ALU.mult),
                                 reads=[tconst], writes=[t_h2])
                        if st_i == NTOK // NS - 1:
                            p.op("dve", lambda e: e.tensor_scalar(out=h2[:, :, NW - 1:NW], in0=h2[:, :, NW - 1:NW],
                                                                  scalar1=hm_sb[:, 1:2], scalar2=None, op0=ALU.mult),
                                 reads=[tconst], writes=[t_h2])
                    else:
                        if lo < 0:
                            p.op("pool", lambda e: e.memset(h2[:, :, 0:1], 0.0), writes=[t_h2])
                        if hi > L:
                            p.op("pool", lambda e: e.memset(h2[:, :, NW - 1:NW], 0.0), writes=[t_h2])
                    for j in range(NPAIR):
                        s = pair_it % NWB
                        bsel = pair_it % 2
                        pair_it += 1
                        p.dma("sp", wbf[s][:].rearrange("p k n -> p (k n)"), wupb[l, j], writes=[wbf_t[s]], lane=wbf_l[s])
                        for gv in range(2):
                            ub = urot.next()
                            fch = j if gv == 0 else NPAIR + j
                            ugx, ugt = ug[gv][bsel], ug_t[gv][bsel]
                            a1x, a1t = a1[gv][bsel], a1_t[gv][bsel]
                            for bi, (c0, w) in enumerate(blocks):
                                bk, bt = ub[bi]
                                for k in range(8):
                                    p.op("pe", (lambda e, bk=bk, k=k, c0=c0, w=w, s=s, gv=gv: e.matmul(
                                        bk[:, 0:w], wbf[s][:, k, gv * 128:(gv + 1) * 128], h2[:, k, c0:c0 + w],
                                        start=(k == 0), stop=(k == 7))),
                                        reads=[wbf_t[s], t_h2], writes=[bt])
                                p.op("act", (lambda e, bk=bk, c0=c0, w=w, ugx=ugx: e.activation(
                                    out=ugx[:, c0:c0 + w], in_=bk[:, 0:w], func=AF.Copy)),
                                    reads=[bt], writes=[ugt])
                                (b0, b1), (d0, d1) = ctr[bi]
                                p.op("act", (lambda e, bk=bk, b0=b0, b1=b1, d0=d0, d1=d1, a1x=a1x, fch=fch: e.activation(
                                    out=a1x[:, d0:d1], in_=bk[:, b0:b1], func=AF.Identity,
                                    scale=cw_sb[:, l, fch, 1:2], bias=cw_sb[:, l, fch, 3:4])),
                                    reads=[bt, tconst], writes=[a1t])
                            p.op("dve", (lambda e, fch=fch, ugx=ugx, a1x=a1x: e.scalar_tensor_tensor(
                                out=a1x[:], in0=ugx[:, 0:NS], scalar=cw_sb[:, l, fch, 0:1], op0=ALU.mult,
                                in1=a1x[:], op1=ALU.add)),
                                reads=[ugt, tconst], writes=[a1t])
                            p.op("dve", (lambda e, fch=fch, ugx=ugx, a1x=a1x: e.scalar_tensor_tensor(
                                out=a1x[:], in0=ugx[:, 2:NS + 2], scalar=cw_sb[:, l, fch, 2:3], op0=ALU.mult,
                                in1=a1x[:], op1=ALU.add)),
                                reads=[ugt, tconst], writes=[a1t])
                        p.op("act", (lambda e, bsel=bsel: e.activation(out=sg[bsel][:], in_=a1[0][bsel][:], func=AF.Silu)),
                             reads=[a1_t[0][bsel]], writes=[sg_t[bsel]])
                        p.op("dve", (lambda e, j=j, bsel=bsel: e.tensor_tensor(out=aT[:, j, :], in0=sg[bsel][:], in1=a1[1][bsel][:],
                                                                              op=ALU.mult)),
                             reads=[sg_t[bsel], a1_t[1][bsel]], writes=[t_aT])
                    for dc in range(8):
                        ws_ = wd_it % 2
                        wd_it += 1
                        p.dma("sp", wdbf[ws_][:].rearrange("p j n -> p (j n)"), wdb[l, dc], writes=[wdbf_t[ws_]], lane=wdbf_l[ws_])
                        for hb in range(2):
                            db, dt_ = drot.next()
                            for j in range(NPAIR):
                                p.op("pe", (lambda e, db=db, j=j, ws_=ws_, hb=hb: e.matmul(
                                    db[:, :], wdbf[ws_][:, j, :], aT[:, j, hb * 512:(hb + 1) * 512],
                                    start=(j == 0), stop=(j == NPAIR - 1))),
                                    reads=[wdbf_t[ws_], t_aT], writes=[dt_])
                            p.op("dve", (lambda e, db=db, dc=dc, hb=hb: e.scalar_tensor_tensor(
                                out=xt[:, dc, 1 + hb * 512:1 + (hb + 1) * 512], in0=db[:, :], scalar=mcol(l, 40 + dc), op0=ALU.mult,
                                in1=xt[:, dc, 1 + hb * 512:1 + (hb + 1) * 512], op1=ALU.add)),
                                reads=[dt_, tmod], writes=[t_xt])
                    if not final_norm:
                        p.dma("sp", fm(dst)[:, :, tok0:tok0 + NS], xt[:, :, 1:NS + 1], reads=[t_xt], lane=l_xo)
                    else:
                        norm_mod(xt, 1, NS, [(0, 512), (512, 512)], lambda c: ng_sb[:, 4, c:c + 1], None,
                                 xt, 1, t_xt, t_xt, [banks[0], banks[1]], ntmp)
                        p.dma("sp", fm(dst)[:, :, tok0:tok0 + NS], xt[:, :, 1:NS + 1], reads=[t_xt], lane=lane_out)
            p.barrier()

        if skip_att and ffn0_src != 'skip':
            with ExitStack() as S:
                pc32 = sbuf(S, "tpc32", [128, NPAIR * 128], F32)
                pc16 = sbuf(S, "tpc16", [128, NPAIR * 128], BF)
                tp32, tp16 = p.toks(2, "tpc")
                lp = p.lane("tpc")
                jobs = [(w_up[0, j_], wupb[0, j_], 2048) for j_ in range(NPAIR)] + [(w_down[0, d_], wdb[0, d_], NPAIR * 128) for d_ in range(8)]
                for (srcw, dstw, nel) in jobs:
                    p.dma("sp", pc32[:, 0:nel], srcw, writes=[tp32], lane=lp)
                    p.op("pool", (lambda e, nel=nel: e.tensor_copy(out=pc16[:, 0:nel], in_=pc32[:, 0:nel])), reads=[tp32], writes=[tp16])
                    p.dma("sp", dstw, pc16[:, 0:nel], reads=[tp16], writes=[tp16], lane=lp)
            p.barrier()
        if ffn0_src != 'skip':
            ffn_layer(0, xT if ffn0_src == 'xT' else x1T, x2T, False)
        dump_scr("x2T", x2T)
        if stop_after == "FFN0":
            return finish()

        fsrc = xT if four_src == 'xT' else x2T
        with ExitStack() as S:
            Htok = sbuf(S, "Htok", [128, 32, D], BF)
            t_H = p.tok("Htok")
            wcs = sbuf(S, "wcs", [128, 2, 8, D], BF)
            t_wcs = p.tok("wcs")
            with ExitStack() as S2:
                fw_bf = sbuf(S2, "fw_bf", [128, 8, D], BF)
                t_fw = p.tok("fw")
                load_cols(fw_bf, t_fw, fw, [(0, D)], "fw")
                cs_sb = sbuf(S2, "cs_sb", [128, 2, 2, 256], BF)
                p.dma("sp", cs_sb[:, 0], c256.rearrange("(c p) n -> p c n", p=128), writes=[t_fw], lane=lane_r)
                p.dma("sp", cs_sb[:, 1], s256n.rearrange("(c p) n -> p c n", p=128), writes=[t_fw], lane=lane_r)
                wrot = Rot([banks[4], banks[5]])
                for cs in range(2):
                    for g in range(4):
                        for jc in range(2):
                            for nb in range(2):
                                bk, bt = wrot.next()
                                for kc in range(2):
                                    p.op("pe", (lambda e, bk=bk, cs=cs, g=g, jc=jc, nb=nb, kc=kc: e.matmul(
                                        bk[:, :], cs_sb[:, cs, kc, jc * 128:(jc + 1) * 128], fw_bf[:, g * 2 + kc, nb * 512:(nb + 1) * 512],
                                        start=(kc == 0), stop=(kc == 1))),
                                        reads=[t_fw], writes=[bt])
                                p.op("act", (lambda e, bk=bk, cs=cs, g=g, jc=jc, nb=nb: e.activation(
                                    out=wcs[:, cs, g * 2 + jc, nb * 512:(nb + 1) * 512], in_=bk[:, :], func=AF.Copy)),
                                    reads=[bt], writes=[t_wcs])
            p.barrier()
            with ExitStack() as S2:
                xt = [sbuf(S2, "l1_xt%d" % i, [128, 8, 512], F32) for i in range(2)]
                xt_t = p.toks(2, "l1_xt")
                xt_l = [p.lane("l1_xt%d" % i) for i in range(2)]
                ht = [sbuf(S2, "l1_ht%d" % i, [128, 8, 512], BF) for i in range(2)]
                ht_t = p.toks(2, "l1_ht")
                ntmp = make_norm_tmps(S2, "l1_n", 512)
                trot = Rot([banks[6], banks[7]])
                for ti in range(8):
                    s = ti % 2
                    tok0 = ti * 512
                    p.dma("sp", xt[s][:], fm(fsrc)[:, :, tok0:tok0 + 512], writes=[xt_t[s]], lane=xt_l[s])
                    norm_mod(xt[s], 0, 512, [(0, 512)], lambda c: gs[:, 1, 0, c:c + 1], lambda c: mcol(1, c),
                             ht[s], 0, xt_t[s], ht_t[s], [banks[0]], ntmp)
                    for sub in range(4):
                        tch = ti * 4 + sub
                        bk, bt = trot.next()
                        bkb = bk.bitcast(BF)
                        for c in range(8):
                            p.op("pe", (lambda e, bkb=bkb, c=c, s=s, sub=sub: e.transpose(
                                bkb[:, c * 128:(c + 1) * 128], ht[s][:, c, sub * 128:(sub + 1) * 128], ident_bf[:])),
                                reads=[ht_t[s], tconst], writes=[bt])
                        p.op("dve", (lambda e, bkb=bkb, tch=tch: e.tensor_copy(out=Htok[:, tch, :], in_=bkb[:, 0:1024])),
                             reads=[bt], writes=[t_H])
            p.barrier()
            KB = 256
            dct = [sbuf(S, "dct%d" % i, [128, 32, KB], BF) for i in range(2)]
            dst_ = [sbuf(S, "dst%d" % i, [128, 32, KB], BF) for i in range(2)]
            d_t = p.toks(2, "dft")
            d_l = [p.lane("dft%d" % i) for i in range(2)]
            uT = [sbuf(S, "uT%d" % i, [128, 8, 2, KB], BF) for i in range(2)]
            u_t = p.toks(2, "uT")
            xr = [sbuf(S, "l1_xr%d" % i, [128, 8, KB], F32) for i in range(2)]
            xr_t = p.toks(2, "l1_xr")
            xr_l = [p.lane("l1_xr_%d" % i) for i in range(2)]
            xo_l = [p.lane("l1_xo_%d" % i) for i in range(2)]
            urot = Rot([banks[0], banks[1], banks[2]])
            prot = Rot([banks[3], banks[4]])
            kblocks = [(i * KB, KB) for i in range(2048 // KB)] + [(2048, 2)]
            for kb, (k0, w) in enumerate(kblocks):
                s = kb % 2
                p.dma("sp", dct[s][:, :, 0:w], dftc.rearrange("(c p) k -> p c k", p=128)[:, :, k0:k0 + w], writes=[d_t[s]], lane=d_l[s])
                p.dma("sp", dst_[s][:, :, 0:w], dfts.rearrange("(c p) k -> p c k", p=128)[:, :, k0:k0 + w], writes=[d_t[s]], lane=d_l[s])
                if w == KB:
                    p.op("sp", (lambda e, s=s, k0=k0: e.dma_start(
                        out=xr[s][:], in_=fm(fsrc)[:, :, bass.ds(core_par(e) * 2048 + k0, KB)])),
                        writes=[xr_t[s]], lane=xr_l[s])
                else:
                    p.op("sp", (lambda e, s=s: e.dma_start(
                        out=xr[s][:, :, 0:1], in_=fm(fsrc)[:, :, bass.ds(core_par(e) * 2047, 1)], allow_slow_non_contiguous=True)),
                        writes=[xr_t[s]], lane=xr_l[s])
                    p.op("sp", (lambda e, s=s: e.dma_start(
                        out=xr[s][:, :, 1:2], in_=fm(fsrc)[:, :, bass.ds(core_par(e) * 2047 + 2048, 1)], allow_slow_non_contiguous=True)),
                        writes=[xr_t[s]], lane=xr_l[s])
                for jc in range(8):
                    bk, bt = urot.next()
                    for cs, tab in enumerate((dct, dst_)):
                        for tc_ in range(32):
                            p.op("pe", (lambda e, bk=bk, cs=cs, tab=tab, tc_=tc_, jc=jc, s=s, w=w: e.matmul(
                                bk[:, cs * KB:cs * KB + w], Htok[:, tc_, jc * 128:(jc + 1) * 128], tab[s][:, tc_, 0:w],
                                start=(tc_ == 0), stop=(tc_ == 31))),
                                reads=[t_H, d_t[s]], writes=[bt])
                    p.op("act", (lambda e, bk=bk, jc=jc, s=s, w=w: e.activation(
                        out=uT[s][:, jc, :, 0:w], in_=bk[:, 0:2 * KB].rearrange("p (c k) -> p c k", c=2)[:, :, 0:w], func=AF.Copy)),
                        reads=[bt], writes=[u_t[s]])
                for nch in range(8):
                    bk, bt = prot.next()
                    for cs in range(2):
                        for jc in range(8):
                            p.op("pe", (lambda e, bk=bk, cs=cs, jc=jc, nch=nch, s=s, w=w: e.matmul(
                                bk[:, 0:w], wcs[:, cs, jc, nch * 128:(nch + 1) * 128], uT[s][:, jc, cs, 0:w],
                                start=(cs == 0 and jc == 0), stop=(cs == 1 and jc == 7))),
                                reads=[t_wcs, u_t[s]], writes=[bt])
                    p.op("dve", (lambda e, bk=bk, nch=nch, s=s, w=w: e.scalar_tensor_tensor(
                        out=xr[s][:, nch, 0:w], in0=bk[:, 0:w], scalar=mcol(1, 16 + nch), op0=ALU.mult,
                        in1=xr[s][:, nch, 0:w], op1=ALU.add)),
                        reads=[bt, tmod], writes=[xr_t[s]])
                if w == KB:
                    p.dma("sp", fm(x3T)[:, :, 1 + k0:1 + k0 + KB], xr[s][:], reads=[xr_t[s]], lane=xo_l[s])
                else:
                    p.op("sp", (lambda e, s=s: e.dma_start(out=fm(x3T)[:, :, 0:1], in_=xr[s][:, :, 0:1], allow_slow_non_contiguous=True)),
                         reads=[xr_t[s]], lane=xo_l[s])
                    p.op("sp", (lambda e, s=s: e.dma_start(out=fm(x3T)[:, :, 2049:2050], in_=xr[s][:, :, 1:2], allow_slow_non_contiguous=True)),
                         reads=[xr_t[s]], lane=xo_l[s])
        p.barrier()
        if stop_after == "FOUR":
            return finish()

        ffn_layer(1, x3T, yT, True, local=True)
        return finish()

def prep_core_inputs(b, x, c, ctx, c_ctx, mod_w, mod_b, norm1_g, norm2_g, attn_w_in, attn_w_out, q_norm_g,
                     k_norm_g, na_rpb, fourier_w_out, ffn_w_up, ffn_conv_w, ffn_conv_b, ffn_w_down, final_g, shared):
    f32 = np.float32
    m = dict(shared)
    m["xT"] = np.ascontiguousarray(x[b].T.astype(f32))
    m["ctxT"] = np.ascontiguousarray(ctx[b].T.astype(f32))
    ccv = np.stack([c[b], c_ctx], axis=-1).astype(f32)
    m["cc"] = np.ascontiguousarray(ccv.reshape(8, 128, 2).transpose(1, 0, 2))
    return m


_DFT_CORE = {}


def dft_core_tables(s):
    if s not in _DFT_CORE:
        cst = consts()
        base = s * 2048
        cols = np.concatenate([np.arange(base, base + 2048), [max(base - 1, 0)], [min(base + 2048, L - 1)]])
        hm = np.zeros((128, 2), np.float32)
        hm[:, 0] = 0.0 if s == 0 else 1.0
        hm[:, 1] = 1.0 if s == 0 else 0.0
        _DFT_CORE[s] = dict(dftc=np.ascontiguousarray(cst["dftc_full"][:, cols]),
                            dfts=np.ascontiguousarray(cst["dfts_full"][:, cols]), hmask=hm)
    return _DFT_CORE[s]


def prep_shared(mod_w, mod_b, norm1_g, norm2_g, attn_w_in, attn_w_out, q_norm_g, k_norm_g, na_rpb,
                fourier_w_out, ffn_w_up, ffn_conv_w, ffn_conv_b, ffn_w_down, final_g):
    f32 = np.float32
    sh = {}
    sh["mod_w"] = np.ascontiguousarray(mod_w.astype(f32))
    sh["mod_b2"] = np.ascontiguousarray(np.repeat(mod_b.astype(f32)[:, None, :], 2, axis=1))
    ngs = np.stack([norm1_g[0], norm2_g[0], norm1_g[1], norm2_g[1], final_g], axis=0).astype(f32)
    sh["ng"] = np.ascontiguousarray(ngs.reshape(5, 8, 128).transpose(2, 0, 1))
    sh["w_in"] = np.ascontiguousarray(attn_w_in[0].astype(f32))
    sh["w_out"] = np.ascontiguousarray(attn_w_out[0].astype(f32))
    sh["qkg"] = np.ascontiguousarray(np.stack([q_norm_g[0], k_norm_g[0]], axis=-1).astype(f32))
    sh["btiles"] = build_bias_tiles(na_rpb[0].astype(f32))
    sh["fw"] = np.ascontiguousarray(fourier_w_out[0].astype(f32))
    wu = ffn_w_up.astype(f32).reshape(2, 8, 128, 2, NPAIR, 128)
    sh["w_up"] = np.ascontiguousarray(wu.transpose(0, 4, 2, 1, 3, 5).reshape(2, NPAIR, 128, 8 * 256))
    cwb = np.concatenate([ffn_conv_w.astype(f32), ffn_conv_b.astype(f32)[:, None, :]], axis=1)
    sh["cw"] = np.ascontiguousarray(cwb.reshape(2, 4, 44, 128).transpose(3, 0, 2, 1))
    wd = ffn_w_down.astype(f32).reshape(2, NPAIR, 128, 8, 128)
    sh["w_down"] = np.ascontiguousarray(wd.transpose(0, 3, 2, 1, 4).reshape(2, 8, 128, NPAIR * 128))
    sh.update({k: v for k, v in consts().items() if not k.endswith('_full')})
    return sh


def kernel(x, c, ctx, c_ctx, mod_w, mod_b, norm1_g, norm2_g, attn_w_in, attn_w_out, q_norm_g,
           k_norm_g, na_rpb, fourier_w_out, ffn_w_up, ffn_conv_w, ffn_conv_b, ffn_w_down, final_g):
    args = [np.asarray(a) for a in (x, c, ctx, c_ctx, mod_w, mod_b, norm1_g, norm2_g, attn_w_in, attn_w_out,
                                    q_norm_g, k_norm_g, na_rpb, fourier_w_out, ffn_w_up, ffn_conv_w,
                                    ffn_conv_b, ffn_w_down, final_g)]
    shared = prep_shared(*args[4:])
    in_maps = [prep_core_inputs(core // 2, *args, shared) for core in range(8)]
    for core in range(8):
        in_maps[core].update(dft_core_tables(core % 2))
    nc = build()
    res = run_bass_kernel_spmd(nc, in_maps, core_ids=list(range(8)))
    out = np.empty((4, L, D), np.float32)
    for b in range(4):
        y0 = res.results[2 * b]["yT"]
        y1 = res.results[2 * b + 1]["yT"]
        out[b, :2048] = y0.T
        out[b, 2048:] = y1.T
    return out
```

```python
import numpy as np
import ml_dtypes
from contextlib import ExitStack
import concourse.bass as bass
import concourse.mybir as mybir
from concourse.bass_utils import run_bass_kernel_spmd

F32 = mybir.dt.float32
BF = mybir.dt.bfloat16
AF = mybir.ActivationFunctionType
ALU = mybir.AluOpType

D = 1024
L = 4096
CTX = 256
DFF = 2816
NPAIR = DFF // 128
GW = 64
EPS = 1e-6
NEG = -30000.0
SCALE_A = 128 ** -0.5
SCALE_B = 64 ** -0.5


class Tok:
    __slots__ = ("w", "r", "name")

    def __init__(self, name=""):
        self.w = None
        self.r = []
        self.name = name


class Lane:
    def __init__(self, sem):
        self.sem = sem
        self.count = 0
        self.last = None


class Prog:
    ENGS = ("pe", "act", "dve", "pool", "sp")

    def __init__(self, nc, stack):
        self.nc = nc
        self.stack = stack
        self.ops = []
        self.lanes = []
        self.esem = {e: stack.enter_context(nc.semaphore("es_" + e)) for e in self.ENGS}
        self.last_on = {e: None for e in self.ENGS}

    def tok(self, name=""):
        return Tok(name)

    def toks(self, n, name=""):
        return [Tok(name + str(i)) for i in range(n)]

    def lane(self, name):
        ln = Lane(self.stack.enter_context(self.nc.semaphore("ln%d_%s" % (len(self.lanes), name))))
        self.lanes.append(ln)
        return ln

    def op(self, eng, fn, reads=(), writes=(), lane=None, extra_deps=()):
        i = len(self.ops)
        deps = set(extra_deps)
        for t in reads:
            if t.w is not None:
                deps.add(t.w)
        for t in writes:
            if t.w is not None:
                deps.add(t.w)
            last = {}
            for j in t.r:
                oj = self.ops[j]
                if oj["lane"] is not None:
                    deps.add(j)
                else:
                    last[oj["eng"]] = max(last.get(oj["eng"], -1), j)
            deps.update(last.values())
        for t in reads:
            t.r.append(i)
        for t in writes:
            t.w = i
            t.r = []
        deps.discard(i)
        self.ops.append(dict(eng=eng, fn=fn, deps=deps, lane=lane, signal=False, sigval=None))
        self.last_on[eng] = i
        if lane is not None:
            lane.last = i
        return i

    def dma(self, q, out, in_, reads=(), writes=(), lane=None):
        assert lane is not None
        return self.op(q, lambda e: e.dma_start(out=out, in_=in_), reads, writes, lane=lane)

    def barrier(self):
        deps = set(v for v in self.last_on.values() if v is not None)
        deps.update(ln.last for ln in self.lanes if ln.last is not None)
        for e in self.ENGS:
            self.op(e, lambda eng: eng.nop(), extra_deps=deps)

    def emit(self, final_lanes=()):
        nc = self.nc
        ops = self.ops
        for i, o in enumerate(ops):
            for j in o["deps"]:
                d = ops[j]
                if d["lane"] is not None:
                    continue
                if d["eng"] == "pe" and o["eng"] == "pe" and o["lane"] is None:
                    continue
                d["signal"] = True
        cnt = {e: 0 for e in self.ENGS}
        for o in ops:
            if o["lane"] is not None:
                o["lane"].count += 16
                o["sigval"] = o["lane"].count
            elif o["signal"]:
                cnt[o["eng"]] += 1
                o["sigval"] = cnt[o["eng"]]
        per_eng = {e: [] for e in self.ENGS}
        for i, o in enumerate(ops):
            per_eng[o["eng"]].append(i)

        def run(ename, eng):
            waited = {}
            for i in per_eng[ename]:
                o = ops[i]
                need = {}
                for j in o["deps"]:
                    d = ops[j]
                    if d["lane"] is not None:
                        sem = d["lane"].sem
                    else:
                        if d["eng"] == "pe" and ename == "pe" and o["lane"] is None:
                            continue
                        sem = self.esem[d["eng"]]
                    k = id(sem)
                    if need.get(k, (None, 0))[1] < d["sigval"]:
                        need[k] = (sem, d["sigval"])
                for k, (sem, v) in need.items():
                    if waited.get(k, 0) < v:
                        eng.wait_ge(sem, v)
                        waited[k] = v
                ins = o["fn"](eng)
                if o["lane"] is not None:
                    ins.then_inc(o["lane"].sem, 16)
                elif o["signal"]:
                    ins.then_inc(self.esem[ename], 1)
            if ename == "sp":
                for ln in final_lanes:
                    if ln.count:
                        eng.wait_ge(ln.sem, ln.count)

        with nc.Block() as block:
            @block.tensor
            def _(e):
                run("pe", e)

            @block.scalar
            def _(e):
                run("act", e)

            @block.vector
            def _(e):
                run("dve", e)

            @block.gpsimd
            def _(e):
                run("pool", e)

            @block.sync
            def _(e):
                run("sp", e)


class Rot:
    def __init__(self, items):
        self.items = items
        self.i = 0

    def next(self):
        it = self.items[self.i % len(self.items)]
        self.i += 1
        return it


STRIPS = (("RE", (-4, -2, 0, 2)), ("RO", (-5, -3, -1, 1, 3)), ("UE", tuple(range(-6, 7, 2))), ("UO", tuple(range(-7, 6, 2))))
NBT = sum(len(v) for _, v in STRIPS)


def na_row_plan():
    off = {}
    o = 0
    for name, v in STRIPS:
        off[name] = o
        o += len(v)
    plan = []
    for r in range(64):
        r0 = min(max(r - 4, 0), 56)
        m_lo = r0 // 2
        m_hi = (r0 + 7) // 2
        nw = m_hi - m_lo + 1
        dr0 = 2 * m_lo - r
        if 4 <= r <= 60:
            if r % 2 == 0:
                assert dr0 == -4 and nw == 4
                k = off["RE"]
            else:
                assert dr0 == -5 and nw == 5
                k = off["RO"]
        else:
            assert nw == 4
            if dr0 % 2 == 0:
                k = off["UE"] + (dr0 + 6) // 2
            else:
                k = off["UO"] + (dr0 + 7) // 2
        plan.append((m_lo, nw, k))
    return plan


def build_bias_tiles(rpb):
    out = np.full((8, NBT, 128, 64), NEG, np.float32)
    c = np.arange(64)
    c0 = np.clip(c - 8, 0, 48)
    kc = np.arange(64)
    colok = (kc[:, None] >= c0[None, :]) & (kc[:, None] < c0[None, :] + 16)
    dcc = np.clip(kc[:, None] - c[None, :] + 15, 0, 30)
    k = 0
    for name, drs in STRIPS:
        for dr0 in drs:
            for e in (0, 1):
                dr = dr0 + e
                ok = (-4 <= dr <= 3) if name[0] == "R" else (-7 <= dr <= 7)
                if ok:
                    vals = rpb[:, dr + 7, :][:, dcc]
                    out[:, k, e * 64:(e + 1) * 64, :] = np.where(colok[None], vals, np.float32(NEG))
            k += 1
    out = out.reshape(8 * NBT, 128, 64).transpose(1, 0, 2)
    return np.ascontiguousarray(out)


def rope_tables():
    t = np.arange(L)
    row = (t // GW).astype(np.float32)
    col = (t % GW).astype(np.float32)
    freqs = (10000.0 ** (-np.arange(32, dtype=np.float32) / 32)).astype(np.float32)
    ang = np.concatenate([row[:, None] * freqs, col[:, None] * freqs], axis=-1)
    cos = np.cos(ang).astype(np.float32)
    sin = np.sin(ang).astype(np.float32)
    COS = np.repeat(cos, 2, axis=1).T
    SINS = np.repeat(sin, 2, axis=1).T.copy()
    SINS[0::2] *= -1.0
    return (np.ascontiguousarray(COS).astype(ml_dtypes.bfloat16),
            np.ascontiguousarray(SINS).astype(ml_dtypes.bfloat16))


def dft_tables():
    t = np.arange(L, dtype=np.int64)
    ph = (np.outer(t, t) % L).astype(np.float64) * (2 * np.pi / L)
    CL = (np.cos(ph) / 64.0).astype(ml_dtypes.bfloat16)
    SL = (np.sin(ph) / 64.0).astype(ml_dtypes.bfloat16)
    j = np.arange(256, dtype=np.int64)
    ph2 = (np.outer(j, j) % 256).astype(np.float64) * (2 * np.pi / 256)
    C2 = (np.cos(ph2) / 16.0).astype(ml_dtypes.bfloat16)
    S2n = (-np.sin(ph2) / 16.0).astype(ml_dtypes.bfloat16)
    return CL, SL, C2, S2n


_CONST = {}


def consts():
    if not _CONST:
        COS, SINS = rope_tables()
        CL, SL, C2, S2n = dft_tables()
        ident = np.eye(128, dtype=np.float32)
        rm = np.zeros((128, 128), np.float32)
        for d in range(128):
            rm[d ^ 1, d] = 1.0
        _CONST.update(
            rope_cos=COS, rope_sin=SINS, dftc_full=CL, dfts_full=SL, c256=C2, s256n=S2n,
            ident_bf=ident.astype(ml_dtypes.bfloat16),
            ones_bf=np.ones((128, 128), np.float32).astype(ml_dtypes.bfloat16),
            rm_bf=rm.astype(ml_dtypes.bfloat16),
            ident32=ident.copy(),
        )
    return _CONST


def build(debug=(), stop_after=None, skip_att=False, ffn0_src=None, four_src=None):
    plan = na_row_plan()
    NTB = NBT
    nc = bass.Bass("TRN2", target_bir_lowering=False)

    def din(name, shape, dt=F32):
        return nc.dram_tensor(name, list(shape), dt, kind="ExternalInput").ap()

    xT = din("xT", [D, L])
    ctxT = din("ctxT", [D, CTX])
    cc = din("cc", [128, 8, 2])
    mod_w = din("mod_w", [2, D, 6 * D])
    mod_b2 = din("mod_b2", [2, 2, 6 * D])
    ng = din("ng", [128, 5, 8])
    w_in = din("w_in", [D, 2560])
    w_out = din("w_out", [D, D])
    qkg = din("qkg", [128, 2])
    btiles = din("btiles", [128, NTB * 8, 64])
    rope_cos = din("rope_cos", [128, L], BF)
    rope_sin = din("rope_sin", [128, L], BF)
    fw = din("fw", [D, D])
    w_up = din("w_up", [2, NPAIR, 128, 8 * 256])
    cw = din("cw", [128, 2, 44, 4])
    w_down = din("w_down", [2, 8, 128, NPAIR * 128])
    NLOC = 2048
    dftc = din("dftc", [L, NLOC + 2], BF)
    dfts = din("dfts", [L, NLOC + 2], BF)
    hmask = din("hmask", [128, 2])
    c256 = din("c256", [256, 256], BF)
    s256n = din("s256n", [256, 256], BF)
    ident_bf_d = din("ident_bf", [128, 128], BF)
    ones_bf_d = din("ones_bf", [128, 128], BF)
    rm_bf_d = din("rm_bf", [128, 128], BF)
    ident32_d = din("ident32", [128, 128])

    yT = nc.dram_tensor("yT", [D, 2048], F32, kind="ExternalOutput").ap()
    dbg = {}
    for name, shape in debug:
        dbg[name] = nc.dram_tensor("dbg_" + name, list(shape), F32, kind="ExternalOutput").ap()

    x1T = nc.dram_tensor("x1T_scr", [D, L], F32, kind="Internal").ap()
    x2T = nc.dram_tensor("x2T_scr", [D, L], F32, kind="Internal").ap()
    x3T = nc.dram_tensor("x3T_scr", [D, 2048 + 2], F32, kind="Internal").ap()
    wupb = nc.dram_tensor("wupb_scr", [2, NPAIR, 128, 8 * 256], BF, kind="Internal").ap()
    wdb = nc.dram_tensor("wdb_scr", [2, 8, 128, NPAIR * 128], BF, kind="Internal").ap()

    def fm(ap):
        return ap.rearrange("(c p) t -> p c t", p=128)

    with ExitStack() as G:
        p = Prog(nc, G)
        lane_out = p.lane("out")
        lane_c = p.lane("const")
        lane_scr = p.lane("scr")
        lane_r = p.lane("rope")

        ucnt = [0]

        def sbuf(st, name, shape, dt):
            ucnt[0] += 1
            return st.enter_context(nc.sbuf_tensor("s%d_%s" % (ucnt[0], name), list(shape), dt))

        banks = []
        for i in range(8):
            t = G.enter_context(nc.psum_tensor("psb%d" % i, [128, 512], F32))
            banks.append((t, p.tok("psb%d" % i)))

        ident_bf = sbuf(G, "ident_bf", [128, 128], BF)
        ones_bf = sbuf(G, "ones_bf", [128, 128], BF)
        rm_bf = sbuf(G, "rm_bf", [128, 128], BF)
        ident32 = sbuf(G, "ident32", [128, 128], F32)
        ng_sb = sbuf(G, "ng_sb", [128, 5, 8], F32)
        qkg_sb = sbuf(G, "qkg_sb", [128, 2], F32)
        cw_sb = sbuf(G, "cw_sb", [128, 2, 44, 4], F32)
        cc_sb = sbuf(G, "cc_sb", [128, 8, 2], F32)
        hm_sb = sbuf(G, "hm_sb", [128, 2], F32)
        sc_sb = sbuf(G, "sc_sb", [128, 8, 2], F32)
        modT = sbuf(G, "modT", [128, 2, 48, 2], F32)
        gs = sbuf(G, "gs", [128, 2, 3, 8], F32)
        tconst = p.tok("const")
        for dst, src in ((ident_bf, ident_bf_d), (ones_bf, ones_bf_d), (rm_bf, rm_bf_d), (ident32, ident32_d),
                         (ng_sb, ng), (qkg_sb, qkg), (cw_sb, cw), (cc_sb, cc), (hm_sb, hmask)):
            p.dma("sp", dst[:], src, writes=[tconst], lane=lane_c)

        tmod = p.tok("modT")
        with ExitStack() as S:
            mrow = sbuf(S, "mrow", [2, 6 * D], F32)
            mb_sb = sbuf(S, "mb_sb", [2, 6 * D], F32)
            stg = [sbuf(S, "mw_stg%d" % i, [128, 8, 512], F32) for i in range(2)]
            stg_t = p.toks(2, "mwstg")
            stg_l = [p.lane("mw%d" % i) for i in range(2)]
            tmrow = p.tok("mrow")
            tmb = p.tok("mb")
            lane_mb = p.lane("mb")
            p.op("act", lambda e: e.activation(out=sc_sb[:], in_=cc_sb[:], func=AF.Silu), reads=[tconst], writes=[tconst])
            it = 0
            for l in range(2):
                p.dma("sp", mb_sb[:], mod_b2[l], writes=[tmb], lane=lane_mb)
                for blk in range(12):
                    s = it % 2
                    it += 1
                    src = mod_w[l].rearrange("(c p) n -> p c n", p=128)[:, :, blk * 512:(blk + 1) * 512]
                    p.dma("sp", stg[s][:], src, writes=[stg_t[s]], lane=stg_l[s])
                    bk, bt = banks[s]
                    for k in range(8):
                        p.op("pe", (lambda e, s=s, k=k, bk=bk: e.matmul(bk[0:2, :], sc_sb[:, k, :], stg[s][:, k, :],
                                                                         start=(k == 0), stop=(k == 7))),
                             reads=[stg_t[s], tconst], writes=[bt])
                    p.op("dve", (lambda e, bk=bk, blk=blk: e.tensor_tensor(
                        out=mrow[:, blk * 512:(blk + 1) * 512], in0=bk[0:2, :],
                        in1=mb_sb[:, blk * 512:(blk + 1) * 512], op=ALU.add)),
                        reads=[bt, tmb], writes=[tmrow])
                bk, bt = banks[2]
                for ch in range(48):
                    p.op("pe", (lambda e, ch=ch, bk=bk: e.matmul(bk[:, ch * 2:ch * 2 + 2], mrow[0:2, ch * 128:(ch + 1) * 128],
                                                                ident32[0:2, 0:2], start=True, stop=True)),
                         reads=[tmrow, tconst], writes=[bt])
                p.op("dve", (lambda e, l=l, bk=bk: e.tensor_copy(out=modT[:, l].rearrange("p a b -> p (a b)"), in_=bk[:, 0:96])),
                     reads=[bt], writes=[tmod])
            for l in range(2):
                for (i, chunk0, j, ngi) in ((0, 8, 0, 2 * l), (1, 32, 0, 2 * l + 1), (2, 8, 1, 2 * l)):
                    p.op("dve", (lambda e, l=l, i=i, chunk0=chunk0, j=j, ngi=ngi: e.scalar_tensor_tensor(
                        out=gs[:, l, i, :], in0=modT[:, l, chunk0:chunk0 + 8, j], scalar=1.0, op0=ALU.add,
                        in1=ng_sb[:, ngi, :], op1=ALU.mult)),
                        reads=[tmod, tconst], writes=[tmod])
        p.barrier()

        _par = {}

        def core_par(e):
            if "v" not in _par:
                _par["v"] = e.partition_id() % 2
            return _par["v"]

        def mcol(l, chunk, j=0):
            return modT[:, l, chunk, j:j + 1]

        def norm_mod(xt, xoff, n, blocks, gcols, bcols, hout, hoff, t_x, t_h, ss_banks, tmp):
            sqc, lnv, rstd, xnc, t_sq, t_r, t_xn = tmp
            for c in range(8):
                i = c % 2
                p.op("act", (lambda e, c=c, i=i: e.activation(out=sqc[i][:, 0:n], in_=xt[:, c, xoff:xoff + n], func=AF.Square)),
                     reads=[t_x], writes=[t_sq[i]])
                for bi, (c0, w) in enumerate(blocks):
                    bk, bt = ss_banks[bi]
                    p.op("pe", (lambda e, c=c, i=i, bk=bk, c0=c0, w=w: e.matmul(bk[:, 0:w], ones_bf[:], sqc[i][:, c0:c0 + w],
                                                                               start=(c == 0), stop=(c == 7))),
                         reads=[t_sq[i], tconst], writes=[bt])
            for bi, (c0, w) in enumerate(blocks):
                bk, bt = ss_banks[bi]
                p.op("act", (lambda e, bk=bk, c0=c0, w=w: e.activation(out=lnv[:, c0:c0 + w], in_=bk[:, 0:w], func=AF.Ln,
                                                                       scale=1.0 / D, bias=eps_sb[:, 0:1])),
                     reads=[bt, tconst], writes=[t_r])
            p.op("act", lambda e: e.activation(out=rstd[:, 0:n], in_=lnv[:, 0:n], func=AF.Exp, scale=-0.5), reads=[t_r], writes=[t_r])
            for c in range(8):
                i = c % 2
                if bcols is not None:
                    p.op("dve", (lambda e, c=c, i=i: e.scalar_tensor_tensor(
                        out=xnc[i][:, 0:n], in0=xt[:, c, xoff:xoff + n], scalar=gcols(c), op0=ALU.mult,
                        in1=rstd[:, 0:n], op1=ALU.mult)),
                        reads=[t_x, t_r, tmod, tconst], writes=[t_xn[i]])
                    p.op("act", (lambda e, c=c, i=i: e.activation(out=hout[:, c, hoff:hoff + n], in_=xnc[i][:, 0:n],
                                                                  func=AF.Identity, bias=bcols(c))),
                         reads=[t_xn[i], tmod, tconst], writes=[t_h])
                else:
                    p.op("dve", (lambda e, c=c: e.scalar_tensor_tensor(
                        out=hout[:, c, hoff:hoff + n], in0=xt[:, c, xoff:xoff + n], scalar=gcols(c), op0=ALU.mult,
                        in1=rstd[:, 0:n], op1=ALU.mult)),
                        reads=[t_x, t_r, tmod, tconst], writes=[t_h])

        def make_norm_tmps(S_, tag, n):
            sqc = [sbuf(S_, "%s_sq%d" % (tag, i), [128, n], BF) for i in range(2)]
            lnv = sbuf(S_, tag + "_lnv", [128, n], F32)
            rstd = sbuf(S_, tag + "_rstd", [128, n], F32)
            xnc = [sbuf(S_, "%s_xn%d" % (tag, i), [128, n], F32) for i in range(2)]
            return (sqc, lnv, rstd, xnc, p.toks(2, tag + "sq"), p.tok(tag + "r"), p.toks(2, tag + "xn"))

        def load_cols(dst, t_dst, src2d, ranges, tag):
            with ExitStack() as S2:
                stg = [sbuf(S2, "%s_stg%d" % (tag, i), [128, 8, 512], F32) for i in range(2)]
                stg_t = p.toks(2, tag + "stg")
                stg_l = [p.lane("%s_l%d" % (tag, i)) for i in range(2)]
                it = 0
                d0 = 0
                for (c0, n) in ranges:
                    for b0 in range(0, n, 512):
                        w = min(512, n - b0)
                        s = it % 2
                        it += 1
                        src = src2d.rearrange("(c p) n -> p c n", p=128)[:, :, c0 + b0:c0 + b0 + w]
                        p.dma("sp", stg[s][:, :, 0:w], src, writes=[stg_t[s]], lane=stg_l[s])
                        p.op("pool", (lambda e, s=s, d0=d0, w=w: e.tensor_copy(out=dst[:, :, d0:d0 + w], in_=stg[s][:, :, 0:w])),
                             reads=[stg_t[s]], writes=[t_dst])
                        d0 += w
            p.barrier()

        eps_sb = sbuf(G, "eps_sb", [128, 1], F32)
        p.op("pool", lambda e: e.memset(eps_sb[:], EPS), writes=[tconst])

        def load_cast(S_, src_ap, stage, stage_t, stage_l, dst_ap, dst_t, shape_ok=True):
            p.dma("sp", stage, src_ap, writes=[stage_t], lane=stage_l)
            p.op("pool", lambda e: e.tensor_copy(out=dst_ap, in_=stage), reads=[stage_t], writes=[dst_t])

        def finish():
            p.emit(final_lanes=[lane_out])
            return nc

        def attention_phase():
            A = ExitStack()
            CKbT = sbuf(A, "CKbT", [128, 4, CTX], BF)
            CVb = sbuf(A, "CVb", [128, 2, 512], BF)
            OaT = sbuf(A, "OaT", [128, 4, L], BF)
            t_ckb = p.tok("CKb")
            t_oa = p.toks(8, "OaT")
            A1 = ExitStack()
            KaT = sbuf(A1, "KaT", [128, 2, L + CTX], BF)
            Va = sbuf(A1, "Va", [128, 34, 256], BF)
            t_ka = p.tok("KaT")
            t_va = p.tok("Va")

            QA0, QB0, KA0, VA0, KB0, VB0 = 0, 512, 1024, 1280, 1536, 2048

            def qk_head(ps_bank, n, gcol, dst_ap, tok_dst, cos_ap, sin_ap, t_rope, scale, rots, tmps):
                (pk, pt) = ps_bank
                ss_b, ss_t = rots["ss"].next()
                qb, qsq, lnv, rstd, t1, t2, t_q, t_r, t_1, t_2 = tmps.next()
                rope = cos_ap is not None
                p.op("act", lambda e: e.activation(out=qb[:, 0:n], in_=pk[:, 0:n], func=AF.Copy, scale=gcol),
                     reads=[pt, tconst], writes=[t_q])
                p.op("act", lambda e: e.activation(out=qsq[:, 0:n], in_=pk[:, 0:n], func=AF.Square), reads=[pt], writes=[t_q])
                p.op("pe", lambda e: e.matmul(ss_b[:, 0:n], ones_bf[:], qsq[:, 0:n], start=True, stop=True),
                     reads=[t_q, tconst], writes=[ss_t])
                if rope:
                    qr_b, qr_t = rots["qr"].next()
                    p.op("pe", lambda e: e.matmul(qr_b[:, 0:n], rm_bf[:], qb[:, 0:n], start=True, stop=True),
                         reads=[t_q, tconst], writes=[qr_t])
                p.op("act", lambda e: e.activation(out=lnv[:, 0:n], in_=ss_b[:, 0:n], func=AF.Ln, scale=1.0 / 128, bias=eps_sb[:, 0:1]),
                     reads=[ss_t, tconst], writes=[t_r])
                p.op("act", lambda e: e.activation(out=rstd[:, 0:n], in_=lnv[:, 0:n], func=AF.Exp, scale=-0.5), reads=[t_r], writes=[t_r])
                if rope:
                    p.op("pool", lambda e: e.tensor_tensor(out=t1[:, 0:n], in0=qb[:, 0:n], in1=cos_ap, op=ALU.mult),
                         reads=[t_q, t_rope], writes=[t_1])
                    p.op("dve", lambda e: e.tensor_tensor(out=t2[:, 0:n], in0=qr_b[:, 0:n], in1=sin_ap, op=ALU.mult),
                         reads=[qr_t, t_rope], writes=[t_2])
                    p.op("dve", lambda e: e.tensor_tensor(out=t2[:, 0:n], in0=t2[:, 0:n], in1=t1[:, 0:n], op=ALU.add),
                         reads=[t_1, t_2], writes=[t_2])
                    p.op("dve", lambda e: e.scalar_tensor_tensor(out=dst_ap, in0=t2[:, 0:n], scalar=float(scale), op0=ALU.mult,
                                                                 in1=rstd[:, 0:n], op1=ALU.mult),
                         reads=[t_2, t_r], writes=[tok_dst])
                else:
                    p.op("dve", lambda e: e.scalar_tensor_tensor(out=dst_ap, in0=qb[:, 0:n], scalar=float(scale), op0=ALU.mult,
                                                                 in1=rstd[:, 0:n], op1=ALU.mult),
                         reads=[t_q, t_r], writes=[tok_dst])

            def proj_fm(wsb, t_w, htile, t_h, n, col0, bank):
                bk, bt = bank
                for k in range(8):
                    p.op("pe", (lambda e, k=k: e.matmul(bk[:, 0:n], wsb[:, k, col0:col0 + 128], htile[:, k, 0:n],
                                                        start=(k == 0), stop=(k == 7))),
                         reads=[t_h, t_w], writes=[bt])

            def proj_tm(wsb, t_w, htile, t_h, tok0, col0, ncols, bank):
                bk, bt = bank
                for k in range(8):
                    p.op("pe", (lambda e, k=k: e.matmul(bk[:, 0:ncols], htile[:, k, tok0:tok0 + 128],
                                                        wsb[:, k, col0:col0 + ncols], start=(k == 0), stop=(k == 7))),
                         reads=[t_h, t_w], writes=[bt])

            def make_qk_tmps(S_, tag, nbuf=2):
                items = []
                for i in range(nbuf):
                    qb = sbuf(S_, "%s_qb%d" % (tag, i), [128, 512], BF)
                    qsq = sbuf(S_, "%s_qsq%d" % (tag, i), [128, 512], BF)
                    lnv = sbuf(S_, "%s_lnv%d" % (tag, i), [128, 512], F32)
                    rstd = sbuf(S_, "%s_rstd%d" % (tag, i), [128, 512], F32)
                    t1 = sbuf(S_, "%s_t1%d" % (tag, i), [128, 512], F32)
                    t2 = sbuf(S_, "%s_t2%d" % (tag, i), [128, 512], F32)
                    items.append((qb, qsq, lnv, rstd, t1, t2) + tuple(p.toks(4, tag + "tmp")))
                return Rot(items)

            with ExitStack() as S:
                wsb = sbuf(S, "pk_w", [128, 8, 1536], BF)
                t_w = p.tok("pk_w")
                load_cols(wsb, t_w, w_in, [(KA0, 1536)], "pkw")
                cos_sb = sbuf(S, "pk_cos", [128, L], BF)
                sin_sb = sbuf(S, "pk_sin", [128, L], BF)
                t_rope = p.tok("pk_rope")
                p.dma("sp", cos_sb[:], rope_cos, writes=[t_rope], lane=lane_r)
                p.dma("sp", sin_sb[:], rope_sin, writes=[t_rope], lane=lane_r)
                xt = [sbuf(S, "pk_xt%d" % i, [128, 8, 512], F32) for i in range(2)]
                xt_t = p.toks(2, "pk_xt")
                xt_l = [p.lane("pk_xt%d" % i) for i in range(2)]
                ht = [sbuf(S, "pk_ht%d" % i, [128, 8, 512], BF) for i in range(2)]
                ht_t = p.toks(2, "pk_ht")
                ntmp = make_norm_tmps(S, "pk_n", 512)
                qtmps = make_qk_tmps(S, "pk_q")
                rots = dict(ss=Rot([banks[3], banks[4]]), qr=Rot([banks[5], banks[6]]))
                prj = Rot([banks[1], banks[2]])
                for ti in range(9):
                    s = ti % 2
                    is_ctx = (ti == 8)
                    n = CTX if is_ctx else 512
                    tok0 = ti * 512
                    src = fm(ctxT) if is_ctx else fm(xT)[:, :, tok0:tok0 + 512]
                    p.dma("sp", xt[s][:, :, 0:n], src, writes=[xt_t[s]], lane=xt_l[s])
                    li = 2 if is_ctx else 0
                    jj = 1 if is_ctx else 0
                    norm_mod(xt[s], 0, n, [(0, n)], lambda c, li=li: gs[:, 0, li, c:c + 1],
                             lambda c, jj=jj: mcol(0, c, jj), ht[s], 0, xt_t[s], ht_t[s], [banks[0]], ntmp)
                    for kv in range(2):
                        bank = prj.next()
                        proj_fm(wsb, t_w, ht[s], ht_t[s], n, kv * 128, bank)
                        if is_ctx:
                            qk_head(bank, n, qkg_sb[:, 1:2], KaT[:, kv, tok0:tok0 + n], t_ka, None, None, None,
                                    1.0, rots, qtmps)
                        else:
                            qk_head(bank, n, qkg_sb[:, 1:2], KaT[:, kv, tok0:tok0 + n], t_ka,
                                    cos_sb[:, tok0:tok0 + n], sin_sb[:, tok0:tok0 + n], t_rope, 1.0, rots, qtmps)
                    for sub in range(n // 128):
                        chunk = ti * 4 + sub
                        proj_tm(wsb, t_w, ht[s], ht_t[s], sub * 128, 256, 256, banks[7])
                        p.op("act", (lambda e, chunk=chunk: e.activation(out=Va[:, chunk, :], in_=banks[7][0][:, 0:256], func=AF.Copy)),
                             reads=[banks[7][1]], writes=[t_va])
                    if is_ctx:
                        for ch in range(4):
                            bank = prj.next()
                            proj_fm(wsb, t_w, ht[s], ht_t[s], n, 512 + ch * 128, bank)
                            p.op("act", (lambda e, ch=ch, bank=bank: e.activation(out=CKbT[:, ch, :], in_=bank[0][:, 0:CTX], func=AF.Copy)),
                                 reads=[bank[1]], writes=[t_ckb])
                        for sub in range(2):
                            proj_tm(wsb, t_w, ht[s], ht_t[s], sub * 128, 1024, 512, banks[7])
                            p.op("act", (lambda e, sub=sub: e.activation(out=CVb[:, sub, :], in_=banks[7][0][:, 0:512], func=AF.Copy)),
                                 reads=[banks[7][1]], writes=[t_ckb])
            p.barrier()

            def dump_bf(name, tens, shape, tk):
                if name in dbg:
                    with ExitStack() as S:
                        tmpf = sbuf(S, "dbgt_" + name, shape, F32)
                        tt = p.tok()
                        p.op("dve", lambda e: e.tensor_copy(out=tmpf[:], in_=tens[:]), reads=tk, writes=[tt])
                        p.dma("sp", dbg[name], tmpf[:], reads=[tt], lane=lane_out)
                        p.barrier()

            dump_bf("KaT", KaT, [128, 2, L + CTX], [t_ka])
            dump_bf("Va", Va, [128, 34, 256], [t_va])
            if "modT" in dbg:
                p.dma("sp", dbg["modT"], modT[:], reads=[tmod], lane=lane_out)

            if stop_after == "PK":
                A1.close()
                A.close()
                return True

            precast_jobs = []
            for l_ in range(2):
                for j_ in range(NPAIR):
                    precast_jobs.append((w_up[l_, j_], wupb[l_, j_], 8 * 256))
                for dc_ in range(8):
                    precast_jobs.append((w_down[l_, dc_], wdb[l_, dc_], NPAIR * 128))
            precast_it = [0]
            for half in range(2):
                q0 = half * 2048
                with ExitStack() as H:
                    QaT = sbuf(H, "QaT", [128, 4, 2048], BF)
                    t_qa = p.tok("QaT")
                    with ExitStack() as S:
                        wsb = sbuf(S, "gq_w", [128, 8, 512], BF)
                        t_w = p.tok("gq_w")
                        load_cols(wsb, t_w, w_in, [(QA0, 512)], "gqw%d" % half)
                        cos_sb = sbuf(S, "gq_cos", [128, 2048], BF)
                        sin_sb = sbuf(S, "gq_sin", [128, 2048], BF)
                        t_rope = p.tok("gq_rope")
                        p.dma("sp", cos_sb[:], rope_cos[:, q0:q0 + 2048], writes=[t_rope], lane=lane_r)
                        p.dma("sp", sin_sb[:], rope_sin[:, q0:q0 + 2048], writes=[t_rope], lane=lane_r)
                        xt = [sbuf(S, "gq_xt%d" % i, [128, 8, 512], F32) for i in range(2)]
                        xt_t = p.toks(2, "gq_xt")
                        xt_l = [p.lane("gq_xt%d_%d" % (half, i)) for i in range(2)]
                        ht = [sbuf(S, "gq_ht%d" % i, [128, 8, 512], BF) for i in range(2)]
                        ht_t = p.toks(2, "gq_ht")
                        ntmp = make_norm_tmps(S, "gq_n", 512)
                        qtmps = make_qk_tmps(S, "gq_q")
                        rots = dict(ss=Rot([banks[3], banks[4]]), qr=Rot([banks[5], banks[6]]))
                        prj = Rot([banks[1], banks[2]])
                        for ti in range(4):
                            s = ti % 2
                            tok0 = q0 + ti * 512
                            lt0 = ti * 512
                            p.dma("sp", xt[s][:], fm(xT)[:, :, tok0:tok0 + 512], writes=[xt_t[s]], lane=xt_l[s])
                            norm_mod(xt[s], 0, 512, [(0, 512)], lambda c: gs[:, 0, 0, c:c + 1], lambda c: mcol(0, c, 0),
                                     ht[s], 0, xt_t[s], ht_t[s], [banks[0]], ntmp)
                            for a in range(4):
                                bank = prj.next()
                                proj_fm(wsb, t_w, ht[s], ht_t[s], 512, a * 128, bank)
                                qk_head(bank, 512, qkg_sb[:, 0:1], QaT[:, a, lt0:lt0 + 512], t_qa,
                                        cos_sb[:, lt0:lt0 + 512], sin_sb[:, lt0:lt0 + 512], t_rope, SCALE_A, rots, qtmps)
                    p.barrier()
                    with ExitStack() as S:
                        pts = [sbuf(S, "g_pt%d" % i, [128, 512], BF) for i in range(4)]
                        pt_t = p.toks(4, "g_pt")
                        rl = sbuf(S, "g_rl", [128, 512], F32)
                        t_rl = p.tok("g_rl")
                        pc32 = [sbuf(S, "pc32_%d" % i, [128, NPAIR * 128], F32) for i in range(2)]
                        pc16 = [sbuf(S, "pc16_%d" % i, [128, NPAIR * 128], BF) for i in range(2)]
                        pc32_t = p.toks(2, "pc32")
                        pc16_t = p.toks(2, "pc16")
                        pc32_l = [p.lane("pc32_%d_%d" % (half, i)) for i in range(2)]
                        pc16_l = [p.lane("pc16_%d_%d" % (half, i)) for i in range(2)]
                        srot = Rot([banks[0], banks[1], banks[2], banks[7]])
                        orot = Rot([(banks[3], banks[5]), (banks[4], banks[6])])
                        NKC = 34
                        for a in range(4):
                            kv = a // 2
                            for qt in range(4):
                                for _ in range(2):
                                    if precast_jobs:
                                        (srcw, dstw, nel) = precast_jobs.pop(0)
                                        s = precast_it[0] % 2
                                        precast_it[0] += 1
                                        p.dma("sp", pc32[s][:, 0:nel], srcw, writes=[pc32_t[s]], lane=pc32_l[s])
                                        p.op("pool", (lambda e, s=s, nel=nel: e.tensor_copy(out=pc16[s][:, 0:nel], in_=pc32[s][:, 0:nel])),
                                             reads=[pc32_t[s]], writes=[pc16_t[s]])
                                        p.dma("sp", dstw, pc16[s][:, 0:nel], reads=[pc16_t[s]], lane=pc16_l[s])
                                (ob, ot), (lb, lt) = orot.next()
                                qsl = slice(qt * 512, (qt + 1) * 512)
                                gsl = slice(q0 + qt * 512, q0 + (qt + 1) * 512)

                                def s_mm(kc, a=a, kv=kv, qsl=qsl):
                                    sb_, st_ = srot.next()
                                    p.op("pe", lambda e: e.matmul(sb_[:, :], KaT[:, kv, kc * 128:(kc + 1) * 128], QaT[:, a, qsl],
                                                                  start=True, stop=True),
                                         reads=[t_ka, t_qa], writes=[st_])
                                    return sb_, st_

                                def pv_mm(kc, sbk, stk, kv=kv, ob=ob, ot=ot, lb=lb, lt=lt):
                                    i = kc % 4
                                    p.op("act", lambda e: e.activation(out=pts[i][:], in_=sbk[:, :], func=AF.Exp),
                                         reads=[stk], writes=[pt_t[i]])
                                    p.op("pe", lambda e: e.matmul(ob[:, :], Va[:, kc, kv * 128:(kv + 1) * 128], pts[i][:],
                                                                  start=(kc == 0), stop=(kc == NKC - 1)),
                                         reads=[pt_t[i], t_va], writes=[ot])
                                    p.op("pe", lambda e: e.matmul(lb[:, :], ones_bf[:], pts[i][:],
                                                                  start=(kc == 0), stop=(kc == NKC - 1)),
                                         reads=[pt_t[i], tconst], writes=[lt])

                                q_s = [s_mm(0), s_mm(1)]
                                for kc in range(NKC):
                                    if kc + 2 < NKC:
                                        q_s.append(s_mm(kc + 2))
                                    cur = q_s.pop(0)
                                    pv_mm(kc, cur[0], cur[1])
                                p.op("dve", lambda e, lb=lb: e.reciprocal(out=rl[:], in_=lb[:, :]), reads=[lt], writes=[t_rl])
                                p.op("dve", (lambda e, ob=ob, a=a, gsl=gsl: e.tensor_tensor(out=OaT[:, a, gsl], in0=ob[:, :], in1=rl[:],
                                                                                           op=ALU.mult)),
                                     reads=[ot, t_rl], writes=[t_oa[half * 4 + qt]])
                        assert half == 0 or not precast_jobs
                    p.barrier()
            A1.close()
            p.barrier()
            dump_bf("OaT", OaT, [128, 4, L], t_oa)
            if stop_after == "GQA":
                A.close()
                return True

            for half in range(2):
                q0 = half * 2048
                kf0 = 0 if half == 0 else 1536
                with ExitStack() as H:
                    QbT = sbuf(H, "QbT", [128, 4, 2048], BF)
                    KbT = sbuf(H, "KbT", [128, 4, 2560], BF)
                    Vb = sbuf(H, "Vb", [128, 20, 512], BF)
                    t_qb, t_kb, t_vb = p.toks(3, "na")
                    with ExitStack() as S:
                        wsb = sbuf(S, "na_w", [128, 8, 1536], BF)
                        t_w = p.tok("na_w")
                        load_cols(wsb, t_w, w_in, [(QB0, 512), (KB0, 1024)], "naw%d" % half)
                        xt = [sbuf(S, "na_xt%d" % i, [128, 8, 512], F32) for i in range(2)]
                        xt_t = p.toks(2, "na_xt")
                        xt_l = [p.lane("na_xt%d_%d" % (half, i)) for i in range(2)]
                        ht = [sbuf(S, "na_ht%d" % i, [128, 8, 512], BF) for i in range(2)]
                        ht_t = p.toks(2, "na_ht")
                        ntmp = make_norm_tmps(S, "na_n", 512)
                        prj = Rot([banks[1], banks[2], banks[3]])
                        vrot = Rot([banks[6], banks[7]])
                        for ti in range(5):
                            s = ti % 2
                            tok0 = kf0 + ti * 512
                            own = (q0 <= tok0 < q0 + 2048)
                            p.dma("sp", xt[s][:], fm(xT)[:, :, tok0:tok0 + 512], writes=[xt_t[s]], lane=xt_l[s])
                            norm_mod(xt[s], 0, 512, [(0, 512)], lambda c: gs[:, 0, 0, c:c + 1], lambda c: mcol(0, c, 0),
                                     ht[s], 0, xt_t[s], ht_t[s], [banks[0]], ntmp)
                            for ch in range(4):
                                bank = prj.next()
                                proj_fm(wsb, t_w, ht[s], ht_t[s], 512, 512 + ch * 128, bank)
                                p.op("act", (lambda e, ch=ch, bank=bank, ti=ti: e.activation(
                                    out=KbT[:, ch, ti * 512:(ti + 1) * 512], in_=bank[0][:, 0:512], func=AF.Copy)),
                                    reads=[bank[1]], writes=[t_kb])
                            for sub in range(4):
                                vb_ = vrot.next()
                                proj_tm(wsb, t_w, ht[s], ht_t[s], sub * 128, 1024, 512, vb_)
                                p.op("dve", (lambda e, c=ti * 4 + sub, vb_=vb_: e.tensor_copy(out=Vb[:, c, :], in_=vb_[0][:, 0:512])),
                                     reads=[vb_[1]], writes=[t_vb])
                            if own:
                                lt0 = tok0 - q0
                                for ch in range(4):
                                    bank = prj.next()
                                    proj_fm(wsb, t_w, ht[s], ht_t[s], 512, ch * 128, bank)
                                    p.op("act", (lambda e, ch=ch, bank=bank, lt0=lt0: e.activation(
                                        out=QbT[:, ch, lt0:lt0 + 512], in_=bank[0][:, 0:512], func=AF.Copy, scale=SCALE_B)),
                                        reads=[bank[1]], writes=[t_qb])
                    p.barrier()
                    with ExitStack() as S:
                        bt_sb = sbuf(S, "bt_sb", [128, NTB * 8, 64], BF)
                        t_bt = p.tok("bt")
                        wout_bf = sbuf(S, "wout_bf", [128, 8, D], BF)
                        t_wo = p.tok("wout")
                        load_cols(wout_bf, t_wo, w_out, [(0, D)], "wo%d" % half)
                        with ExitStack() as S2:
                            NP = 8
                            per = (NTB * 8 + NP - 1) // NP
                            stg = sbuf(S2, "bt_stg", [128, per, 64], F32)
                            stg_t = p.tok("btstg")
                            stg_l = p.lane("btstg%d" % half)
                            for pi in range(NP):
                                a0 = pi * per
                                a1 = min(NTB * 8, a0 + per)
                                if a1 <= a0:
                                    continue
                                p.dma("sp", stg[:, 0:a1 - a0, :], btiles[:, a0:a1, :], writes=[stg_t], lane=stg_l)
                                p.op("pool", (lambda e, a0=a0, a1=a1: e.tensor_copy(out=bt_sb[:, a0:a1, :], in_=stg[:, 0:a1 - a0, :])),
                                     reads=[stg_t], writes=[t_bt])
                        p.barrier()
                        ObT = [sbuf(S, "ObT%d" % i, [128, 4, 512], BF) for i in range(2)]
                        ob_t = p.toks(2, "ObT")
                        ptA = [sbuf(S, "na_ptA%d" % i, [128, 512], BF) for i in range(2)]
                        ptB = [sbuf(S, "na_ptB%d" % i, [128, 512], BF) for i in range(2)]
                        ptA_t = p.toks(2, "na_ptA")
                        ptB_t = p.toks(2, "na_ptB")
                        rln = [sbuf(S, "na_rl%d" % i, [128, 64], F32) for i in range(2)]
                        sadd = [[sbuf(S, "na_sa%d_%d" % (hh_, i), [128, 320], F32) for i in range(2)] for hh_ in range(2)]
                        sadd_t = [p.toks(2, "na_sa%d" % hh_) for hh_ in range(2)]
                        rln_t = p.toks(2, "na_rl")
                        xr = [sbuf(S, "na_xr%d" % i, [128, 8, 512], F32) for i in range(2)]
                        xr_t = p.toks(2, "na_xr")
                        xr_l = [p.lane("na_xr%d_%d" % (half, i)) for i in range(2)]
                        xo_l = [p.lane("na_xo%d_%d" % (half, i)) for i in range(2)]
                        srot = Rot([(banks[0], banks[1]), (banks[2], banks[3])])
                        olrot = Rot([banks[4], banks[5]])
                        worot = Rot([banks[6], banks[7]])
                        it = 0
                        pending = [None]
                        for rt in range(4):
                            ts_ = rt % 2
                            tokA = q0 + rt * 512
                            p.dma("sp", xr[ts_][:], fm(xT)[:, :, tokA:tokA + 512], writes=[xr_t[ts_]], lane=xr_l[ts_])
                            for rr in range(8):
                                r = half * 32 + rt * 8 + rr
                                m_lo, nw, k0t = plan[r]
                                ncols = (nw + 2) * 64
                                ql = slice((rt * 8 + rr) * 64, (rt * 8 + rr + 1) * 64)
                                for pr in range(4):
                                    (sA, sB) = srot.next()
                                    olb, olt = olrot.next()
                                    bsel = it % 2
                                    it += 1

                                    def stage_a(m_lo=m_lo, nw=nw, k0t=k0t, ncols=ncols, ql=ql, pr=pr, sA=sA, sB=sB, bsel=bsel):
                                        for hh, (sbk, stk) in enumerate((sA, sB)):
                                            h = pr * 2 + hh
                                            prt = slice(hh * 64, hh * 64 + 64)
                                            for wi in range(nw):
                                                kt0 = (m_lo + wi) * 128 - kf0
                                                p.op("pe", (lambda e, sbk=sbk, wi=wi, kt0=kt0, prt=prt: e.matmul(
                                                    sbk[:, wi * 64:(wi + 1) * 64], KbT[prt, pr, kt0:kt0 + 128], QbT[prt, pr, ql],
                                                    start=True, stop=True)),
                                                    reads=[t_kb, t_qb], writes=[stk])
                                            for cc_ in range(2):
                                                wi = nw + cc_
                                                p.op("pe", (lambda e, sbk=sbk, wi=wi, cc_=cc_, prt=prt: e.matmul(
                                                    sbk[:, wi * 64:(wi + 1) * 64], CKbT[prt, pr, cc_ * 128:(cc_ + 1) * 128], QbT[prt, pr, ql],
                                                    start=True, stop=True)),
                                                    reads=[t_ckb, t_qb], writes=[stk])
                                            ptx, ptt = ((ptA, ptA_t), (ptB, ptB_t))[hh]
                                            sa, sat = sadd[hh][bsel], sadd_t[hh][bsel]
                                            wc = nw * 64
                                            p.op("dve", (lambda e, sbk=sbk, sa=sa, wc=wc, h=h: e.tensor_tensor(
                                                out=sa[:, 0:wc], in0=sbk[:, 0:wc],
                                                in1=bt_sb[:, h * NTB + k0t:h * NTB + k0t + wc // 64, :].rearrange("p a b -> p (a b)"),
                                                op=ALU.add)),
                                                reads=[stk, t_bt], writes=[sat])
                                            p.op("act", (lambda e, sa=sa, ptx=ptx, wc=wc: e.activation(
                                                out=ptx[bsel][:, 0:wc], in_=sa[:, 0:wc], func=AF.Exp)),
                                                reads=[sat], writes=[ptt[bsel]])
                                            p.op("act", (lambda e, sbk=sbk, ptx=ptx, wc=wc: e.activation(
                                                out=ptx[bsel][:, wc:ncols], in_=sbk[:, wc:ncols], func=AF.Exp)),
                                                reads=[stk], writes=[ptt[bsel]])

                                    def stage_b(m_lo=m_lo, nw=nw, pr=pr, rr=rr, olb=olb, olt=olt, bsel=bsel, ts_=ts_):
                                        for hh in range(2):
                                            h = pr * 2 + hh
                                            ptx, ptt = ((ptA, ptA_t), (ptB, ptB_t))[hh]
                                            orow = slice(hh * 64, hh * 64 + 64)
                                            for wi in range(nw + 2):
                                                if wi < nw:
                                                    vch = m_lo + wi - kf0 // 128
                                                    lhs = Vb[:, vch, h * 64:(h + 1) * 64]
                                                    rd = [t_vb]
                                                else:
                                                    lhs = CVb[:, wi - nw, h * 64:(h + 1) * 64]
                                                    rd = [t_ckb]
                                                p.op("pe", (lambda e, lhs=lhs, ptx=ptx, wi=wi, orow=orow: e.matmul(
                                                    olb[orow, 0:64], lhs, ptx[bsel][:, wi * 64:(wi + 1) * 64],
                                                    start=(wi == 0), stop=(wi == nw + 1))),
                                                    reads=rd + [ptt[bsel]], writes=[olt])
                                        for hh in range(2):
                                            ptx, ptt = ((ptA, ptA_t), (ptB, ptB_t))[hh]
                                            orow = slice(hh * 64, hh * 64 + 64)
                                            for wi in range(nw + 2):
                                                p.op("pe", (lambda e, ptx=ptx, wi=wi, orow=orow: e.matmul(
                                                    olb[orow, 64:128], ones_bf[:, 0:64], ptx[bsel][:, wi * 64:(wi + 1) * 64],
                                                    start=(wi == 0), stop=(wi == nw + 1))),
                                                    reads=[ptt[bsel], tconst], writes=[olt])
                                        p.op("dve", (lambda e: e.reciprocal(out=rln[bsel][:], in_=olb[:, 64:128])),
                                             reads=[olt], writes=[rln_t[bsel]])
                                        p.op("dve", (lambda e: e.tensor_tensor(
                                            out=ObT[ts_][:, pr, rr * 64:(rr + 1) * 64], in0=olb[:, 0:64], in1=rln[bsel][:], op=ALU.mult)),
                                            reads=[olt, rln_t[bsel]], writes=[ob_t[ts_]])

                                    stage_a()
                                    if pending[0] is not None:
                                        pending[0]()
                                    pending[0] = stage_b
                            pending[0]()
                            pending[0] = None

                            for dc in range(8):
                                wb, wt = worot.next()
                                for kc in range(8):
                                    if kc < 4:
                                        rhs = OaT[:, kc, tokA:tokA + 512]
                                        rd = [t_oa[half * 4 + rt]]
                                    else:
                                        rhs = ObT[ts_][:, kc - 4, :]
                                        rd = [ob_t[ts_]]
                                    p.op("pe", (lambda e, wb=wb, kc=kc, dc=dc, rhs=rhs: e.matmul(
                                        wb[:, :], wout_bf[:, kc, dc * 128:(dc + 1) * 128], rhs, start=(kc == 0), stop=(kc == 7))),
                                        reads=rd + [t_wo], writes=[wt])
                                p.op("dve", (lambda e, wb=wb, dc=dc, ts_=ts_: e.scalar_tensor_tensor(
                                    out=xr[ts_][:, dc, :], in0=wb[:, :], scalar=mcol(0, 16 + dc), op0=ALU.mult,
                                    in1=xr[ts_][:, dc, :], op1=ALU.add)),
                                    reads=[wt, tmod], writes=[xr_t[ts_]])
                            p.dma("sp", fm(x1T)[:, :, tokA:tokA + 512], xr[ts_][:], reads=[xr_t[ts_]], lane=xo_l[ts_])
                    p.barrier()
            A.close()
            p.barrier()


            return False

        if not skip_att:
            if attention_phase():
                return finish()

        def dump_scr(name, scr):
            if name in dbg:
                with ExitStack() as S:
                    tmpf = sbuf(S, "dbgs_" + name, [128, 8, 512], F32)
                    tt = p.tok()
                    ll = p.lane("dbg" + name)
                    for ti in range(8):
                        p.dma("sp", tmpf[:], fm(scr)[:, :, ti * 512:(ti + 1) * 512], writes=[tt], lane=ll)
                        p.dma("sp", fm(dbg[name])[:, :, ti * 512:(ti + 1) * 512], tmpf[:], reads=[tt], writes=[tt], lane=ll)
                    p.barrier()

        dump_scr("x1T", x1T)
        if stop_after == "ATT":
            return finish()

        def ffn_layer(l, src, dst, final_norm, local=False):
            with ExitStack() as S:
                NS = 1024
                NW = NS + 2
                xt = sbuf(S, "f_xt", [128, 8, NW], F32)
                t_xt = p.tok("f_xt")
                l_xt = p.lane("f_xt%d" % l)
                h2 = sbuf(S, "f_h2", [128, 8, NW], BF)
                t_h2 = p.tok("f_h2")
                ntmp = make_norm_tmps(S, "f_n", NW)
                aT = sbuf(S, "f_aT", [128, NPAIR, NS], BF)
                t_aT = p.tok("f_aT")
                NWB = 3
                wbf = [sbuf(S, "f_wbf%d" % i, [128, 8, 256], BF) for i in range(NWB)]
                wbf_t = p.toks(NWB, "f_wbf")
                wbf_l = [p.lane("f_wbf%d_%d" % (l, i)) for i in range(NWB)]
                wdbf = [sbuf(S, "f_wdbf%d" % i, [128, NPAIR, 128], BF) for i in range(2)]
                wdbf_t = p.toks(2, "f_wdbf")
                wdbf_l = [p.lane("f_wdbf%d_%d" % (l, i)) for i in range(2)]
                ug = [[sbuf(S, "f_ug%d_%d" % (g_, i), [128, NW], F32) for i in range(2)] for g_ in range(2)]
                ug_t = [p.toks(2, "f_ug%d" % g_) for g_ in range(2)]
                a1 = [[sbuf(S, "f_a1%d_%d" % (g_, i), [128, NS], F32) for i in range(2)] for g_ in range(2)]
                a1_t = [p.toks(2, "f_a1%d" % g_) for g_ in range(2)]
                sg = [sbuf(S, "f_sg%d" % i, [128, NS], F32) for i in range(2)]
                sg_t = p.toks(2, "f_sg")
                l_xo = p.lane("f_xo%d" % l)
                blocks = [(0, 342), (342, 342), (684, 342)]
                ctr = [((1, 342), (0, 341)), ((0, 342), (341, 683)), ((0, 341), (683, 1024))]
                urot = Rot([(banks[0], banks[1], banks[2]), (banks[3], banks[4], banks[5])])
                drot = Rot([banks[6], banks[7]])
                pair_it = 0
                wd_it = 0
                NTOK = 2048 if local else L
                for st_i in range(NTOK // NS):
                    tok0 = st_i * NS
                    lo = tok0 - 1
                    hi = tok0 + NS + 1
                    clo = max(lo, 0)
                    chi = min(hi, L)
                    if local:
                        p.dma("sp", xt[:, :, 0:NW], fm(src)[:, :, tok0:tok0 + NW], writes=[t_xt], lane=l_xt)
                    else:
                        if lo < 0:
                            p.op("pool", lambda e: e.memset(xt[:, :, 0:1], 1.0), writes=[t_xt])
                        if hi > L:
                            p.op("pool", lambda e: e.memset(xt[:, :, NW - 1:NW], 1.0), writes=[t_xt])
                        p.dma("sp", xt[:, :, clo - lo:chi - lo], fm(src)[:, :, clo:chi], writes=[t_xt], lane=l_xt)
                    norm_mod(xt, 0, NW, blocks, lambda c: gs[:, l, 1, c:c + 1], lambda c: mcol(l, 24 + c),
                             h2, 0, t_xt, t_h2, [banks[0], banks[1], banks[2]], ntmp)
                    if local:
                        if st_i == 0:
                            p.op("dve", lambda e: e.tensor_scalar(out=h2[:, :, 0:1], in0=h2[:, :, 0:1], scalar1=hm_sb[:, 0:1],
                                                                  scalar2=None, op0=ALU.mult),
                                 reads=[tconst], writes=[t_h2])
                        if st_i == NTOK // NS - 1:
                            p.op("dve", lambda e: e.tensor_scalar(out=h2[:, :, NW - 1:NW], in0=h2[:, :, NW - 1:NW],
                                                                  scalar1=hm_sb[:, 1:2], scalar2=None, op0=ALU.mult),
                                 reads=[tconst], writes=[t_h2])
                    else:
                        if lo < 0:
                            p.op("pool", lambda e: e.memset(h2[:, :, 0:1], 0.0), writes=[t_h2])
                        if hi > L:
                            p.op("pool", lambda e: e.memset(h2[:, :, NW - 1:NW], 0.0), writes=[t_h2])
                    for j in range(NPAIR):
                        s = pair_it % NWB
                        bsel = pair_it % 2
                        pair_it += 1
                        p.dma("sp", wbf[s][:].rearrange("p k n -> p (k n)"), wupb[l, j], writes=[wbf_t[s]], lane=wbf_l[s])
                        for gv in range(2):
                            ub = urot.next()
                            fch = j if gv == 0 else NPAIR + j
                            ugx, ugt = ug[gv][bsel], ug_t[gv][bsel]
                            a1x, a1t = a1[gv][bsel], a1_t[gv][bsel]
                            for bi, (c0, w) in enumerate(blocks):
                                bk, bt = ub[bi]
                                for k in range(8):
                                    p.op("pe", (lambda e, bk=bk, k=k, c0=c0, w=w, s=s, gv=gv: e.matmul(
                                        bk[:, 0:w], wbf[s][:, k, gv * 128:(gv + 1) * 128], h2[:, k, c0:c0 + w],
                                        start=(k == 0), stop=(k == 7))),
                                        reads=[wbf_t[s], t_h2], writes=[bt])
                                p.op("act", (lambda e, bk=bk, c0=c0, w=w, ugx=ugx: e.activation(
                                    out=ugx[:, c0:c0 + w], in_=bk[:, 0:w], func=AF.Copy)),
                                    reads=[bt], writes=[ugt])
                                (b0, b1), (d0, d1) = ctr[bi]
                                p.op("act", (lambda e, bk=bk, b0=b0, b1=b1, d0=d0, d1=d1, a1x=a1x, fch=fch: e.activation(
                                    out=a1x[:, d0:d1], in_=bk[:, b0:b1], func=AF.Identity,
                                    scale=cw_sb[:, l, fch, 1:2], bias=cw_sb[:, l, fch, 3:4])),
                                    reads=[bt, tconst], writes=[a1t])
                            p.op("dve", (lambda e, fch=fch, ugx=ugx, a1x=a1x: e.scalar_tensor_tensor(
                                out=a1x[:], in0=ugx[:, 0:NS], scalar=cw_sb[:, l, fch, 0:1], op0=ALU.mult,
                                in1=a1x[:], op1=ALU.add)),
                                reads=[ugt, tconst], writes=[a1t])
                            p.op("dve", (lambda e, fch=fch, ugx=ugx, a1x=a1x: e.scalar_tensor_tensor(
                                out=a1x[:], in0=ugx[:, 2:NS + 2], scalar=cw_sb[:, l, fch, 2:3], op0=ALU.mult,
                                in1=a1x[:], op1=ALU.add)),
                                reads=[ugt, tconst], writes=[a1t])
                        p.op("act", (lambda e, bsel=bsel: e.activation(out=sg[bsel][:], in_=a1[0][bsel][:], func=AF.Silu)),
                             reads=[a1_t[0][bsel]], writes=[sg_t[bsel]])
                        p.op("dve", (lambda e, j=j, bsel=bsel: e.tensor_tensor(out=aT[:, j, :], in0=sg[bsel][:], in1=a1[1][bsel][:],
                                                                              op=ALU.mult)),
                             reads=[sg_t[bsel], a1_t[1][bsel]], writes=[t_aT])
                    for dc in range(8):
                        ws_ = wd_it % 2
                        wd_it += 1
                        p.dma("sp", wdbf[ws_][:].rearrange("p j n -> p (j n)"), wdb[l, dc], writes=[wdbf_t[ws_]], lane=wdbf_l[ws_])
                        for hb in range(2):
                            db, dt_ = drot.next()
                            for j in range(NPAIR):
                                p.op("pe", (lambda e, db=db, j=j, ws_=ws_, hb=hb: e.matmul(
                                    db[:, :], wdbf[ws_][:, j, :], aT[:, j, hb * 512:(hb + 1) * 512],
                                    start=(j == 0), stop=(j == NPAIR - 1))),
                                    reads=[wdbf_t[ws_], t_aT], writes=[dt_])
                            p.op("dve", (lambda e, db=db, dc=dc, hb=hb: e.scalar_tensor_tensor(
                                out=xt[:, dc, 1 + hb * 512:1 + (hb + 1) * 512], in0=db[:, :], scalar=mcol(l, 40 + dc), op0=ALU.mult,
                                in1=xt[:, dc, 1 + hb * 512:1 + (hb + 1) * 512], op1=ALU.add)),
                                reads=[dt_, tmod], writes=[t_xt])
                    if not final_norm:
                        p.dma("sp", fm(dst)[:, :, tok0:tok0 + NS], xt[:, :, 1:NS + 1], reads=[t_xt], lane=l_xo)
                    else:
                        norm_mod(xt, 1, NS, [(0, 512), (512, 512)], lambda c: ng_sb[:, 4, c:c + 1], None,
                                 xt, 1, t_xt, t_xt, [banks[0], banks[1]], ntmp)
                        p.dma("sp", fm(dst)[:, :, tok0:tok0 + NS], xt[:, :, 1:NS + 1], reads=[t_xt], lane=lane_out)
            p.barrier()

        if skip_att and ffn0_src != 'skip':
            with ExitStack() as S:
                pc32 = sbuf(S, "tpc32", [128, NPAIR * 128], F32)
                pc16 = sbuf(S, "tpc16", [128, NPAIR * 128], BF)
                tp32, tp16 = p.toks(2, "tpc")
                lp = p.lane("tpc")
                jobs = [(w_up[0, j_], wupb[0, j_], 2048) for j_ in range(NPAIR)] + [(w_down[0, d_], wdb[0, d_], NPAIR * 128) for d_ in range(8)]
                for (srcw, dstw, nel) in jobs:
                    p.dma("sp", pc32[:, 0:nel], srcw, writes=[tp32], lane=lp)
                    p.op("pool", (lambda e, nel=nel: e.tensor_copy(out=pc16[:, 0:nel], in_=pc32[:, 0:nel])), reads=[tp32], writes=[tp16])
                    p.dma("sp", dstw, pc16[:, 0:nel], reads=[tp16], writes=[tp16], lane=lp)
            p.barrier()
        if ffn0_src != 'skip':
            ffn_layer(0, xT if ffn0_src == 'xT' else x1T, x2T, False)
        dump_scr("x2T", x2T)
        if stop_after == "FFN0":
            return finish()

        fsrc = xT if four_src == 'xT' else x2T
        with ExitStack() as S:
            Htok = sbuf(S, "Htok", [128, 32, D], BF)
            t_H = p.tok("Htok")
            wcs = sbuf(S, "wcs", [128, 2, 8, D], BF)
            t_wcs = p.tok("wcs")
            with ExitStack() as S2:
                fw_bf = sbuf(S2, "fw_bf", [128, 8, D], BF)
                t_fw = p.tok("fw")
                load_cols(fw_bf, t_fw, fw, [(0, D)], "fw")
                cs_sb = sbuf(S2, "cs_sb", [128, 2, 2, 256], BF)
                p.dma("sp", cs_sb[:, 0], c256.rearrange("(c p) n -> p c n", p=128), writes=[t_fw], lane=lane_r)
                p.dma("sp", cs_sb[:, 1], s256n.rearrange("(c p) n -> p c n", p=128), writes=[t_fw], lane=lane_r)
                wrot = Rot([banks[4], banks[5]])
                for cs in range(2):
                    for g in range(4):
                        for jc in range(2):
                            for nb in range(2):
                                bk, bt = wrot.next()
                                for kc in range(2):
                                    p.op("pe", (lambda e, bk=bk, cs=cs, g=g, jc=jc, nb=nb, kc=kc: e.matmul(
                                        bk[:, :], cs_sb[:, cs, kc, jc * 128:(jc + 1) * 128], fw_bf[:, g * 2 + kc, nb * 512:(nb + 1) * 512],
                                        start=(kc == 0), stop=(kc == 1))),
                                        reads=[t_fw], writes=[bt])
                                p.op("act", (lambda e, bk=bk, cs=cs, g=g, jc=jc, nb=nb: e.activation(
                                    out=wcs[:, cs, g * 2 + jc, nb * 512:(nb + 1) * 512], in_=bk[:, :], func=AF.Copy)),
                                    reads=[bt], writes=[t_wcs])
            p.barrier()
            with ExitStack() as S2:
                xt = [sbuf(S2, "l1_xt%d" % i, [128, 8, 512], F32) for i in range(2)]
                xt_t = p.toks(2, "l1_xt")
                xt_l = [p.lane("l1_xt%d" % i) for i in range(2)]
                ht = [sbuf(S2, "l1_ht%d" % i, [128, 8, 512], BF) for i in range(2)]
                ht_t = p.toks(2, "l1_ht")
                ntmp = make_norm_tmps(S2, "l1_n", 512)
                trot = Rot([banks[6], banks[7]])
                for ti in range(8):
                    s = ti % 2
                    tok0 = ti * 512
                    p.dma("sp", xt[s][:], fm(fsrc)[:, :, tok0:tok0 + 512], writes=[xt_t[s]], lane=xt_l[s])
                    norm_mod(xt[s], 0, 512, [(0, 512)], lambda c: gs[:, 1, 0, c:c + 1], lambda c: mcol(1, c),
                             ht[s], 0, xt_t[s], ht_t[s], [banks[0]], ntmp)
                    for sub in range(4):
                        tch = ti * 4 + sub
                        bk, bt = trot.next()
                        bkb = bk.bitcast(BF)
                        for c in range(8):
                            p.op("pe", (lambda e, bkb=bkb, c=c, s=s, sub=sub: e.transpose(
                                bkb[:, c * 128:(c + 1) * 128], ht[s][:, c, sub * 128:(sub + 1) * 128], ident_bf[:])),
                                reads=[ht_t[s], tconst], writes=[bt])
                        p.op("dve", (lambda e, bkb=bkb, tch=tch: e.tensor_copy(out=Htok[:, tch, :], in_=bkb[:, 0:1024])),
                             reads=[bt], writes=[t_H])
            p.barrier()
            KB = 256
            dct = [sbuf(S, "dct%d" % i, [128, 32, KB], BF) for i in range(2)]
            dst_ = [sbuf(S, "dst%d" % i, [128, 32, KB], BF) for i in range(2)]
            d_t = p.toks(2, "dft")
            d_l = [p.lane("dft%d" % i) for i in range(2)]
            uT = [sbuf(S, "uT%d" % i, [128, 8, 2, KB], BF) for i in range(2)]
            u_t = p.toks(2, "uT")
            xr = [sbuf(S, "l1_xr%d" % i, [128, 8, KB], F32) for i in range(2)]
            xr_t = p.toks(2, "l1_xr")
            xr_l = [p.lane("l1_xr_%d" % i) for i in range(2)]
            xo_l = [p.lane("l1_xo_%d" % i) for i in range(2)]
            urot = Rot([banks[0], banks[1], banks[2]])
            prot = Rot([banks[3], banks[4]])
            kblocks = [(i * KB, KB) for i in range(2048 // KB)] + [(2048, 2)]
            for kb, (k0, w) in enumerate(kblocks):
                s = kb % 2
                p.dma("sp", dct[s][:, :, 0:w], dftc.rearrange("(c p) k -> p c k", p=128)[:, :, k0:k0 + w], writes=[d_t[s]], lane=d_l[s])
                p.dma("sp", dst_[s][:, :, 0:w], dfts.rearrange("(c p) k -> p c k", p=128)[:, :, k0:k0 + w], writes=[d_t[s]], lane=d_l[s])
                if w == KB:
                    p.op("sp", (lambda e, s=s, k0=k0: e.dma_start(
                        out=xr[s][:], in_=fm(fsrc)[:, :, bass.ds(core_par(e) * 2048 + k0, KB)])),
                        writes=[xr_t[s]], lane=xr_l[s])
                else:
                    p.op("sp", (lambda e, s=s: e.dma_start(
                        out=xr[s][:, :, 0:1], in_=fm(fsrc)[:, :, bass.ds(core_par(e) * 2047, 1)], allow_slow_non_contiguous=True)),
                        writes=[xr_t[s]], lane=xr_l[s])
                    p.op("sp", (lambda e, s=s: e.dma_start(
                        out=xr[s][:, :, 1:2], in_=fm(fsrc)[:, :, bass.ds(core_par(e) * 2047 + 2048, 1)], allow_slow_non_contiguous=True)),
                        writes=[xr_t[s]], lane=xr_l[s])
                for jc in range(8):
                    bk, bt = urot.next()
                    for cs, tab in enumerate((dct, dst_)):
                        for tc_ in range(32):
                            p.op("pe", (lambda e, bk=bk, cs=cs, tab=tab, tc_=tc_, jc=jc, s=s, w=w: e.matmul(
                                bk[:, cs * KB:cs * KB + w], Htok[:, tc_, jc * 128:(jc + 1) * 128], tab[s][:, tc_, 0:w],
                                start=(tc_ == 0), stop=(tc_ == 31))),
                                reads=[t_H, d_t[s]], writes=[bt])
                    p.op("act", (lambda e, bk=bk, jc=jc, s=s, w=w: e.activation(
                        out=uT[s][:, jc, :, 0:w], in_=bk[:, 0:2 * KB].rearrange("p (c k) -> p c k", c=2)[:, :, 0:w], func=AF.Copy)),
                        reads=[bt], writes=[u_t[s]])
                for nch in range(8):
                    bk, bt = prot.next()
                    for cs in range(2):
                        for jc in range(8):
                            p.op("pe", (lambda e, bk=bk, cs=cs, jc=jc, nch=nch, s=s, w=w: e.matmul(
                                bk[:, 0:w], wcs[:, cs, jc, nch * 128:(nch + 1) * 128], uT[s][:, jc, cs, 0:w],
                                start=(cs == 0 and jc == 0), stop=(cs == 1 and jc == 7))),
                                reads=[t_wcs, u_t[s]], writes=[bt])
                    p.op("dve", (lambda e, bk=bk, nch=nch, s=s, w=w: e.scalar_tensor_tensor(
                        out=xr[s][:, nch, 0:w], in0=bk[:, 0:w], scalar=mcol(1, 16 + nch), op0=ALU.mult,
                        in1=xr[s][:, nch, 0:w], op1=ALU.add)),
                        reads=[bt, tmod], writes=[xr_t[s]])
                if w == KB:
                    p.dma("sp", fm(x3T)[:, :, 1 + k0:1 + k0 + KB], xr[s][:], reads=[xr_t[s]], lane=xo_l[s])
                else:
                    p.op("sp", (lambda e, s=s: e.dma_start(out=fm(x3T)[:, :, 0:1], in_=xr[s][:, :, 0:1], allow_slow_non_contiguous=True)),
                         reads=[xr_t[s]], lane=xo_l[s])
                    p.op("sp", (lambda e, s=s: e.dma_start(out=fm(x3T)[:, :, 2049:2050], in_=xr[s][:, :, 1:2], allow_slow_non_contiguous=True)),
                         reads=[xr_t[s]], lane=xo_l[s])
        p.barrier()
        if stop_after == "FOUR":
            return finish()

        ffn_layer(1, x3T, yT, True, local=True)
        return finish()

def prep_core_inputs(b, x, c, ctx, c_ctx, mod_w, mod_b, norm1_g, norm2_g, attn_w_in, attn_w_out, q_norm_g,
                     k_norm_g, na_rpb, fourier_w_out, ffn_w_up, ffn_conv_w, ffn_conv_b, ffn_w_down, final_g, shared):
    f32 = np.float32
    m = dict(shared)
    m["xT"] = np.ascontiguousarray(x[b].T.astype(f32))
    m["ctxT"] = np.ascontiguousarray(ctx[b].T.astype(f32))
    ccv = np.stack([c[b], c_ctx], axis=-1).astype(f32)
    m["cc"] = np.ascontiguousarray(ccv.reshape(8, 128, 2).transpose(1, 0, 2))
    return m


_DFT_CORE = {}


def dft_core_tables(s):
    if s not in _DFT_CORE:
        cst = consts()
        base = s * 2048
        cols = np.concatenate([np.arange(base, base + 2048), [max(base - 1, 0)], [min(base + 2048, L - 1)]])
        hm = np.zeros((128, 2), np.float32)
        hm[:, 0] = 0.0 if s == 0 else 1.0
        hm[:, 1] = 1.0 if s == 0 else 0.0
        _DFT_CORE[s] = dict(dftc=np.ascontiguousarray(cst["dftc_full"][:, cols]),
                            dfts=np.ascontiguousarray(cst["dfts_full"][:, cols]), hmask=hm)
    return _DFT_CORE[s]


def prep_shared(mod_w, mod_b, norm1_g, norm2_g, attn_w_in, attn_w_out, q_norm_g, k_norm_g, na_rpb,
                fourier_w_out, ffn_w_up, ffn_conv_w, ffn_conv_b, ffn_w_down, final_g):
    f32 = np.float32
    sh = {}
    sh["mod_w"] = np.ascontiguousarray(mod_w.astype(f32))
    sh["mod_b2"] = np.ascontiguousarray(np.repeat(mod_b.astype(f32)[:, None, :], 2, axis=1))
    ngs = np.stack([norm1_g[0], norm2_g[0], norm1_g[1], norm2_g[1], final_g], axis=0).astype(f32)
    sh["ng"] = np.ascontiguousarray(ngs.reshape(5, 8, 128).transpose(2, 0, 1))
    sh["w_in"] = np.ascontiguousarray(attn_w_in[0].astype(f32))
    sh["w_out"] = np.ascontiguousarray(attn_w_out[0].astype(f32))
    sh["qkg"] = np.ascontiguousarray(np.stack([q_norm_g[0], k_norm_g[0]], axis=-1).astype(f32))
    sh["btiles"] = build_bias_tiles(na_rpb[0].astype(f32))
    sh["fw"] = np.ascontiguousarray(fourier_w_out[0].astype(f32))
    wu = ffn_w_up.astype(f32).reshape(2, 8, 128, 2, NPAIR, 128)
    sh["w_up"] = np.ascontiguousarray(wu.transpose(0, 4, 2, 1, 3, 5).reshape(2, NPAIR, 128, 8 * 256))
    cwb = np.concatenate([ffn_conv_w.astype(f32), ffn_conv_b.astype(f32)[:, None, :]], axis=1)
    sh["cw"] = np.ascontiguousarray(cwb.reshape(2, 4, 44, 128).transpose(3, 0, 2, 1))
    wd = ffn_w_down.astype(f32).reshape(2, NPAIR, 128, 8, 128)
    sh["w_down"] = np.ascontiguousarray(wd.transpose(0, 3, 2, 1, 4).reshape(2, 8, 128, NPAIR * 128))
    sh.update({k: v for k, v in consts().items() if not k.endswith('_full')})
    return sh


def kernel(x, c, ctx, c_ctx, mod_w, mod_b, norm1_g, norm2_g, attn_w_in, attn_w_out, q_norm_g,
           k_norm_g, na_rpb, fourier_w_out, ffn_w_up, ffn_conv_w, ffn_conv_b, ffn_w_down, final_g):
    args = [np.asarray(a) for a in (x, c, ctx, c_ctx, mod_w, mod_b, norm1_g, norm2_g, attn_w_in, attn_w_out,
                                    q_norm_g, k_norm_g, na_rpb, fourier_w_out, ffn_w_up, ffn_conv_w,
                                    ffn_conv_b, ffn_w_down, final_g)]
    shared = prep_shared(*args[4:])
    in_maps = [prep_core_inputs(core // 2, *args, shared) for core in range(8)]
    for core in range(8):
        in_maps[core].update(dft_core_tables(core % 2))
    nc = build()
    res = run_bass_kernel_spmd(nc, in_maps, core_ids=list(range(8)))
    out = np.empty((4, L, D), np.float32)
    for b in range(4):
        y0 = res.results[2 * b]["yT"]
        y1 = res.results[2 * b + 1]["yT"]
        out[b, :2048] = y0.T
        out[b, 2048:] = y1.T
    return out
```
